# Optimizing a Trainium2 kernel written in Bass

```python
import math
import jax, jax.numpy as jnp
from jax import lax
import numpy as np

D_MODEL = 1024
BATCH = 2
SEQ = 8192
DEPTH = 1

CHUNK = 64
Q_BLOCK = 128
EPS = 1e-6

DIFF_HEADS = 8
DIFF_DH = 64
DIFF_W = DIFF_HEADS * 2 * DIFF_DH

CH_HEADS = 8
CH_DH = 64
CH_W = CH_HEADS * CH_DH
CH_LEFT = 8
REL_CLIP = 128

PEER_HEADS = 8
PEER_NKEYS = 128
PEER_NEXP = PEER_NKEYS * PEER_NKEYS
PEER_DKEY = 128
PEER_TOPK = 16
PEER_BLOCK = 128

IN_SIZES = (DIFF_W, DIFF_W, DIFF_W, CH_W, CH_W, CH_W, D_MODEL, D_MODEL)
IN_COLS = sum(IN_SIZES)
IN_SPLITS = tuple(int(s) for s in np.cumsum(IN_SIZES)[:-1])

kernel_name = "hybrid_diffattn_chunkattn_peer"


def rmsnorm(x, w):
    xf = x.astype(jnp.float32)
    y = xf * lax.rsqrt(jnp.mean(xf * xf, axis=-1, keepdims=True) + EPS)
    return (y * w.astype(jnp.float32)).astype(x.dtype)


def alibi_slopes(n_heads):
    return jnp.exp2(-8.0 * jnp.arange(1, n_heads + 1, dtype=jnp.float32) / n_heads)


def diff_attention(q, k, v, lq1, lk1, lq2, lk2, subln_w, lam_init):
    B, S = q.shape[0], q.shape[1]
    nqb = S // Q_BLOCK
    f32 = jnp.float32
    lam = (jnp.exp(jnp.sum(lq1.astype(f32) * lk1.astype(f32)))
           - jnp.exp(jnp.sum(lq2.astype(f32) * lk2.astype(f32))) + lam_init)
    k1, k2 = k[..., 0, :], k[..., 1, :]
    qb = jnp.moveaxis(q.reshape(B, nqb, Q_BLOCK, DIFF_HEADS, 2, DIFF_DH), 1, 0)
    slopes = alibi_slopes(DIFF_HEADS)
    kpos = jnp.arange(S, dtype=jnp.int32)
    scale = DIFF_DH ** -0.5

    def block(args):
        qblk, bi = args
        qpos = bi * Q_BLOCK + jnp.arange(Q_BLOCK, dtype=jnp.int32)
        dist = jnp.abs(qpos[:, None] - kpos[None, :]).astype(f32)
        bias = -slopes[:, None, None] * dist
        allowed = (kpos[None, :] // CHUNK) <= (qpos[:, None] // CHUNK)

        def probs(qq, kk):
            s = jnp.einsum('bqhd,bkhd->bhqk', qq, kk).astype(f32) * scale + bias
            return jax.nn.softmax(jnp.where(allowed, s, -jnp.inf), axis=-1)

        a = probs(qblk[..., 0, :], k1) - lam * probs(qblk[..., 1, :], k2)
        return jnp.einsum('bhqk,bkhe->bqhe', a.astype(v.dtype), v)

    o = lax.map(block, (qb, jnp.arange(nqb, dtype=jnp.int32)))
    o = jnp.moveaxis(o, 0, 1).reshape(B, S, DIFF_HEADS, 2 * DIFF_DH)
    o = rmsnorm(o, subln_w) * (1.0 - lam_init)
    return o.reshape(B, S, DIFF_W)


def chunk_attention(q, k, v, rel_table):
    B, S = q.shape[0], q.shape[1]
    nc = S // CHUNK
    band = (CH_LEFT + 1) * CHUNK
    f32 = jnp.float32
    qc = q.reshape(B, nc, CHUNK, CH_HEADS, CH_DH)
    pad = ((0, 0), (CH_LEFT * CHUNK, 0), (0, 0), (0, 0))
    kp = jnp.pad(k, pad).reshape(B, nc + CH_LEFT, CHUNK, CH_HEADS, CH_DH)
    vp = jnp.pad(v, pad).reshape(B, nc + CH_LEFT, CHUNK, CH_HEADS, CH_DH)
    kb = jnp.concatenate([kp[:, j:j + nc] for j in range(CH_LEFT + 1)], axis=2)
    vb = jnp.concatenate([vp[:, j:j + nc] for j in range(CH_LEFT + 1)], axis=2)
    r = jnp.arange(band, dtype=jnp.int32)
    qi = jnp.arange(CHUNK, dtype=jnp.int32)
    rel = qi[:, None] - r[None, :] + CH_LEFT * CHUNK
    bias = rel_table[:, jnp.clip(rel, -REL_CLIP, REL_CLIP) + REL_CLIP].astype(f32)
    valid = (jnp.arange(nc, dtype=jnp.int32)[:, None] * CHUNK - CH_LEFT * CHUNK + r[None, :]) >= 0
    s = jnp.einsum('bcqhd,bckhd->bhcqk', qc, kb).astype(f32) * (CH_DH ** -0.5) + bias[:, None]
    s = jnp.where(valid[:, None, :], s, -jnp.inf)
    p = jax.nn.softmax(s, axis=-1)
    o = jnp.einsum('bhcqk,bckhd->bcqhd', p.astype(v.dtype), vb)
    return o.reshape(B, S, CH_W)


def peer(xn, wq, keys, u, v):
    B, S, D = xn.shape
    xt = xn.reshape(-1, PEER_BLOCK, D)
    f32 = jnp.float32

    def block(xb):
        q = (xb @ wq).reshape(PEER_BLOCK, PEER_HEADS, 2, PEER_DKEY // 2)
        s1 = jnp.einsum('thd,hnd->thn', q[:, :, 0], keys[:, 0]).astype(f32)
        s2 = jnp.einsum('thd,hnd->thn', q[:, :, 1], keys[:, 1]).astype(f32)
        v1, i1 = lax.top_k(s1, PEER_TOPK)
        v2, i2 = lax.top_k(s2, PEER_TOPK)
        cand = (v1[..., :, None] + v2[..., None, :]).reshape(PEER_BLOCK, PEER_HEADS, PEER_TOPK * PEER_TOPK)
        sc, ci = lax.top_k(cand, PEER_TOPK)
        e = (jnp.take_along_axis(i1, ci // PEER_TOPK, axis=-1) * PEER_NKEYS
             + jnp.take_along_axis(i2, ci % PEER_TOPK, axis=-1))
        g = jax.nn.softmax(sc, axis=-1)
        ue = u[e]
        ve = v[e]
        hid = jnp.einsum('thkd,td->thk', ue, xb).astype(f32)
        a = (jax.nn.gelu(hid, approximate=False) * g).astype(xb.dtype)
        return jnp.einsum('thk,thkd->td', a, ve)

    return lax.map(block, xt).reshape(B, S, D)


def setup_inputs(seed: int = 0) -> dict:
    key = jax.random.key(seed)
    ks = jax.random.split(key, 20)
    n = jax.random.normal
    f32 = jnp.float32
    L, D = DEPTH, D_MODEL
    return {
        "x": n(ks[0], (BATCH, SEQ, D), f32),
        "norm1_w": 1.0 + 0.02 * n(ks[1], (L, D), f32),
        "w_in": n(ks[2], (L, D, IN_COLS), f32) * D ** -0.5,
        "b_gate": 0.01 * n(ks[3], (L, 2 * D), f32),
        "diff_lq1": 0.1 * n(ks[4], (L, DIFF_DH), f32),
        "diff_lk1": 0.1 * n(ks[5], (L, DIFF_DH), f32),
        "diff_lq2": 0.1 * n(ks[6], (L, DIFF_DH), f32),
        "diff_lk2": 0.1 * n(ks[7], (L, DIFF_DH), f32),
        "diff_subln_w": 1.0 + 0.02 * n(ks[8], (L, 2 * DIFF_DH), f32),
        "chunk_rel_bias": 0.1 * n(ks[9], (L, CH_HEADS, 2 * REL_CLIP + 1), f32),
        "w_branch_diff": n(ks[10], (L, DIFF_W, D), f32) * DIFF_W ** -0.5,
        "w_branch_chunk": n(ks[11], (L, CH_W, D), f32) * CH_W ** -0.5,
        "w_out": n(ks[12], (L, D, D), f32) * D ** -0.5,
        "norm2_w": 1.0 + 0.02 * n(ks[13], (L, D), f32),
        "peer_wq": n(ks[14], (L, D, PEER_HEADS * PEER_DKEY), f32) * D ** -0.5,
        "peer_keys": n(ks[15], (L, PEER_HEADS, 2, PEER_NKEYS, PEER_DKEY // 2), f32) * (PEER_DKEY // 2) ** -0.5,
        "peer_u": n(ks[16], (L, PEER_NEXP, D), f32) * D ** -0.5,
        "peer_v": n(ks[17], (L, PEER_NEXP, D), f32) * D ** -0.5,
        "final_norm_w": 1.0 + 0.02 * n(ks[18], (D,), f32),
    }


def reference(x, norm1_w, w_in, b_gate, diff_lq1, diff_lk1, diff_lq2, diff_lk2, diff_subln_w,
              chunk_rel_bias, w_branch_diff, w_branch_chunk, w_out, norm2_w, peer_wq, peer_keys,
              peer_u, peer_v, final_norm_w):
    B, S, D = x.shape
    h = x
    for l in range(DEPTH):
        lam_init = 0.8 - 0.6 * math.exp(-0.3 * l)
        xn = rmsnorm(h, norm1_w[l])
        proj = xn @ w_in[l]
        qd, kd, vd, qc, kc, vc, gd, gc = jnp.split(proj, IN_SPLITS, axis=-1)
        o_d = diff_attention(qd.reshape(B, S, DIFF_HEADS, 2, DIFF_DH),
                             kd.reshape(B, S, DIFF_HEADS, 2, DIFF_DH),
                             vd.reshape(B, S, DIFF_HEADS, 2 * DIFF_DH),
                             diff_lq1[l], diff_lk1[l], diff_lq2[l], diff_lk2[l], diff_subln_w[l], lam_init)
        o_c = chunk_attention(qc.reshape(B, S, CH_HEADS, CH_DH), kc.reshape(B, S, CH_HEADS, CH_DH),
                              vc.reshape(B, S, CH_HEADS, CH_DH), chunk_rel_bias[l])
        g_d = jax.nn.sigmoid(gd + b_gate[l, :D])
        g_c = jax.nn.sigmoid(gc + b_gate[l, D:])
        merged = g_d * (o_d @ w_branch_diff[l]) + g_c * (o_c @ w_branch_chunk[l])
        h = h + merged @ w_out[l]
        h = h + peer(rmsnorm(h, norm2_w[l]), peer_wq[l], peer_keys[l], peer_u[l], peer_v[l])
    return rmsnorm(h, final_norm_w)
```

```python
import numpy as np
from contextlib import ExitStack
import concourse.bass as bass
import concourse.mybir as mybir
from concourse.bass_utils import run_bass_kernel_spmd

F32 = mybir.dt.float32
BF16 = mybir.dt.bfloat16
I32 = mybir.dt.int32
U32 = mybir.dt.uint32
ALU = mybir.AluOpType
AF = mybir.ActivationFunctionType

NB = 68
NT = NB * 128
NQ = 16
EPS = 1e-6
NEG = -30000.0
P2_HEADS = 8
P2_MODE = 3
P2_NQ = 16


class Sched:
    ENGS = ("pe", "act", "dve", "pool", "sp")

    def __init__(self, nc, es, prefix="p"):
        self.nc = nc
        self.es = es
        self.prefix = prefix
        self.sem = {}
        self.cnt = {}
        self.prog = {}
        for name in self.ENGS:
            self.sem[name] = es.enter_context(nc.semaphore(prefix + "s_" + name))
            self.cnt[name] = 0
            self.prog[name] = []
        self.seen = {name: {} for name in self.ENGS}
        self.res_w = {}
        self.res_r = {}
        self.dma_sem = {}
        self.dma_cnt = {}

    def _dma_tag(self, tag):
        if tag not in self.dma_sem:
            self.dma_sem[tag] = self.es.enter_context(self.nc.semaphore(self.prefix + "d_" + tag))
            self.dma_cnt[tag] = 0
        return self.dma_sem[tag]

    def _sem_of(self, key):
        return self.sem[key[1]] if key[0] == "e" else self.dma_sem[key[1]]

    def _collect(self, reads, writes, skip_waw=False):
        deps = {}

        def add(tok):
            if tok is None:
                return
            k, v = tok
            if deps.get(k, 0) < v:
                deps[k] = v

        for r in reads:
            add(self.res_w.get(r))
        for w in writes:
            if not skip_waw:
                add(self.res_w.get(w))
            for k, v in self.res_r.get(w, {}).items():
                add((k, v))
        return deps

    def _emit_waits(self, ename, deps):
        seen = self.seen[ename]
        for k, v in deps.items():
            if k == ("e", "pe") and ename == "pe":
                continue
            if seen.get(k, 0) >= v:
                continue
            self.prog[ename].append(("wait", self._sem_of(k), v))
            seen[k] = v

    def _update(self, tok, reads, writes):
        k, v = tok
        for w in writes:
            self.res_w[w] = tok
            self.res_r[w] = {}
        for r in reads:
            d = self.res_r.setdefault(r, {})
            if d.get(k, 0) < v:
                d[k] = v

    def op(self, ename, fn, reads=(), writes=()):
        deps = self._collect(reads, writes)
        self._emit_waits(ename, deps)
        self.cnt[ename] += 1
        self.prog[ename].append(("ins", fn, self.sem[ename], 1))
        self._update((("e", ename), self.cnt[ename]), reads, writes)

    def dma(self, qname, tag, fn, reads=(), writes=(), part=False):
        tag = writes[0] if writes else reads[0]
        if tag in ("kdT_s", "kcT_s", "vd_s", "vc_s", "qdT_s", "qcT_s", "od_s", "oc_s", "h1_s", "y", "puv16"):
            tag = tag + "_" + (reads[0] if reads else "w")
        sem = self._dma_tag(tag)
        deps = self._collect(reads, writes, skip_waw=part)
        self._emit_waits(qname, deps)
        self.dma_cnt[tag] += 16
        self.prog[qname].append(("ins", fn, sem, 16))
        self._update((("d", tag), self.dma_cnt[tag]), reads, writes)

    def barrier(self):
        for ename in self.ENGS:
            for tag, c in self.dma_cnt.items():
                if c:
                    self.prog[ename].append(("wait", self.dma_sem[tag], c))
            for name, c in self.cnt.items():
                if c and name != ename:
                    self.prog[ename].append(("wait", self.sem[name], c))

    def replay(self, ename, e):
        for item in self.prog[ename]:
            if item[0] == "wait":
                e.wait_ge(item[1], item[2])
            else:
                item[1](e).then_inc(item[2], item[3])

    def emit(self, block):
        s = self

        @block.tensor
        def _(e):
            s.replay("pe", e)

        @block.scalar
        def _(e):
            s.replay("act", e)

        @block.vector
        def _(e):
            s.replay("dve", e)

        @block.gpsimd
        def _(e):
            s.replay("pool", e)

        @block.sync
        def _(e):
            s.replay("sp", e)


def MM(out, lhsT, rhs, start, stop):
    return lambda e: e.matmul(out, lhsT=lhsT, rhs=rhs, start=start, stop=stop)


def TR(out, in_, ident):
    return lambda e: e.transpose(out, in_, ident)


def ACTF(out, in_, func, **kw):
    return lambda e: e.activation(out=out, in_=in_, func=func, **kw)


def ACOPY(out, in_):
    return lambda e: e.copy(out=out, in_=in_)


def CP(out, in_):
    return lambda e: e.tensor_copy(out=out, in_=in_)


def DMA(out, in_):
    return lambda e: e.dma_start(out=out, in_=in_)


def STT(out, in0, scalar, in1, op0, op1, accum_out=None):
    if accum_out is None:
        return lambda e: e.scalar_tensor_tensor(out=out, in0=in0, scalar=scalar, in1=in1, op0=op0, op1=op1)
    return lambda e: e.scalar_tensor_tensor(out=out, in0=in0, scalar=scalar, in1=in1, op0=op0, op1=op1, accum_out=accum_out)


def TS(out, in0, s1, s2, op0, op1):
    return lambda e: e.tensor_scalar(out=out, in0=in0, scalar1=s1, scalar2=s2, op0=op0, op1=op1)


def TSMUL(out, in0, s1):
    return lambda e: e.tensor_scalar_mul(out=out, in0=in0, scalar1=s1)


def TT(out, in0, in1, op):
    return lambda e: e.tensor_tensor(out=out, in0=in0, in1=in1, op=op)


def RECIP(out, in_):
    return lambda e: e.reciprocal(out=out, in_=in_)


def MEMSET(ap, v):
    return lambda e: e.memset(ap, v)


def make_ident(s, ident, name):
    s.op("pool", MEMSET(ident[:], 0.0), writes=[name])
    s.op("pool", lambda e: e.affine_select(out=ident[:], in_=ident[:], pattern=[[-1, 128]], compare_op=ALU.not_equal,
                                           fill=1.0, base=0, channel_multiplier=1), reads=[name], writes=[name])


class Ctx:
    pass


def load_weight(s, c, dst, dst_res, src, rows_k, col0, ncols, dcol0, wst, cnt):
    for k in range(rows_k):
        for cc in range(0, ncols, 1024):
            n = min(1024, ncols - cc)
            j = cnt[0] % len(wst)
            cnt[0] += 1
            s.dma("sp", f"wst{j}", DMA(wst[j][:, 0:n], src[k * 128:(k + 1) * 128, col0 + cc:col0 + cc + n]), writes=[f"wst{j}"])
            eng = "pool" if (cnt[0] % 2) else "dve"
            s.op(eng, CP(dst[:, k, dcol0 + cc:dcol0 + cc + n], wst[j][:, 0:n]), reads=[f"wst{j}"], writes=[dst_res])


def norm_tile(s, c, i, src_rows, w_tile, w_res, bf16_out=True):
    xt, xnb, xnT = c.xt[i], c.xnb[i], c.xnT[i]
    for bi, src in enumerate(src_rows):
        s.dma("sp", f"xt{i}", DMA(xt[:, bi, :], src), writes=[f"xt{i}"], part=(bi != 0))
    for bi in range(4):
        s.op("act", ACTF(c.junk[:], xt[:, bi, :], AF.Square, accum_out=c.ssq[:, i, bi:bi + 1]), reads=[f"xt{i}"],
             writes=["junk", f"ssq{i}"])
    s.op("dve", TS(c.rstd[:, i, :], c.ssq[:, i, :], 1.0 / 1024, EPS, ALU.mult, ALU.add), reads=[f"ssq{i}"], writes=[f"rstd{i}"])
    s.op("act", ACTF(c.rstd[:, i, :], c.rstd[:, i, :], AF.Ln), reads=[f"rstd{i}"], writes=[f"rstd{i}"])
    s.op("act", ACTF(c.rstd[:, i, :], c.rstd[:, i, :], AF.Exp, scale=-0.5), reads=[f"rstd{i}"], writes=[f"rstd{i}"])
    for bi in range(4):
        s.op("dve", STT(xnb[:, bi, :], xt[:, bi, :], c.rstd[:, i, bi:bi + 1], w_tile[:], ALU.mult, ALU.mult),
             reads=[f"xt{i}", f"rstd{i}", w_res], writes=[f"xnb{i}_{bi}"])
    for bi in range(4):
        tpb = c.tp[bi % 2]
        for k in range(8):
            s.op("pe", TR(tpb[:, k, :], xnb[:, bi, k * 128:(k + 1) * 128], c.ident[:]), reads=[f"xnb{i}_{bi}", "ident"],
                 writes=[f"tp{bi % 2}"])
        if bi % 2 == 0:
            s.op("act", ACOPY(xnT[:, :, bi * 128:(bi + 1) * 128], tpb[:, :, :]), reads=[f"tp{bi % 2}"], writes=[f"xnT{i}_{bi}"])
        else:
            s.op("dve", CP(xnT[:, :, bi * 128:(bi + 1) * 128], tpb[:, :, :]), reads=[f"tp{bi % 2}"], writes=[f"xnT{i}_{bi}"])
    return [f"xnT{i}_{bi}" for bi in range(4)]


def phase1(nc, ses, D):
    with ExitStack() as es:
        def sb(name, shape, dt):
            return es.enter_context(nc.sbuf_tensor("a_" + name, shape, dt))

        def ps(name, shape, dt=F32):
            return es.enter_context(nc.psum_tensor("a_" + name, shape, dt))

        s = Sched(nc, ses, "a")
        c = Ctx()
        wb = sb("wb", [128, 8, 3072], BF16)
        wst = [sb(f"wst{i}", [128, 1024], F32) for i in range(2)]
        n1 = sb("n1", [128, 1024], F32)
        c.ident = sb("ident", [128, 128], BF16)
        c.xt = [sb(f"xt{i}", [128, 4, 1024], F32) for i in range(2)]
        c.xnb = [sb(f"xnb{i}", [128, 4, 1024], BF16) for i in range(2)]
        c.xnT = [sb(f"xnT{i}", [128, 8, 512], BF16) for i in range(2)]
        c.ssq = sb("ssq", [128, 2, 4], F32)
        c.rstd = sb("rstd", [128, 2, 4], F32)
        c.junk = sb("junk", [128, 1024], BF16)
        stgk = [sb(f"stgk{i}", [128, 12, 512], BF16) for i in range(2)]
        stgv = [sb(f"stgv{i}", [128, 4, 1536], BF16) for i in range(2)]
        c.tp = [ps(f"tp{i}", [128, 8, 128], BF16) for i in range(2)]
        mp = [ps(f"mp{i}", [128, 512], F32) for i in range(4)]

        make_ident(s, c.ident, "ident")
        s.dma("sp", "n1", DMA(n1[:], D["n1w"].partition_broadcast(128)), writes=["n1"])
        cnt = [0]
        w_in = D["w_in"]
        load_weight(s, c, wb, "wb", w_in, 8, 1024, 2048, 0, wst, cnt)
        load_weight(s, c, wb, "wb", w_in, 8, 3584, 1024, 2048, wst, cnt)
        mcnt = [0]

        def evac(dst, bank, bres, dres, scale=None):
            eng = "act" if (mcnt[0] % 2 == 0) else "dve"
            if scale is not None:
                s.op("act", (lambda dst, bank: lambda e: e.mul(out=dst, in_=bank, mul=scale))(dst, bank), reads=[bres], writes=[dres])
            elif eng == "act":
                s.op("act", ACOPY(dst, bank), reads=[bres], writes=[dres])
            else:
                s.op("dve", CP(dst, bank), reads=[bres], writes=[dres])

        def featmajor_proj(i, xres, ncolchunks, colfn, stg, stgres, qc_scale=False):
            for hh in range(ncolchunks):
                col0 = colfn(hh)
                bi_ = mcnt[0] % 4
                bank = mp[bi_]
                for k in range(8):
                    s.op("pe", MM(bank[:, :], wb[:, k, col0:col0 + 128], c.xnT[i][:, k, :], k == 0, k == 7),
                         reads=["wb"] + xres, writes=[f"mp{bi_}"])
                evac(stg[:, hh, :], bank[:, :], f"mp{bi_}", stgres, scale=(0.125 if (qc_scale and hh >= 8) else None))
                mcnt[0] += 1

        for T in range(NB // 4):
            i = T % 2
            rows = [D["xall"][(T * 4 + bi) * 128:(T * 4 + bi + 1) * 128, :] for bi in range(4)]
            xres = norm_tile(s, c, i, rows, n1, "n1")
            featmajor_proj(i, xres, 12, lambda hh: hh * 128 if hh < 8 else 2048 + (hh - 8) * 128, stgk[i], f"stgk{i}")
            s.dma("sp", f"stgk{i}", DMA(D["kdT_s"][:, :, T * 512:(T + 1) * 512].rearrange("h p t -> p h t"), stgk[i][:, 0:8, :]),
                  reads=[f"stgk{i}"], writes=["kdT_s"])
            s.dma("sp", f"stgk{i}", DMA(D["kcT_s"][:, :, T * 512:(T + 1) * 512].rearrange("h p t -> p h t"), stgk[i][:, 8:12, :]),
                  reads=[f"stgk{i}"], writes=["kcT_s"])
            for bi in range(4):
                for half in range(3):
                    col0 = 1024 + half * 512 if half < 2 else 2560
                    bi_ = mcnt[0] % 4
                    bank = mp[bi_]
                    for k in range(8):
                        s.op("pe", MM(bank[:, :], c.xnT[i][:, k, bi * 128:(bi + 1) * 128], wb[:, k, col0:col0 + 512], k == 0, k == 7),
                             reads=["wb"] + xres, writes=[f"mp{bi_}"])
                    evac(stgv[i][:, bi, half * 512:(half + 1) * 512], bank[:, :], f"mp{bi_}", f"stgv{i}")
                    mcnt[0] += 1
            for bi in range(4):
                s.dma("sp", f"stgv{i}", DMA(D["vd_s"][:, :, T * 4 + bi, :].rearrange("h p e -> p h e"),
                                            stgv[i][:, bi, 0:1024].rearrange("p (h e) -> p h e", h=8)),
                      reads=[f"stgv{i}"], writes=["vd_s"])
                s.dma("sp", f"stgv{i}", DMA(D["vc_s"][:, :, T * 4 + bi, :].rearrange("h p e -> p h e"),
                                            stgv[i][:, bi, 1024:1536].rearrange("p (h e) -> p h e", h=8)),
                      reads=[f"stgv{i}"], writes=["vc_s"])
        load_weight(s, c, wb, "wb", w_in, 8, 0, 1024, 0, wst, cnt)
        load_weight(s, c, wb, "wb", w_in, 8, 3072, 512, 1024, wst, cnt)
        for G in range(4):
            i = G % 2
            rows = [D["xall"][(4 * (4 * G + bi) + 4) * 128:(4 * (4 * G + bi) + 5) * 128, :] for bi in range(4)]
            xres = norm_tile(s, c, i, rows, n1, "n1")
            featmajor_proj(i, xres, 12, lambda hh: hh * 128, stgk[i], f"stgk{i}", qc_scale=True)
            s.dma("sp", f"stgk{i}", DMA(D["qdT_s"][:, :, G * 512:(G + 1) * 512].rearrange("h p t -> p h t"), stgk[i][:, 0:8, :]),
                  reads=[f"stgk{i}"], writes=["qdT_s"])
            s.dma("sp", f"stgk{i}", DMA(D["qcT_s"][:, :, G * 512:(G + 1) * 512].rearrange("h p t -> p h t"), stgk[i][:, 8:12, :]),
                  reads=[f"stgk{i}"], writes=["qcT_s"])
        s.barrier()
        with nc.Block() as block:
            s.emit(block)


def phase2(nc, ses, D):
    with ExitStack() as es:
        def sb(name, shape, dt):
            return es.enter_context(nc.sbuf_tensor("b_" + name, shape, dt))

        def ps(name, shape, dt=F32):
            return es.enter_context(nc.psum_tensor("b_" + name, shape, dt))

        s = Sched(nc, ses, "b")
        kT = [sb(f"kT{i}", [128, NT], BF16) for i in range(2)]
        vv = [sb(f"vv{i}", [128, NB, 132], BF16) for i in range(2)]
        qT = [sb(f"qT{i}", [128, 2048], BF16) for i in range(2)]
        kcT = [sb(f"kcT{i}", [128, NT], BF16) for i in range(2)]
        qcT = [sb(f"qcT{i}", [128, 2048], BF16) for i in range(2)]
        vc = [sb(f"vc{i}", [128, NB, 68], BF16) for i in range(2)]
        colb = [sb(f"colb{i}", [128, 544], F32) for i in range(2)]
        tdg = [sb(f"tdg{i}", [128, 256], F32) for i in range(2)]
        cbg = [sb(f"cbg{i}", [128, 5, 128], F32) for i in range(2)]
        cb0 = [sb(f"cb0{i}", [128, 5, 128], F32) for i in range(2)]
        P = [sb(f"P{i}", [128, 256], BF16) for i in range(4)]
        tmp = [sb(f"tmp{i}", [128, 256], F32) for i in range(2)]
        lq = sb("lq", [128, 4, 64], F32)
        lj = sb("lj", [128, 64], F32)
        lsc = sb("lsc", [128, 4], F32)
        neglam = sb("neglam", [128, 1], F32)
        sw8 = sb("sw8", [128, 128], F32)
        r12 = sb("r12", [128, 4], F32)
        t1 = sb("t1", [128, 128], F32)
        o_ = sb("o_", [128, 128], F32)
        oj = sb("oj", [128, 128], F32)
        sq = sb("sq", [128, 2], F32)
        odst = [sb(f"odst{i}", [128, 128], BF16) for i in range(2)]
        ocst = [sb(f"ocst{i}", [128, 64], BF16) for i in range(2)]
        identf = sb("identf", [128, 128], F32)
        S = [ps(f"S{i}", [128, 2, 512], F32) for i in range(2)]
        O1 = ps("O1", [128, 512], F32)
        O2 = ps("O2", [128, 512], F32)
        Oc = ps("Oc", [128, 512], F32)

        do_conv = D["pu"].shape[0] == 16384
        if do_conv:
            cin = [sb(f"cin{i}", [128, 2, 1024], F32) for i in range(3)]
            cout = [sb(f"cout{i}", [128, 2, 1024], BF16) for i in range(2)]
            dst_v = D["puv16"].rearrange("(p r) d -> p r d", p=128)

            def conv_src(c):
                tab = "pu" if c < 64 else "pv"
                return D[tab].rearrange("(p r) d -> p r d", p=128)[:, 2 * (c % 64):2 * (c % 64) + 2, :]

            def conv_in(c):
                s.dma("sp", None, DMA(cin[c % 3][:, :, :], conv_src(c)), writes=[f"cin{c % 3}"])

            def conv_step(c):
                if c + 1 < 128:
                    conv_in(c + 1)
                s.op("dve", CP(cout[c % 2][:, :, :], cin[c % 3][:, :, :]), reads=[f"cin{c % 3}"], writes=[f"cout{c % 2}"])
                off = 0 if c < 64 else 1024
                s.dma("sp", None, DMA(dst_v[:, 2 * (c % 64):2 * (c % 64) + 2, off:off + 1024], cout[c % 2][:, :, :]),
                      reads=[f"cout{c % 2}"], writes=["puv16"])

            conv_in(0)
        make_ident(s, identf, "identf")
        for i in range(2):
            s.op("pool", MEMSET(vv[i][:, :, 128:132], 1.0), writes=[f"vv{i}_ones"])
            s.op("pool", MEMSET(vc[i][:, :, 64:68], 1.0), writes=[f"vc{i}_ones"])
        for n_, key in enumerate(["lq1", "lk1", "lq2", "lk2"]):
            s.dma("sp", "lq", DMA(lq[:, n_, :], D[key].partition_broadcast(128)), writes=["lq"])
        s.dma("sp", "lq", DMA(sw8[:], D["subln_w"].partition_broadcast(128)), writes=["sw8"])
        s.op("dve", STT(lj[:], lq[:, 0, :], 1.0, lq[:, 1, :], ALU.mult, ALU.mult, accum_out=lsc[:, 0:1]), reads=["lq"], writes=["lj", "lsc"])
        s.op("dve", STT(lj[:], lq[:, 2, :], 1.0, lq[:, 3, :], ALU.mult, ALU.mult, accum_out=lsc[:, 1:2]), reads=["lq", "lj"], writes=["lj", "lsc"])
        s.op("act", ACTF(lsc[:, 2:4], lsc[:, 0:2], AF.Exp), reads=["lsc"], writes=["lsc2"])
        s.op("dve", TT(neglam[:], lsc[:, 3:4], lsc[:, 2:3], ALU.subtract), reads=["lsc2"], writes=["neglam"])
        s.op("dve", TS(neglam[:], neglam[:], -0.2, None, ALU.add, ALU.add) if False else
             (lambda e: e.tensor_scalar_add(out=neglam[:], in0=neglam[:], scalar1=-0.2)), reads=["neglam"], writes=["neglam"])
        s.op("dve", TSMUL(sw8[:], sw8[:], 0.8), reads=["sw8"], writes=["sw8"])

        for h in range(P2_HEADS):
            b = h % 2
            pb_ = (h // 2) % 2
            s.dma("sp", f"kT{b}", DMA(kT[b][:, :], D["kdT_s"][h]), reads=["kdT_s"], writes=[f"kT{b}"])
            s.dma("sp", f"vv{b}", DMA(vv[b][:, :, 0:128], D["vd_s"][h]), reads=["vd_s"], writes=[f"vv{b}"])
            s.dma("sp", f"qT{b}", DMA(qT[b][:, :], D["qdT_s"][h]), reads=["qdT_s"], writes=[f"qT{b}"])
            if h % 2 == 0:
                s.dma("sp", f"kcT{pb_}", DMA(kcT[pb_][:, :], D["kcT_s"][h // 2]), reads=["kcT_s"], writes=[f"kcT{pb_}"])
                s.dma("sp", f"qcT{pb_}", DMA(qcT[pb_][:, :], D["qcT_s"][h // 2]), reads=["qcT_s"], writes=[f"qcT{pb_}"])
            s.dma("sp", f"vc{b}", DMA(vc[b][:, :, 0:64], D["vc_s"][h]), reads=["vc_s"], writes=[f"vc{b}"])
            s.dma("sp", f"cst{b}", DMA(colb[b][:, :], D["colb"][:, h, :]), writes=[f"colb{b}"])
            s.dma("sp", f"cst{b}", DMA(tdg[b][:, :], D["tdiag"][:, h, :]), writes=[f"tdg{b}"])
            s.dma("sp", f"cst{b}", DMA(cbg[b][:, :, :], D["cbg"][:, h, :, :]), writes=[f"cbg{b}"])
            s.dma("sp", f"cst{b}", DMA(cb0[b][:, :, :], D["cb0"][:, h, :, :]), writes=[f"cb0{b}"])
            hp = (h % 2) * 64
            for m in range(P2_NQ):
                items = [("c", 4 * m + 4 - dlt, dlt) for dlt in (4, 3, 2, 1, 0)]
                dmax = int(160.0 / (128.0 * 2.0 ** (-(h + 1)))) + 1
                diff_L = [L for L in range(4 * m + 4) if (4 * m + 4 - L) <= dmax]
                first_L = diff_L[0] if diff_L else 4 * m + 4
                items += [("d", L, 0) for L in diff_L]
                items.append(("g", 4 * m + 4, 0))
                n = len(items)
                qs = slice(m * 128, (m + 1) * 128)
                if do_conv:
                    conv_step(h * NQ + m)

                def emit_qk(t):
                    kind, L, dlt = items[t]
                    si = t % 2
                    pi = t % 4
                    Sb, Pb = S[si], P[pi]
                    ks = slice(L * 128, (L + 1) * 128)
                    if kind == "c":
                        bt = cb0[b] if m == 0 else cbg[b]
                        btr = f"cb0{b}" if m == 0 else f"cbg{b}"
                        s.op("pe", MM(Sb[:, 0, 0:128], kcT[pb_][hp:hp + 64, ks], qcT[pb_][hp:hp + 64, qs], True, False),
                             reads=[f"kcT{pb_}", f"qcT{pb_}"], writes=[f"S{si}"])
                        s.op("pe", MM(Sb[:, 0, 0:128], identf[:, :], bt[:, dlt, :], False, True),
                             reads=["identf", btr], writes=[f"S{si}"])
                        s.op("act", ACTF(Pb[:, 0:128], Sb[:, 0, 0:128], AF.Exp), reads=[f"S{si}"], writes=[f"P{pi}"])
                    else:
                        dg_ = (kind == "g")
                        s.op("pe", MM(Sb[:, 0, 0:128], kT[b][0:64, ks], qT[b][0:64, qs], True, not dg_),
                             reads=[f"kT{b}", f"qT{b}"], writes=[f"S{si}"])
                        if dg_:
                            s.op("pe", MM(Sb[:, 0, 0:128], identf[:, :], tdg[b][:, 0:128], False, True),
                                 reads=["identf", f"tdg{b}"], writes=[f"S{si}"])
                        s.op("pe", MM(Sb[:, 1, 0:128], kT[b][64:128, ks], qT[b][64:128, qs], True, not dg_),
                             reads=[f"kT{b}", f"qT{b}"], writes=[f"S{si}"])
                        if dg_:
                            s.op("pe", MM(Sb[:, 1, 0:128], identf[:, :], tdg[b][:, 128:256], False, True),
                                 reads=["identf", f"tdg{b}"], writes=[f"S{si}"])
                        if kind == "d":
                            idx = 2 * m * m + 2 * m + L
                            s.op("act", ACTF(Pb[:, 0:256].rearrange("p (a n) -> p a n", a=2), Sb[:, :, 0:128], AF.Exp,
                                             bias=colb[b][:, idx:idx + 1], scale=0.125),
                                 reads=[f"S{si}", f"colb{b}"], writes=[f"P{pi}"])
                        else:
                            s.op("act", ACTF(Pb[:, 0:256].rearrange("p (a n) -> p a n", a=2), Sb[:, :, 0:128], AF.Exp, scale=0.125),
                                 reads=[f"S{si}"], writes=[f"P{pi}"])

                def emit_av(t):
                    kind, L, dlt = items[t]
                    si = t % 4
                    Pb = P[si]
                    if kind == "c":
                        s.op("pe", MM(Oc[:, 0:68], Pb[:, 0:128], vc[b][:, L, :], dlt == 4, dlt == 0),
                             reads=[f"P{si}", f"vc{b}", f"vc{b}_ones"], writes=["Oc"])
                    else:
                        s.op("pe", MM(O1[:, 0:132], Pb[:, 0:128], vv[b][:, L, :], L == first_L, kind == "g"),
                             reads=[f"P{si}", f"vv{b}", f"vv{b}_ones"], writes=["O1"])
                        s.op("pe", MM(O2[:, 0:132], Pb[:, 128:256], vv[b][:, L, :], L == first_L, kind == "g"),
                             reads=[f"P{si}", f"vv{b}", f"vv{b}_ones"], writes=["O2"])

                for t in range(n + 1):
                    if t < n:
                        emit_qk(t)
                    if t >= 1 and P2_MODE >= 2:
                        emit_av(t - 1)
                    if t - 1 == 4 and P2_MODE >= 3:
                        st = ocst[m % 2]
                        s.op("dve", RECIP(r12[:, 2:3], Oc[:, 64:65]), reads=["Oc"], writes=["r12c"])
                        s.op("dve", TSMUL(st[:, :], Oc[:, 0:64], r12[:, 2:3]), reads=["Oc", "r12c"], writes=[f"ocst{m % 2}"])
                        s.dma("sp", f"ocst{m % 2}", DMA(D["oc_s"][m * 128:(m + 1) * 128, h * 64:(h + 1) * 64], st[:, :]),
                              reads=[f"ocst{m % 2}"], writes=["oc_s"])
                if P2_MODE < 3:
                    continue
                st = odst[m % 2]
                s.op("dve", RECIP(r12[:, 0:1], O1[:, 128:129]), reads=["O1"], writes=["r12a"])
                s.op("dve", RECIP(r12[:, 1:2], O2[:, 128:129]), reads=["O2"], writes=["r12b"])
                s.op("dve", TT(r12[:, 3:4], r12[:, 1:2], neglam[:, 0:1], ALU.mult), reads=["r12b", "neglam"], writes=["r12d"])
                s.op("dve", TSMUL(t1[:, :], O1[:, 0:128], r12[:, 0:1]), reads=["O1", "r12a"], writes=["t1"])
                s.op("dve", STT(o_[:, :], O2[:, 0:128], r12[:, 3:4], t1[:, :], ALU.mult, ALU.add), reads=["O2", "r12d", "t1"], writes=["o_"])
                s.op("dve", STT(oj[:, :], o_[:, :], 1.0, o_[:, :], ALU.mult, ALU.mult, accum_out=sq[:, 0:1]), reads=["o_"], writes=["oj", "sq"])
                s.op("dve", TS(sq[:, 1:2], sq[:, 0:1], 1.0 / 128, EPS, ALU.mult, ALU.add), reads=["sq"], writes=["sq1"])
                s.op("act", ACTF(sq[:, 1:2], sq[:, 1:2], AF.Ln), reads=["sq1"], writes=["sq1"])
                s.op("act", ACTF(sq[:, 1:2], sq[:, 1:2], AF.Exp, scale=-0.5), reads=["sq1"], writes=["sq1"])
                s.op("dve", STT(st[:, :], o_[:, :], sq[:, 1:2], sw8[:, :], ALU.mult, ALU.mult), reads=["o_", "sq1", "sw8"], writes=[f"odst{m % 2}"])
                s.dma("sp", f"odst{m % 2}", DMA(D["od_s"][m * 128:(m + 1) * 128, h * 128:(h + 1) * 128], st[:, :]),
                      reads=[f"odst{m % 2}"], writes=["od_s"])
        s.barrier()
        with nc.Block() as block:
            s.emit(block)


def phase3(nc, ses, D):
    with ExitStack() as es:
        def sb(name, shape, dt):
            return es.enter_context(nc.sbuf_tensor("c_" + name, shape, dt))

        def ps(name, shape, dt=F32):
            return es.enter_context(nc.psum_tensor("c_" + name, shape, dt))

        s = Sched(nc, ses, "c")
        c = Ctx()
        wbd = sb("wbd", [128, 8, 1024], BF16)
        wbc = sb("wbc", [128, 4, 1024], BF16)
        wg = sb("wg", [128, 8, 2048], BF16)
        wo = sb("wo", [128, 8, 1024], BF16)
        wst = [sb(f"wst{i}", [128, 1024], F32) for i in range(2)]
        n1 = sb("n1", [128, 1024], F32)
        bg = sb("bg", [128, 16], F32)
        c.ident = sb("ident", [128, 128], BF16)
        c.xt = [sb(f"xt{i}", [128, 4, 1024], F32) for i in range(2)]
        c.xnb = [sb(f"xnb{i}", [128, 4, 1024], BF16) for i in range(1)] * 2
        c.xnT = [sb(f"xnT{i}", [128, 8, 512], BF16) for i in range(1)] * 2
        c.ssq = sb("ssq", [128, 2, 4], F32)
        c.rstd = sb("rstd", [128, 2, 4], F32)
        c.junk = sb("junk", [128, 1024], BF16)
        odb = sb("odb", [128, 4, 1024], BF16)
        ocb = sb("ocb", [128, 4, 512], BF16)
        odT = sb("odT", [128, 8, 512], BF16)
        ocT = sb("ocT", [128, 4, 512], BF16)
        sg = [sb(f"sg{i}", [128, 512], F32) for i in range(2)]
        m12 = [sb(f"m12{i}", [128, 512], F32) for i in range(2)]
        mT = sb("mT", [128, 8, 512], BF16)
        h1 = [sb(f"h1{i}", [128, 1024], F32) for i in range(2)]
        c.tp = [ps(f"tp{i}", [128, 8, 128], BF16) for i in range(2)]
        pa = [ps(f"pa{i}", [128, 512], F32) for i in range(4)]
        ph = [ps(f"ph{i}", [128, 512], F32) for i in range(2)]

        make_ident(s, c.ident, "ident")
        s.dma("sp", "n1", DMA(n1[:], D["n1w"].partition_broadcast(128)), writes=["n1"])
        s.dma("sp", "n1", DMA(bg[:], D["bgate"]), writes=["bg"])
        cnt = [0]
        load_weight(s, c, wbd, "wbd", D["w_bd"], 8, 0, 1024, 0, wst, cnt)
        load_weight(s, c, wbc, "wbc", D["w_bc"], 4, 0, 1024, 0, wst, cnt)
        load_weight(s, c, wg, "wg", D["w_in"], 8, 4608, 2048, 0, wst, cnt)
        load_weight(s, c, wo, "wo", D["w_out"], 8, 0, 1024, 0, wst, cnt)
        for G in range(4):
            i = G % 2
            rows = [D["xall"][(4 * (4 * G + bi) + 4) * 128:(4 * (4 * G + bi) + 5) * 128, :] for bi in range(4)]
            xres = norm_tile_p3(s, c, i, rows, n1)
            s.dma("sp", "odb", DMA(odb[:, :, :], D["od_s"][G * 512:(G + 1) * 512, :].rearrange("(l p) d -> p l d", p=128)),
                  reads=["od_s"], writes=["odb"])
            s.dma("sp", "ocb", DMA(ocb[:, :, :], D["oc_s"][G * 512:(G + 1) * 512, :].rearrange("(l p) d -> p l d", p=128)),
                  reads=["oc_s"], writes=["ocb"])
            tcount = 0
            for bi in range(4):
                for (src, nk, dstT, sres, dres) in ((odb, 8, odT, "odb", "odT"), (ocb, 4, ocT, "ocb", "ocT")):
                    tpi = tcount % 2
                    tcount += 1
                    tpb = c.tp[tpi]
                    for k in range(nk):
                        s.op("pe", TR(tpb[:, k, :], src[:, bi, k * 128:(k + 1) * 128], c.ident[:]), reads=[sres, "ident"], writes=[f"tp{tpi}"])
                    if tpi == 0:
                        s.op("act", ACOPY(dstT[:, :, bi * 128:(bi + 1) * 128], tpb[:, 0:nk, :]), reads=[f"tp{tpi}"], writes=[dres])
                    else:
                        s.op("dve", CP(dstT[:, :, bi * 128:(bi + 1) * 128], tpb[:, 0:nk, :]), reads=[f"tp{tpi}"], writes=[dres])
            for oc in range(8):
                cs = slice(oc * 128, (oc + 1) * 128)
                A, B, C_, D_ = pa
                for k in range(8):
                    s.op("pe", MM(A[:, :], wbd[:, k, cs], odT[:, k, :], k == 0, k == 7), reads=["wbd", "odT"], writes=["pa0"])
                for k in range(4):
                    s.op("pe", MM(B[:, :], wbc[:, k, cs], ocT[:, k, :], k == 0, k == 3), reads=["wbc", "ocT"], writes=["pa1"])
                for k in range(8):
                    s.op("pe", MM(C_[:, :], wg[:, k, cs], c.xnT[0][:, k, :], k == 0, k == 7), reads=["wg"] + xres, writes=["pa2"])
                gs = slice(1024 + oc * 128, 1024 + (oc + 1) * 128)
                for k in range(8):
                    s.op("pe", MM(D_[:, :], wg[:, k, gs], c.xnT[0][:, k, :], k == 0, k == 7), reads=["wg"] + xres, writes=["pa3"])
                s.op("act", ACTF(sg[0][:, :], C_[:, :], AF.Sigmoid, bias=bg[:, oc:oc + 1]), reads=["pa2", "bg"], writes=["sg0"])
                s.op("act", ACTF(sg[1][:, :], D_[:, :], AF.Sigmoid, bias=bg[:, 8 + oc:9 + oc]), reads=["pa3", "bg"], writes=["sg1"])
                s.op("dve", TT(m12[0][:, :], A[:, :], sg[0][:, :], ALU.mult), reads=["pa0", "sg0"], writes=["m120"])
                s.op("dve", TT(m12[1][:, :], B[:, :], sg[1][:, :], ALU.mult), reads=["pa1", "sg1"], writes=["m121"])
                s.op("pool", TT(mT[:, oc, :], m12[0][:, :], m12[1][:, :], ALU.add), reads=["m120", "m121"], writes=["mT"])
            for bi in range(4):
                m = 4 * G + bi
                hb = h1[bi % 2]
                for half in range(2):
                    pb = ph[half]
                    for k in range(8):
                        s.op("pe", MM(pb[:, :], mT[:, k, bi * 128:(bi + 1) * 128], wo[:, k, half * 512:(half + 1) * 512], k == 0, k == 7),
                             reads=["mT", "wo"], writes=[f"ph{half}"])
                    s.op("dve", TT(hb[:, half * 512:(half + 1) * 512], pb[:, :], c.xt[i][:, bi, half * 512:(half + 1) * 512], ALU.add),
                         reads=[f"ph{half}", f"xt{i}"], writes=[f"h1{bi % 2}"])
                s.dma("sp", f"h1{bi % 2}", DMA(D["h1_s"][m * 128:(m + 1) * 128, :], hb[:, :]), reads=[f"h1{bi % 2}"], writes=["h1_s"])
        s.barrier()
        with nc.Block() as block:
            s.emit(block)


def norm_tile_p3(s, c, i, src_rows, n1):
    xt, xnb, xnT = c.xt[i], c.xnb[0], c.xnT[0]
    for bi, src in enumerate(src_rows):
        s.dma("sp", f"xt{i}", DMA(xt[:, bi, :], src), writes=[f"xt{i}"], part=(bi != 0))
    for bi in range(4):
        s.op("act", ACTF(c.junk[:], xt[:, bi, :], AF.Square, accum_out=c.ssq[:, i, bi:bi + 1]), reads=[f"xt{i}"],
             writes=["junk", f"ssq{i}"])
    s.op("dve", TS(c.rstd[:, i, :], c.ssq[:, i, :], 1.0 / 1024, EPS, ALU.mult, ALU.add), reads=[f"ssq{i}"], writes=[f"rstd{i}"])
    s.op("act", ACTF(c.rstd[:, i, :], c.rstd[:, i, :], AF.Ln), reads=[f"rstd{i}"], writes=[f"rstd{i}"])
    s.op("act", ACTF(c.rstd[:, i, :], c.rstd[:, i, :], AF.Exp, scale=-0.5), reads=[f"rstd{i}"], writes=[f"rstd{i}"])
    for bi in range(4):
        s.op("dve", STT(xnb[:, bi, :], xt[:, bi, :], c.rstd[:, i, bi:bi + 1], n1[:], ALU.mult, ALU.mult),
             reads=[f"xt{i}", f"rstd{i}", "n1"], writes=[f"xnb_{bi}"])
    for bi in range(4):
        tpb = c.tp[bi % 2]
        for k in range(8):
            s.op("pe", TR(tpb[:, k, :], xnb[:, bi, k * 128:(k + 1) * 128], c.ident[:]), reads=[f"xnb_{bi}", "ident"],
                 writes=[f"tp{bi % 2}"])
        if bi % 2 == 0:
            s.op("act", ACOPY(xnT[:, :, bi * 128:(bi + 1) * 128], tpb[:, :, :]), reads=[f"tp{bi % 2}"], writes=[f"xnT_{bi}"])
        else:
            s.op("dve", CP(xnT[:, :, bi * 128:(bi + 1) * 128], tpb[:, :, :]), reads=[f"tp{bi % 2}"], writes=[f"xnT_{bi}"])
    return [f"xnT_{bi}" for bi in range(4)]


NUV = 16


def phase4(nc, ses, D):
    with ExitStack() as es:
        def sb(name, shape, dt):
            return es.enter_context(nc.sbuf_tensor("d_" + name, shape, dt))

        def ps(name, shape, dt=F32):
            return es.enter_context(nc.psum_tensor("d_" + name, shape, dt))

        s = Sched(nc, ses, "d")
        wq = sb("wq", [128, 8, 1024], F32)
        kt0 = sb("kt0", [128, 8, 128], F32)
        keysT = sb("keysT", [128, 8, 128], F32)
        n2 = sb("n2", [128, 1024], F32)
        nf = sb("nf", [128, 1024], F32)
        ident = sb("identf", [128, 128], F32)
        h1 = [sb(f"h1{i}", [128, 1024], F32) for i in range(3)]
        xn2 = [sb(f"xn2{i}", [128, 1024], F32) for i in range(3)]
        junk = sb("junk", [128, 1024], F32)
        junk2 = sb("junk2", [128, 1024], BF16)
        xn2T = sb("xn2T", [128, 8, 128], F32)
        qTt = sb("qTt", [128, 8, 128], F32)
        sc = sb("sc", [128, 16, 128], F32)
        sc2 = sb("sc2", [128, 128], F32)
        mx = sb("mx", [128, 16, 16], F32)
        mi = sb("mi", [128, 16, 16], U32)
        mif = sb("mif", [128, 16, 16], F32)
        cand = sb("cand", [128, 8, 256], F32)
        cand2 = sb("cand2", [128, 256], F32)
        ci = sb("ci", [128, 8, 16], U32)
        cii = sb("cii", [128, 8, 16], U32)
        cij = sb("cij", [128, 8, 16], U32)
        ciif = sb("ciif", [128, 8, 16], F32)
        cijf = sb("cijf", [128, 8, 16], F32)
        iota16 = sb("iota16", [128, 16], F32)
        eq = sb("eq", [128, 8, 16, 16], F32)
        sel1 = sb("sel1", [128, 8, 16], F32)
        sel2 = sb("sel2", [128, 8, 16], F32)
        sc16 = sb("sc16", [128, 8, 16], F32)
        ex = sb("ex", [128, 8, 16], F32)
        zz = sb("zz", [128, 16], F32)
        gg = [sb(f"gg{i}", [128, 128], F32) for i in range(3)]
        ef = sb("ef", [128, 128], F32)
        ei = [sb(f"ei{i}", [128, 128], I32) for i in range(3)]
        hid = [sb(f"hid{i}", [128, 128], F32) for i in range(2)]
        gl = [sb(f"gl{i}", [128, 128], F32) for i in range(2)]
        aa = [sb(f"aa{i}", [128, 128], F32) for i in range(2)]
        dg = [sb(f"dg{i}", [128, 128], BF16) for i in range(4)]
        uvb = [sb(f"uv{i}", [128, 2048], BF16) for i in range(NUV)]
        xn2b = [sb(f"xn2b{i}", [128, 1024], BF16) for i in range(3)]
        h2 = sb("h2", [128, 1024], F32)
        ssq = sb("ssq", [128, 8], F32)
        yb = [sb("yb0", [128, 1024], F32)] * 2
        tq = [ps(f"tq{i}", [128, 512], F32) for i in range(2)]
        sps = [ps(f"sps{i}", [128, 512], F32) for i in range(4)]
        accp = [ps(f"accp{i}", [128, 512], F32) for i in range(2)]

        s.op("pool", MEMSET(ident[:], 0.0), writes=["ident"])
        s.op("pool", lambda e: e.affine_select(out=ident[:], in_=ident[:], pattern=[[-1, 128]], compare_op=ALU.not_equal,
                                               fill=1.0, base=0, channel_multiplier=1), reads=["ident"], writes=["ident"])
        s.op("pool", lambda e: e.iota(iota16[:], pattern=[[1, 16]], base=0, channel_multiplier=0, allow_small_or_imprecise_dtypes=True), writes=["iota16"])
        s.dma("sp", None, DMA(n2[:], D["n2w"].partition_broadcast(128)), writes=["n2"])
        s.dma("sp", None, DMA(nf[:], D["fnw"].partition_broadcast(128)), writes=["nf"])
        for k in range(8):
            s.dma("sp", None, DMA(wq[:, k, :], D["wq"][k * 128:(k + 1) * 128, :]), writes=["wq"])
        wqres = ["wq"]
        s.dma("sp", None, DMA(kt0[:, :, :].rearrange("n h (c d) -> n h c d", c=2), D["keys"].rearrange("h c n d -> n h c d")), writes=["kt0"])
        for h in range(8):
            s.op("pe", TR(tq[0][:, h % 4 * 128:(h % 4 + 1) * 128], kt0[:, h, :], ident[:]), reads=["kt0", "ident"], writes=["tq0"])
            if h % 4 == 3:
                s.op("act", ACOPY(keysT[:, h - 3:h + 1, :], tq[0][:, :].rearrange("p (h n) -> p h n", h=4)), reads=["tq0"], writes=["keysT"])

        def rms(src, src_res, w_tile, w_res, dst, dst_res, col, jk, jkres):
            s.op("act", ACTF(jk[:], src, AF.Square, accum_out=ssq[:, col:col + 1]), reads=[src_res], writes=[jkres, f"ssq{col}"])
            s.op("dve", TS(ssq[:, col + 1:col + 2], ssq[:, col:col + 1], 1.0 / 1024, EPS, ALU.mult, ALU.add), reads=[f"ssq{col}"], writes=[f"ssq{col + 1}"])
            s.op("act", ACTF(ssq[:, col + 1:col + 2], ssq[:, col + 1:col + 2], AF.Ln), reads=[f"ssq{col + 1}"], writes=[f"ssq{col + 1}"])
            s.op("act", ACTF(ssq[:, col + 1:col + 2], ssq[:, col + 1:col + 2], AF.Exp, scale=-0.5), reads=[f"ssq{col + 1}"], writes=[f"ssq{col + 1}"])
            s.op("dve", STT(dst, src, ssq[:, col + 1:col + 2], w_tile[:], ALU.mult, ALU.mult), reads=[src_res, f"ssq{col + 1}", w_res], writes=[dst_res])

        def idx_block(m):
            b = m % 3
            s.dma("sp", None, DMA(h1[b][:, :], D["h1_s"][m * 128:(m + 1) * 128, :]), reads=["h1_s"], writes=[f"h1{b}"])
            rms(h1[b][:, :], f"h1{b}", n2, "n2", xn2[b][:, :], f"xn2{b}", 0, junk, "junk")
            s.op("pool", CP(xn2b[b][:, :], xn2[b][:, :]), reads=[f"xn2{b}"], writes=[f"xn2b{b}"])
            yield
            for k in range(8):
                tb = tq[k // 4]
                s.op("pe", TR(tb[:, (k % 4) * 128:(k % 4 + 1) * 128], xn2[b][:, k * 128:(k + 1) * 128], ident[:]), reads=[f"xn2{b}", "ident"], writes=[f"tq{k // 4}"])
                if k % 4 == 3:
                    s.op("act", ACOPY(xn2T[:, k - 3:k + 1, :], tb[:, :].rearrange("p (h n) -> p h n", h=4)), reads=[f"tq{k // 4}"], writes=["xn2T"])
            for h in range(8):
                qb = tq[h // 4]
                for k in range(8):
                    s.op("pe", MM(qb[:, (h % 4) * 128:(h % 4 + 1) * 128], wq[:, k, h * 128:(h + 1) * 128], xn2T[:, k, :], k == 0, k == 7),
                         reads=wqres + ["xn2T"], writes=[f"tq{h // 4}"])
                if h % 4 == 3:
                    s.op("act", ACOPY(qTt[:, h - 3:h + 1, :], qb[:, :].rearrange("p (h n) -> p h n", h=4)), reads=[f"tq{h // 4}"], writes=["qTt"])
            for h in range(8):
                for cc in range(2):
                    bk = sps[cc * 2 + h // 4]
                    s.op("pe", MM(bk[:, (h % 4) * 128:(h % 4 + 1) * 128], qTt[cc * 64:(cc + 1) * 64, h, :], keysT[cc * 64:(cc + 1) * 64, h, :], True, True),
                         reads=["qTt", "keysT"], writes=[f"sps{cc * 2 + h // 4}"])
            for bq in range(4):
                s.op("act", ACOPY(sc[:, bq * 4:bq * 4 + 4, :], sps[bq][:, :].rearrange("p (h n) -> p h n", h=4)), reads=[f"sps{bq}"], writes=["sc"])
            yield
            for hc in range(16):
                if hc % 4 == 0 and hc:
                    yield
                s.op("dve", lambda e, hc=hc: e.max(out=mx[:, hc, 0:8], in_=sc[:, hc, :]), reads=["sc"], writes=["mx"])
                s.op("dve", lambda e, hc=hc: e.max_index(out=mi[:, hc, 0:8], in_max=mx[:, hc, 0:8], in_values=sc[:, hc, :]), reads=["sc", "mx"], writes=["mi"])
                s.op("dve", lambda e, hc=hc: e.match_replace(out=sc2[:, :], in_to_replace=mx[:, hc, 0:8], in_values=sc[:, hc, :], imm_value=-1e30), reads=["sc", "mx"], writes=["sc2"])
                s.op("dve", lambda e, hc=hc: e.max(out=mx[:, hc, 8:16], in_=sc2[:, :]), reads=["sc2"], writes=["mx"])
                s.op("dve", lambda e, hc=hc: e.max_index(out=mi[:, hc, 8:16], in_max=mx[:, hc, 8:16], in_values=sc2[:, :]), reads=["sc2", "mx"], writes=["mi"])
            s.op("dve", CP(mif[:, :, :], mi[:, :, :]), reads=["mi"], writes=["mif"])
            yield
            mx4 = mx[:, :, :].rearrange("p (c h) k -> p c h k", c=2)
            mif4 = mif[:, :, :].rearrange("p (c h) k -> p c h k", c=2)
            cand4 = cand[:, :, :].rearrange("p h (i j) -> p h i j", i=16)
            for h in range(8):
                s.op("dve", TT(cand4[:, h, :, :], mx4[:, 0, h, :].unsqueeze(2).to_broadcast([128, 16, 16]),
                               mx4[:, 1, h, :].unsqueeze(1).to_broadcast([128, 16, 16]), ALU.add), reads=["mx"], writes=["cand"])
            yield
            for h in range(8):
                if h == 4:
                    yield
                s.op("dve", lambda e, h=h: e.max(out=sc16[:, h, 0:8], in_=cand[:, h, :]), reads=["cand"], writes=["sc16"])
                s.op("dve", lambda e, h=h: e.max_index(out=ci[:, h, 0:8], in_max=sc16[:, h, 0:8], in_values=cand[:, h, :]), reads=["cand", "sc16"], writes=["ci"])
                s.op("dve", lambda e, h=h: e.match_replace(out=cand2[:, :], in_to_replace=sc16[:, h, 0:8], in_values=cand[:, h, :], imm_value=-1e30), reads=["cand", "sc16"], writes=["cand2"])
                s.op("dve", lambda e, h=h: e.max(out=sc16[:, h, 8:16], in_=cand2[:, :]), reads=["cand2"], writes=["sc16"])
                s.op("dve", lambda e, h=h: e.max_index(out=ci[:, h, 8:16], in_max=sc16[:, h, 8:16], in_values=cand2[:, :]), reads=["cand2", "sc16"], writes=["ci"])
            yield
            s.op("dve", TT(ex[:, :, :], sc16[:, :, :], sc16[:, :, 0:1].to_broadcast([128, 8, 16]), ALU.subtract), reads=["sc16"], writes=["ex"])
            s.op("act", ACTF(ex[:, :, :], ex[:, :, :], AF.Exp), reads=["ex"], writes=["ex"])
            s.op("dve", lambda e: e.reduce_sum(out=zz[:, 0:8], in_=ex[:, :, :], axis=mybir.AxisListType.X), reads=["ex"], writes=["zz"])
            s.op("dve", RECIP(zz[:, 8:16], zz[:, 0:8]), reads=["zz"], writes=["zz2"])
            g3 = gg[b][:, :].rearrange("p (h k) -> p h k", h=8)
            s.op("dve", TT(g3, ex[:, :, :], zz[:, 8:16].unsqueeze(2).to_broadcast([128, 8, 16]), ALU.mult), reads=["ex", "zz2"], writes=[f"gg{b}"])
            yield
            s.op("dve", lambda e: e.tensor_single_scalar(out=cii[:], in_=ci[:], scalar=4, op=ALU.logical_shift_right), reads=["ci"], writes=["cii"])
            s.op("dve", lambda e: e.tensor_single_scalar(out=cij[:], in_=ci[:], scalar=15, op=ALU.bitwise_and), reads=["ci"], writes=["cij"])
            s.op("dve", CP(ciif[:], cii[:]), reads=["cii"], writes=["ciif"])
            s.op("dve", CP(cijf[:], cij[:]), reads=["cij"], writes=["cijf"])
            iob = iota16[:, :].unsqueeze(1).unsqueeze(1).to_broadcast([128, 8, 16, 16])
            for (cf, cc_, sel, nm) in ((ciif, 0, sel1, "1"), (cijf, 1, sel2, "2")):
                yield
                s.op("dve", TT(eq[:], cf[:, :, :].unsqueeze(3).to_broadcast([128, 8, 16, 16]), iob, ALU.is_equal),
                     reads=["ciif", "cijf", "iota16", "eq"], writes=["eq"])
                s.op("dve", TT(eq[:], eq[:], mif4[:, cc_, :, :].unsqueeze(2).to_broadcast([128, 8, 16, 16]), ALU.mult), reads=["eq", "mif"], writes=["eq"])
                s.op("dve", (lambda sel: lambda e: e.reduce_sum(out=sel[:], in_=eq[:], axis=mybir.AxisListType.X))(sel), reads=["eq"], writes=["sel" + nm])
            s.op("dve", STT(ef[:, :].rearrange("p (h k) -> p h k", h=8), sel1[:], 128.0, sel2[:], ALU.mult, ALU.add), reads=["sel1", "sel2"], writes=["ef"])
            s.op("dve", lambda e: e.tensor_scalar_min(out=ef[:, :], in0=ef[:, :], scalar1=16383.0), reads=["ef"], writes=["ef"])
            s.op("dve", CP(ei[b][:, :], ef[:, :]), reads=["ef"], writes=[f"ei{b}"])

        def gather(tab, buf, bres, eib, col):
            s.dma("pool", None, lambda e: e.indirect_dma_start(out=buf[:, :], out_offset=None, in_=D[tab],
                                                                in_offset=bass.IndirectOffsetOnAxis(ap=eib[:, col:col + 1], axis=0)),
                  reads=[], writes=[bres])

        GR = 4
        NG = 128 // GR
        NSLOT = NUV // GR

        def rows_block(m, idxgen):
            b = m % 3
            b2 = m % 2
            eires = f"ei{b}"
            eib = ei[b]

            def U(g):
                for r in range(g * GR, (g + 1) * GR):
                    buf = uvb[r % NUV]
                    s.dma("pool", None, (lambda buf, r: lambda e: e.indirect_dma_start(
                        out=buf[:, :], out_offset=None, in_=D["puv16"],
                        in_offset=bass.IndirectOffsetOnAxis(ap=eib[:, r:r + 1], axis=0)))(buf, r),
                          reads=[eires], writes=[f"uvg{g % NSLOT}"], part=(r % GR != 0))

            def H(g):
                for r in range(g * GR, (g + 1) * GR):
                    s.op("dve", STT(junk2[:, :], uvb[r % NUV][:, 0:1024], 1.0, xn2b[b][:, :], ALU.mult, ALU.mult, accum_out=hid[b2][:, r:r + 1]),
                         reads=[f"uvg{g % NSLOT}", f"xn2b{b}", "junk2"], writes=["junk2", f"hid{b2}_{g}"] + (["hfence"] if r % GR == 0 else []))

            def A(g):
                cs = slice(g * GR, (g + 1) * GR)
                s.op("act", ACTF(gl[b2][:, cs], hid[b2][:, cs], AF.Gelu), reads=[f"hid{b2}_{g}", "hfence"], writes=[f"gl{b2}_{g}"])
                s.op("dve", TT(aa[b2][:, cs], gl[b2][:, cs], gg[b][:, cs], ALU.mult), reads=[f"gl{b2}_{g}", f"gg{b}"], writes=[f"aa{b2}_{g}", "hfence"])

            def Pm(g):
                for r in range(g * GR, (g + 1) * GR):
                    di = r % 4
                    s.op("act", (lambda di, r: lambda e: e.mul(out=dg[di][:, :], in_=ident[:, :], mul=aa[b2][:, r:r + 1]))(di, r),
                         reads=["ident", f"aa{b2}_{g}"], writes=[f"dg{di}"])
                    for half in range(2):
                        s.op("pe", MM(accp[half][:, :], dg[di][:, :], uvb[r % NUV][:, 1024 + half * 512:1024 + (half + 1) * 512], r == 0, r == 127),
                             reads=[f"dg{di}", f"uvg{g % NSLOT}"], writes=[f"accp{half}"])

            for g0 in range(NSLOT):
                U(g0)
            for g in range(NG + 1):
                if g < NG:
                    H(g)
                if g >= 1:
                    A(g - 1)
                    Pm(g - 1)
                    if g - 1 + NSLOT < NG:
                        U(g - 1 + NSLOT)
                if idxgen is not None:
                    next(idxgen, None)
            if idxgen is not None:
                for _ in idxgen:
                    pass
            for half in range(2):
                s.op("dve", TT(h2[:, half * 512:(half + 1) * 512], accp[half][:, :], h1[b][:, half * 512:(half + 1) * 512], ALU.add),
                     reads=[f"accp{half}", f"h1{b}"], writes=["h2"])
            rms(h2[:, :], "h2", nf, "nf", yb[0][:, :], "yb0", 2, junk, "junk")
            s.dma("sp", None, DMA(D["y"][m * 128:(m + 1) * 128, :], yb[0][:, :]), reads=["yb0"], writes=["y"])

        for _ in idx_block(0):
            pass
        for _ in idx_block(1):
            pass
        for m in range(NQ):
            rows_block(m, idx_block(m + 2) if m + 2 < NQ else None)
        s.barrier()
        with nc.Block() as block:
            s.emit(block)


PHASES = (1, 2, 3, 4)


def build_program(phases=PHASES, dbg=False):
    nc = bass.Bass("TRN2", target_bir_lowering=False)
    D = {}

    def din(name, shape, dt=F32):
        D[name] = nc.dram_tensor(name, shape, dt, kind="ExternalInput").ap()

    def dscr(name, shape, dt):
        D[name] = nc.dram_tensor(name, shape, dt, kind="ExternalOutput" if dbg else "Internal").ap()

    din("xall", [NT, 1024])
    din("n1w", [1, 1024])
    din("w_in", [1024, 6656])
    din("bgate", [128, 16])
    for k in ("lq1", "lk1", "lq2", "lk2"):
        din(k, [1, 64])
    din("subln_w", [1, 128])
    din("w_bd", [1024, 1024])
    din("w_bc", [512, 1024])
    din("w_out", [1024, 1024])
    din("n2w", [1, 1024])
    din("wq", [1024, 1024])
    din("keys", [8, 2, 128, 64])
    npe = 16384 if 4 in phases else 128
    din("pu", [npe, 1024])
    din("pv", [npe, 1024])
    din("fnw", [1, 1024])
    din("colb", [128, 8, 544])
    din("tdiag", [128, 8, 256])
    din("cbg", [128, 8, 5, 128])
    din("cb0", [128, 8, 5, 128])
    dscr("kdT_s", [8, 128, NT], BF16)
    dscr("kcT_s", [4, 128, NT], BF16)
    dscr("vd_s", [8, 128, NB, 128], BF16)
    dscr("vc_s", [8, 128, NB, 64], BF16)
    dscr("qdT_s", [8, 128, 2048], BF16)
    dscr("qcT_s", [4, 128, 2048], BF16)
    dscr("od_s", [2048, 1024], BF16)
    dscr("oc_s", [2048, 512], BF16)
    dscr("h1_s", [2048, 1024], F32)
    D["puv16"] = nc.dram_tensor("puv16", [npe, 2048], BF16, kind="Internal").ap()
    D["y"] = nc.dram_tensor("y", [2048, 1024], F32, kind="ExternalOutput").ap()
    with ExitStack() as ses:
        for ph, fn in ((1, phase1), (2, phase2), (3, phase3), (4, phase4)):
            if ph in phases:
                fn(nc, ses, D)
    return nc


def host_tables(j, rel_bias):
    slopes = np.exp2(-8.0 * np.arange(1, 9, dtype=np.float64) / 8)
    p = np.arange(128, dtype=np.float64)
    colb = np.zeros((128, 8, 544), np.float32)
    for m in range(16):
        for L in range(4 * m + 4):
            idx = 2 * m * m + 2 * m + L
            dl = 4 * m + 4 - L
            if L < 4 - j:
                colb[:, :, idx] = NEG
            else:
                colb[:, :, idx] = (slopes[None, :] * (p[:, None] - 128.0 * dl - 127.0)).astype(np.float32)
    ki = p[:, None]
    qi = p[None, :]
    allowed = (np.floor(ki / 64) <= np.floor(qi / 64))
    td = np.where(allowed[None], slopes[:, None, None] * (np.minimum(ki, 2 * qi - ki)[None] - 127.0), NEG)
    tdiag = np.zeros((128, 8, 256), np.float32)
    tdiag[:, :, 0:128] = 8.0 * td.transpose(1, 0, 2)
    tdiag[:, :, 128:256] = 8.0 * td.transpose(1, 0, 2)
    cbg = np.zeros((128, 8, 5, 128), np.float32)
    cb0 = np.zeros((128, 8, 5, 128), np.float32)
    kI = np.arange(128)[:, None]
    qI = np.arange(128)[None, :]
    for dlt in range(5):
        rel = (qI - kI) + 128 * dlt
        qc = (512 + qI) // 64
        kc = (512 - 128 * dlt + kI) // 64
        ok = (qc - kc >= 0) & (qc - kc <= 8)
        tab = rel_bias[:, np.clip(rel, -128, 128) + 128]
        tile_ = np.where(ok[None], tab, np.float32(NEG)).astype(np.float32)
        cbg[:, :, dlt, :] = tile_.transpose(1, 0, 2)
        if dlt <= j:
            cb0[:, :, dlt, :] = tile_.transpose(1, 0, 2)
        else:
            cb0[:, :, dlt, :] = NEG
    return colb, tdiag, cbg, cb0


def make_in_maps(inputs):
    x = np.asarray(inputs["x"], np.float32)
    rel = np.asarray(inputs["chunk_rel_bias"], np.float32)[0]
    shared = {
        "n1w": np.ascontiguousarray(inputs["norm1_w"], dtype=np.float32).reshape(1, 1024),
        "w_in": np.ascontiguousarray(inputs["w_in"][0], dtype=np.float32),
        "bgate": np.ascontiguousarray(np.asarray(inputs["b_gate"][0], np.float32).reshape(16, 128).T),
        "lq1": np.asarray(inputs["diff_lq1"], np.float32).reshape(1, 64),
        "lk1": np.asarray(inputs["diff_lk1"], np.float32).reshape(1, 64),
        "lq2": np.asarray(inputs["diff_lq2"], np.float32).reshape(1, 64),
        "lk2": np.asarray(inputs["diff_lk2"], np.float32).reshape(1, 64),
        "subln_w": np.asarray(inputs["diff_subln_w"], np.float32).reshape(1, 128),
        "w_bd": np.ascontiguousarray(inputs["w_branch_diff"][0], dtype=np.float32),
        "w_bc": np.ascontiguousarray(inputs["w_branch_chunk"][0], dtype=np.float32),
        "w_out": np.ascontiguousarray(inputs["w_out"][0], dtype=np.float32),
        "n2w": np.asarray(inputs["norm2_w"], np.float32).reshape(1, 1024),
        "wq": np.ascontiguousarray(inputs["peer_wq"][0], dtype=np.float32),
        "keys": np.ascontiguousarray(inputs["peer_keys"][0], dtype=np.float32),
        "pu": np.ascontiguousarray(inputs["peer_u"][0], dtype=np.float32),
        "pv": np.ascontiguousarray(inputs["peer_v"][0], dtype=np.float32),
        "fnw": np.asarray(inputs["final_norm_w"], np.float32).reshape(1, 1024),
    }
    maps = []
    for c in range(8):
        b, j = c // 4, c % 4
        xall = np.zeros((NT, 1024), np.float32)
        g0 = max(0, j - 4)
        lo = (4 - j) * 128
        n = min(NT - lo, 8192)
        xall[lo:lo + n] = x[b, 0:n]
        colb, tdiag, cbg, cb0 = host_tables(j, rel)
        d = dict(shared)
        d.update({"xall": xall, "colb": colb, "tdiag": tdiag, "cbg": cbg, "cb0": cb0})
        maps.append(d)
    return maps


_NC_CACHE = {}


def kernel(**inputs):
    if "nc" not in _NC_CACHE:
        _NC_CACHE["nc"] = build_program()
    nc = _NC_CACHE["nc"]
    maps = make_in_maps(inputs)
    res = run_bass_kernel_spmd(nc, maps, core_ids=list(range(8)))
    out = np.zeros((2, 8192, 1024), np.float32)
    for c in range(8):
        b, j = c // 4, c % 4
        y = np.asarray(res.results[c]["y"]).reshape(16, 128, 1024)
        for m in range(16):
            gb = 4 * m + j
            out[b, gb * 128:(gb + 1) * 128] = y[m]
    return out
```

```python
import numpy as np
from contextlib import ExitStack
import concourse.bass as bass
import concourse.mybir as mybir
from concourse.bass_utils import run_bass_kernel_spmd

F32 = mybir.dt.float32
BF16 = mybir.dt.bfloat16
I32 = mybir.dt.int32
U32 = mybir.dt.uint32
ALU = mybir.AluOpType
AF = mybir.ActivationFunctionType

NB = 68
NT = NB * 128
NQ = 16
EPS = 1e-6
NEG = -30000.0
P2_HEADS = 8
P2_MODE = 3
P2_NQ = 16


class Sched:
    ENGS = ("pe", "act", "dve", "pool", "sp")

    def __init__(self, nc, es, prefix="p"):
        self.nc = nc
        self.es = es
        self.prefix = prefix
        self.sem = {}
        self.cnt = {}
        self.prog = {}
        for name in self.ENGS:
            self.sem[name] = es.enter_context(nc.semaphore(prefix + "s_" + name))
            self.cnt[name] = 0
            self.prog[name] = []
        self.seen = {name: {} for name in self.ENGS}
        self.res_w = {}
        self.res_r = {}
        self.dma_sem = {}
        self.dma_cnt = {}

    def _dma_tag(self, tag):
        if tag not in self.dma_sem:
            self.dma_sem[tag] = self.es.enter_context(self.nc.semaphore(self.prefix + "d_" + tag))
            self.dma_cnt[tag] = 0
        return self.dma_sem[tag]

    def _sem_of(self, key):
        return self.sem[key[1]] if key[0] == "e" else self.dma_sem[key[1]]

    def _collect(self, reads, writes, skip_waw=False):
        deps = {}

        def add(tok):
            if tok is None:
                return
            k, v = tok
            if deps.get(k, 0) < v:
                deps[k] = v

        for r in reads:
            add(self.res_w.get(r))
        for w in writes:
            if not skip_waw:
                add(self.res_w.get(w))
            for k, v in self.res_r.get(w, {}).items():
                add((k, v))
        return deps

    def _emit_waits(self, ename, deps):
        seen = self.seen[ename]
        for k, v in deps.items():
            if k == ("e", "pe") and ename == "pe":
                continue
            if seen.get(k, 0) >= v:
                continue
            self.prog[ename].append(("wait", self._sem_of(k), v))
            seen[k] = v

    def _update(self, tok, reads, writes):
        k, v = tok
        for w in writes:
            self.res_w[w] = tok
            self.res_r[w] = {}
        for r in reads:
            d = self.res_r.setdefault(r, {})
            if d.get(k, 0) < v:
                d[k] = v

    def op(self, ename, fn, reads=(), writes=()):
        deps = self._collect(reads, writes)
        self._emit_waits(ename, deps)
        self.cnt[ename] += 1
        self.prog[ename].append(("ins", fn, self.sem[ename], 1))
        self._update((("e", ename), self.cnt[ename]), reads, writes)

    def dma(self, qname, tag, fn, reads=(), writes=(), part=False):
        tag = writes[0] if writes else reads[0]
        if tag in ("kdT_s", "kcT_s", "vd_s", "vc_s", "qdT_s", "qcT_s", "od_s", "oc_s", "h1_s", "y", "puv16"):
            tag = tag + "_" + (reads[0] if reads else "w")
        sem = self._dma_tag(tag)
        deps = self._collect(reads, writes, skip_waw=part)
        self._emit_waits(qname, deps)
        self.dma_cnt[tag] += 16
        self.prog[qname].append(("ins", fn, sem, 16))
        self._update((("d", tag), self.dma_cnt[tag]), reads, writes)

    def barrier(self):
        for ename in self.ENGS:
            for tag, c in self.dma_cnt.items():
                if c:
                    self.prog[ename].append(("wait", self.dma_sem[tag], c))
            for name, c in self.cnt.items():
                if c and name != ename:
                    self.prog[ename].append(("wait", self.sem[name], c))

    def replay(self, ename, e):
        for item in self.prog[ename]:
            if item[0] == "wait":
                e.wait_ge(item[1], item[2])
            else:
                item[1](e).then_inc(item[2], item[3])

    def emit(self, block):
        s = self

        @block.tensor
        def _(e):
            s.replay("pe", e)

        @block.scalar
        def _(e):
            s.replay("act", e)

        @block.vector
        def _(e):
            s.replay("dve", e)

        @block.gpsimd
        def _(e):
            s.replay("pool", e)

        @block.sync
        def _(e):
            s.replay("sp", e)


def MM(out, lhsT, rhs, start, stop):
    return lambda e: e.matmul(out, lhsT=lhsT, rhs=rhs, start=start, stop=stop)


def TR(out, in_, ident):
    return lambda e: e.transpose(out, in_, ident)


def ACTF(out, in_, func, **kw):
    return lambda e: e.activation(out=out, in_=in_, func=func, **kw)


def ACOPY(out, in_):
    return lambda e: e.copy(out=out, in_=in_)


def CP(out, in_):
    return lambda e: e.tensor_copy(out=out, in_=in_)


def DMA(out, in_):
    return lambda e: e.dma_start(out=out, in_=in_)


def STT(out, in0, scalar, in1, op0, op1, accum_out=None):
    if accum_out is None:
        return lambda e: e.scalar_tensor_tensor(out=out, in0=in0, scalar=scalar, in1=in1, op0=op0, op1=op1)
    return lambda e: e.scalar_tensor_tensor(out=out, in0=in0, scalar=scalar, in1=in1, op0=op0, op1=op1, accum_out=accum_out)


def TS(out, in0, s1, s2, op0, op1):
    return lambda e: e.tensor_scalar(out=out, in0=in0, scalar1=s1, scalar2=s2, op0=op0, op1=op1)


def TSMUL(out, in0, s1):
    return lambda e: e.tensor_scalar_mul(out=out, in0=in0, scalar1=s1)


def TT(out, in0, in1, op):
    return lambda e: e.tensor_tensor(out=out, in0=in0, in1=in1, op=op)


def RECIP(out, in_):
    return lambda e: e.reciprocal(out=out, in_=in_)


def MEMSET(ap, v):
    return lambda e: e.memset(ap, v)


def make_ident(s, ident, name):
    s.op("pool", MEMSET(ident[:], 0.0), writes=[name])
    s.op("pool", lambda e: e.affine_select(out=ident[:], in_=ident[:], pattern=[[-1, 128]], compare_op=ALU.not_equal,
                                           fill=1.0, base=0, channel_multiplier=1), reads=[name], writes=[name])


class Ctx:
    pass


def load_weight(s, c, dst, dst_res, src, rows_k, col0, ncols, dcol0, wst, cnt):
    for k in range(rows_k):
        for cc in range(0, ncols, 1024):
            n = min(1024, ncols - cc)
            j = cnt[0] % len(wst)
            cnt[0] += 1
            s.dma("sp", f"wst{j}", DMA(wst[j][:, 0:n], src[k * 128:(k + 1) * 128, col0 + cc:col0 + cc + n]), writes=[f"wst{j}"])
            eng = "pool" if (cnt[0] % 2) else "dve"
            s.op(eng, CP(dst[:, k, dcol0 + cc:dcol0 + cc + n], wst[j][:, 0:n]), reads=[f"wst{j}"], writes=[dst_res])


def norm_tile(s, c, i, src_rows, w_tile, w_res, bf16_out=True):
    xt, xnb, xnT = c.xt[i], c.xnb[i], c.xnT[i]
    for bi, src in enumerate(src_rows):
        s.dma("sp", f"xt{i}", DMA(xt[:, bi, :], src), writes=[f"xt{i}"], part=(bi != 0))
    for bi in range(4):
        s.op("act", ACTF(c.junk[:], xt[:, bi, :], AF.Square, accum_out=c.ssq[:, i, bi:bi + 1]), reads=[f"xt{i}"],
             writes=["junk", f"ssq{i}"])
    s.op("dve", TS(c.rstd[:, i, :], c.ssq[:, i, :], 1.0 / 1024, EPS, ALU.mult, ALU.add), reads=[f"ssq{i}"], writes=[f"rstd{i}"])
    s.op("act", ACTF(c.rstd[:, i, :], c.rstd[:, i, :], AF.Ln), reads=[f"rstd{i}"], writes=[f"rstd{i}"])
    s.op("act", ACTF(c.rstd[:, i, :], c.rstd[:, i, :], AF.Exp, scale=-0.5), reads=[f"rstd{i}"], writes=[f"rstd{i}"])
    for bi in range(4):
        s.op("dve", STT(xnb[:, bi, :], xt[:, bi, :], c.rstd[:, i, bi:bi + 1], w_tile[:], ALU.mult, ALU.mult),
             reads=[f"xt{i}", f"rstd{i}", w_res], writes=[f"xnb{i}_{bi}"])
    for bi in range(4):
        tpb = c.tp[bi % 2]
        for k in range(8):
            s.op("pe", TR(tpb[:, k, :], xnb[:, bi, k * 128:(k + 1) * 128], c.ident[:]), reads=[f"xnb{i}_{bi}", "ident"],
                 writes=[f"tp{bi % 2}"])
        if bi % 2 == 0:
            s.op("act", ACOPY(xnT[:, :, bi * 128:(bi + 1) * 128], tpb[:, :, :]), reads=[f"tp{bi % 2}"], writes=[f"xnT{i}_{bi}"])
        else:
            s.op("dve", CP(xnT[:, :, bi * 128:(bi + 1) * 128], tpb[:, :, :]), reads=[f"tp{bi % 2}"], writes=[f"xnT{i}_{bi}"])
    return [f"xnT{i}_{bi}" for bi in range(4)]


def phase1(nc, ses, D):
    with ExitStack() as es:
        def sb(name, shape, dt):
            return es.enter_context(nc.sbuf_tensor("a_" + name, shape, dt))

        def ps(name, shape, dt=F32):
            return es.enter_context(nc.psum_tensor("a_" + name, shape, dt))

        s = Sched(nc, ses, "a")
        c = Ctx()
        wb = sb("wb", [128, 8, 3072], BF16)
        wst = [sb(f"wst{i}", [128, 1024], F32) for i in range(2)]
        n1 = sb("n1", [128, 1024], F32)
        c.ident = sb("ident", [128, 128], BF16)
        c.xt = [sb(f"xt{i}", [128, 4, 1024], F32) for i in range(2)]
        c.xnb = [sb(f"xnb{i}", [128, 4, 1024], BF16) for i in range(2)]
        c.xnT = [sb(f"xnT{i}", [128, 8, 512], BF16) for i in range(2)]
        c.ssq = sb("ssq", [128, 2, 4], F32)
        c.rstd = sb("rstd", [128, 2, 4], F32)
        c.junk = sb("junk", [128, 1024], BF16)
        stgk = [sb(f"stgk{i}", [128, 12, 512], BF16) for i in range(2)]
        stgv = [sb(f"stgv{i}", [128, 4, 1536], BF16) for i in range(2)]
        c.tp = [ps(f"tp{i}", [128, 8, 128], BF16) for i in range(2)]
        mp = [ps(f"mp{i}", [128, 512], F32) for i in range(4)]

        make_ident(s, c.ident, "ident")
        s.dma("sp", "n1", DMA(n1[:], D["n1w"].partition_broadcast(128)), writes=["n1"])
        cnt = [0]
        w_in = D["w_in"]
        load_weight(s, c, wb, "wb", w_in, 8, 1024, 2048, 0, wst, cnt)
        load_weight(s, c, wb, "wb", w_in, 8, 3584, 1024, 2048, wst, cnt)
        mcnt = [0]

        def evac(dst, bank, bres, dres, scale=None):
            eng = "act" if (mcnt[0] % 2 == 0) else "dve"
            if scale is not None:
                s.op("act", (lambda dst, bank: lambda e: e.mul(out=dst, in_=bank, mul=scale))(dst, bank), reads=[bres], writes=[dres])
            elif eng == "act":
                s.op("act", ACOPY(dst, bank), reads=[bres], writes=[dres])
            else:
                s.op("dve", CP(dst, bank), reads=[bres], writes=[dres])

        def featmajor_proj(i, xres, ncolchunks, colfn, stg, stgres, qc_scale=False):
            for hh in range(ncolchunks):
                col0 = colfn(hh)
                bi_ = mcnt[0] % 4
                bank = mp[bi_]
                for k in range(8):
                    s.op("pe", MM(bank[:, :], wb[:, k, col0:col0 + 128], c.xnT[i][:, k, :], k == 0, k == 7),
                         reads=["wb"] + xres, writes=[f"mp{bi_}"])
                evac(stg[:, hh, :], bank[:, :], f"mp{bi_}", stgres, scale=(0.125 if (qc_scale and hh >= 8) else None))
                mcnt[0] += 1

        for T in range(NB // 4):
            i = T % 2
            rows = [D["xall"][(T * 4 + bi) * 128:(T * 4 + bi + 1) * 128, :] for bi in range(4)]
            xres = norm_tile(s, c, i, rows, n1, "n1")
            featmajor_proj(i, xres, 12, lambda hh: hh * 128 if hh < 8 else 2048 + (hh - 8) * 128, stgk[i], f"stgk{i}")
            s.dma("sp", f"stgk{i}", DMA(D["kdT_s"][:, :, T * 512:(T + 1) * 512].rearrange("h p t -> p h t"), stgk[i][:, 0:8, :]),
                  reads=[f"stgk{i}"], writes=["kdT_s"])
            s.dma("sp", f"stgk{i}", DMA(D["kcT_s"][:, :, T * 512:(T + 1) * 512].rearrange("h p t -> p h t"), stgk[i][:, 8:12, :]),
                  reads=[f"stgk{i}"], writes=["kcT_s"])
            for bi in range(4):
                for half in range(3):
                    col0 = 1024 + half * 512 if half < 2 else 2560
                    bi_ = mcnt[0] % 4
                    bank = mp[bi_]
                    for k in range(8):
                        s.op("pe", MM(bank[:, :], c.xnT[i][:, k, bi * 128:(bi + 1) * 128], wb[:, k, col0:col0 + 512], k == 0, k == 7),
                             reads=["wb"] + xres, writes=[f"mp{bi_}"])
                    evac(stgv[i][:, bi, half * 512:(half + 1) * 512], bank[:, :], f"mp{bi_}", f"stgv{i}")
                    mcnt[0] += 1
            for bi in range(4):
                s.dma("sp", f"stgv{i}", DMA(D["vd_s"][:, :, T * 4 + bi, :].rearrange("h p e -> p h e"),
                                            stgv[i][:, bi, 0:1024].rearrange("p (h e) -> p h e", h=8)),
                      reads=[f"stgv{i}"], writes=["vd_s"])
                s.dma("sp", f"stgv{i}", DMA(D["vc_s"][:, :, T * 4 + bi, :].rearrange("h p e -> p h e"),
                                            stgv[i][:, bi, 1024:1536].rearrange("p (h e) -> p h e", h=8)),
                      reads=[f"stgv{i}"], writes=["vc_s"])
        load_weight(s, c, wb, "wb", w_in, 8, 0, 1024, 0, wst, cnt)
        load_weight(s, c, wb, "wb", w_in, 8, 3072, 512, 1024, wst, cnt)
        for G in range(4):
            i = G % 2
            rows = [D["xall"][(4 * (4 * G + bi) + 4) * 128:(4 * (4 * G + bi) + 5) * 128, :] for bi in range(4)]
            xres = norm_tile(s, c, i, rows, n1, "n1")
            featmajor_proj(i, xres, 12, lambda hh: hh * 128, stgk[i], f"stgk{i}", qc_scale=True)
            s.dma("sp", f"stgk{i}", DMA(D["qdT_s"][:, :, G * 512:(G + 1) * 512].rearrange("h p t -> p h t"), stgk[i][:, 0:8, :]),
                  reads=[f"stgk{i}"], writes=["qdT_s"])
            s.dma("sp", f"stgk{i}", DMA(D["qcT_s"][:, :, G * 512:(G + 1) * 512].rearrange("h p t -> p h t"), stgk[i][:, 8:12, :]),
                  reads=[f"stgk{i}"], writes=["qcT_s"])
        s.barrier()
        with nc.Block() as block:
            s.emit(block)


def phase2(nc, ses, D):
    with ExitStack() as es:
        def sb(name, shape, dt):
            return es.enter_context(nc.sbuf_tensor("b_" + name, shape, dt))

        def ps(name, shape, dt=F32):
            return es.enter_context(nc.psum_tensor("b_" + name, shape, dt))

        s = Sched(nc, ses, "b")
        kT = [sb(f"kT{i}", [128, NT], BF16) for i in range(2)]
        vv = [sb(f"vv{i}", [128, NB, 132], BF16) for i in range(2)]
        qT = [sb(f"qT{i}", [128, 2048], BF16) for i in range(2)]
        kcT = [sb(f"kcT{i}", [128, NT], BF16) for i in range(2)]
        qcT = [sb(f"qcT{i}", [128, 2048], BF16) for i in range(2)]
        vc = [sb(f"vc{i}", [128, NB, 68], BF16) for i in range(2)]
        colb = [sb(f"colb{i}", [128, 544], F32) for i in range(2)]
        tdg = [sb(f"tdg{i}", [128, 256], F32) for i in range(2)]
        cbg = [sb(f"cbg{i}", [128, 5, 128], F32) for i in range(2)]
        cb0 = [sb(f"cb0{i}", [128, 5, 128], F32) for i in range(2)]
        P = [sb(f"P{i}", [128, 256], BF16) for i in range(4)]
        tmp = [sb(f"tmp{i}", [128, 256], F32) for i in range(2)]
        lq = sb("lq", [128, 4, 64], F32)
        lj = sb("lj", [128, 64], F32)
        lsc = sb("lsc", [128, 4], F32)
        neglam = sb("neglam", [128, 1], F32)
        sw8 = sb("sw8", [128, 128], F32)
        r12 = sb("r12", [128, 4], F32)
        t1 = sb("t1", [128, 128], F32)
        o_ = sb("o_", [128, 128], F32)
        oj = sb("oj", [128, 128], F32)
        sq = sb("sq", [128, 2], F32)
        odst = [sb(f"odst{i}", [128, 128], BF16) for i in range(2)]
        ocst = [sb(f"ocst{i}", [128, 64], BF16) for i in range(2)]
        identf = sb("identf", [128, 128], F32)
        S = [ps(f"S{i}", [128, 2, 512], F32) for i in range(2)]
        O1 = ps("O1", [128, 512], F32)
        O2 = ps("O2", [128, 512], F32)
        Oc = ps("Oc", [128, 512], F32)

        do_conv = D["pu"].shape[0] == 16384
        if do_conv:
            cin = [sb(f"cin{i}", [128, 2, 1024], F32) for i in range(3)]
            cout = [sb(f"cout{i}", [128, 2, 1024], BF16) for i in range(2)]
            dst_v = D["puv16"].rearrange("(p r) d -> p r d", p=128)

            def conv_src(c):
                tab = "pu" if c < 64 else "pv"
                return D[tab].rearrange("(p r) d -> p r d", p=128)[:, 2 * (c % 64):2 * (c % 64) + 2, :]

            def conv_in(c):
                s.dma("sp", None, DMA(cin[c % 3][:, :, :], conv_src(c)), writes=[f"cin{c % 3}"])

            def conv_step(c):
                if c + 1 < 128:
                    conv_in(c + 1)
                s.op("dve", CP(cout[c % 2][:, :, :], cin[c % 3][:, :, :]), reads=[f"cin{c % 3}"], writes=[f"cout{c % 2}"])
                off = 0 if c < 64 else 1024
                s.dma("sp", None, DMA(dst_v[:, 2 * (c % 64):2 * (c % 64) + 2, off:off + 1024], cout[c % 2][:, :, :]),
                      reads=[f"cout{c % 2}"], writes=["puv16"])

            conv_in(0)
        make_ident(s, identf, "identf")
        for i in range(2):
            s.op("pool", MEMSET(vv[i][:, :, 128:132], 1.0), writes=[f"vv{i}_ones"])
            s.op("pool", MEMSET(vc[i][:, :, 64:68], 1.0), writes=[f"vc{i}_ones"])
        for n_, key in enumerate(["lq1", "lk1", "lq2", "lk2"]):
            s.dma("sp", "lq", DMA(lq[:, n_, :], D[key].partition_broadcast(128)), writes=["lq"])
        s.dma("sp", "lq", DMA(sw8[:], D["subln_w"].partition_broadcast(128)), writes=["sw8"])
        s.op("dve", STT(lj[:], lq[:, 0, :], 1.0, lq[:, 1, :], ALU.mult, ALU.mult, accum_out=lsc[:, 0:1]), reads=["lq"], writes=["lj", "lsc"])
        s.op("dve", STT(lj[:], lq[:, 2, :], 1.0, lq[:, 3, :], ALU.mult, ALU.mult, accum_out=lsc[:, 1:2]), reads=["lq", "lj"], writes=["lj", "lsc"])
        s.op("act", ACTF(lsc[:, 2:4], lsc[:, 0:2], AF.Exp), reads=["lsc"], writes=["lsc2"])
        s.op("dve", TT(neglam[:], lsc[:, 3:4], lsc[:, 2:3], ALU.subtract), reads=["lsc2"], writes=["neglam"])
        s.op("dve", TS(neglam[:], neglam[:], -0.2, None, ALU.add, ALU.add) if False else
             (lambda e: e.tensor_scalar_add(out=neglam[:], in0=neglam[:], scalar1=-0.2)), reads=["neglam"], writes=["neglam"])
        s.op("dve", TSMUL(sw8[:], sw8[:], 0.8), reads=["sw8"], writes=["sw8"])

        def head_loads(h):
            b = h % 2
            pb_ = (h // 2) % 2
            s.dma("sp", f"kT{b}", DMA(kT[b][:, :], D["kdT_s"][h]), reads=["kdT_s"], writes=[f"kT{b}"])
            s.dma("sp", f"vv{b}", DMA(vv[b][:, :, 0:128], D["vd_s"][h]), reads=["vd_s"], writes=[f"vv{b}"])
            s.dma("sp", f"qT{b}", DMA(qT[b][:, :], D["qdT_s"][h]), reads=["qdT_s"], writes=[f"qT{b}"])
            if h % 2 == 0:
                s.dma("sp", f"kcT{pb_}", DMA(kcT[pb_][:, :], D["kcT_s"][h // 2]), reads=["kcT_s"], writes=[f"kcT{pb_}"])
                s.dma("sp", f"qcT{pb_}", DMA(qcT[pb_][:, :], D["qcT_s"][h // 2]), reads=["qcT_s"], writes=[f"qcT{pb_}"])
            s.dma("sp", f"vc{b}", DMA(vc[b][:, :, 0:64], D["vc_s"][h]), reads=["vc_s"], writes=[f"vc{b}"])
            s.dma("sp", f"cst{b}", DMA(colb[b][:, :], D["colb"][:, h, :]), writes=[f"colb{b}"])
            s.dma("sp", f"cst{b}", DMA(tdg[b][:, :], D["tdiag"][:, h, :]), writes=[f"tdg{b}"])
            s.dma("sp", f"cst{b}", DMA(cbg[b][:, :, :], D["cbg"][:, h, :, :]), writes=[f"cbg{b}"])
            s.dma("sp", f"cst{b}", DMA(cb0[b][:, :, :], D["cb0"][:, h, :, :]), writes=[f"cb0{b}"])

        head_loads(0)
        for h in range(P2_HEADS):
            b = h % 2
            pb_ = (h // 2) % 2
            if h + 1 < P2_HEADS:
                head_loads(h + 1)
            hp = (h % 2) * 64
            for m in range(P2_NQ):
                items = [("c", 4 * m + 4 - dlt, dlt) for dlt in (4, 3, 2, 1, 0)]
                dmax = int(160.0 / (128.0 * 2.0 ** (-(h + 1)))) + 1
                diff_L = [L for L in range(4 * m + 4) if (4 * m + 4 - L) <= dmax]
                first_L = diff_L[0] if diff_L else 4 * m + 4
                items += [("d", L, 0) for L in diff_L]
                items.append(("g", 4 * m + 4, 0))
                n = len(items)
                qs = slice(m * 128, (m + 1) * 128)
                if do_conv:
                    conv_step(h * NQ + m)

                def emit_qk(t):
                    kind, L, dlt = items[t]
                    si = t % 2
                    pi = t % 4
                    Sb, Pb = S[si], P[pi]
                    ks = slice(L * 128, (L + 1) * 128)
                    if kind == "c":
                        bt = cb0[b] if m == 0 else cbg[b]
                        btr = f"cb0{b}" if m == 0 else f"cbg{b}"
                        s.op("pe", MM(Sb[:, 0, 0:128], kcT[pb_][hp:hp + 64, ks], qcT[pb_][hp:hp + 64, qs], True, False),
                             reads=[f"kcT{pb_}", f"qcT{pb_}"], writes=[f"S{si}"])
                        s.op("pe", MM(Sb[:, 0, 0:128], identf[:, :], bt[:, dlt, :], False, True),
                             reads=["identf", btr], writes=[f"S{si}"])
                        s.op("act", ACTF(Pb[:, 0:128], Sb[:, 0, 0:128], AF.Exp), reads=[f"S{si}"], writes=[f"P{pi}"])
                    else:
                        dg_ = (kind == "g")
                        s.op("pe", MM(Sb[:, 0, 0:128], kT[b][0:64, ks], qT[b][0:64, qs], True, not dg_),
                             reads=[f"kT{b}", f"qT{b}"], writes=[f"S{si}"])
                        if dg_:
                            s.op("pe", MM(Sb[:, 0, 0:128], identf[:, :], tdg[b][:, 0:128], False, True),
                                 reads=["identf", f"tdg{b}"], writes=[f"S{si}"])
                        s.op("pe", MM(Sb[:, 1, 0:128], kT[b][64:128, ks], qT[b][64:128, qs], True, not dg_),
                             reads=[f"kT{b}", f"qT{b}"], writes=[f"S{si}"])
                        if dg_:
                            s.op("pe", MM(Sb[:, 1, 0:128], identf[:, :], tdg[b][:, 128:256], False, True),
                                 reads=["identf", f"tdg{b}"], writes=[f"S{si}"])
                        if kind == "d":
                            idx = 2 * m * m + 2 * m + L
                            s.op("act", ACTF(Pb[:, 0:256].rearrange("p (a n) -> p a n", a=2), Sb[:, :, 0:128], AF.Exp,
                                             bias=colb[b][:, idx:idx + 1], scale=0.125),
                                 reads=[f"S{si}", f"colb{b}"], writes=[f"P{pi}"])
                        else:
                            s.op("act", ACTF(Pb[:, 0:256].rearrange("p (a n) -> p a n", a=2), Sb[:, :, 0:128], AF.Exp, scale=0.125),
                                 reads=[f"S{si}"], writes=[f"P{pi}"])

                def emit_av(t):
                    kind, L, dlt = items[t]
                    si = t % 4
                    Pb = P[si]
                    if kind == "c":
                        s.op("pe", MM(Oc[:, 0:68], Pb[:, 0:128], vc[b][:, L, :], dlt == 4, dlt == 0),
                             reads=[f"P{si}", f"vc{b}", f"vc{b}_ones"], writes=["Oc"])
                    else:
                        s.op("pe", MM(O1[:, 0:132], Pb[:, 0:128], vv[b][:, L, :], L == first_L, kind == "g"),
                             reads=[f"P{si}", f"vv{b}", f"vv{b}_ones"], writes=["O1"])
                        s.op("pe", MM(O2[:, 0:132], Pb[:, 128:256], vv[b][:, L, :], L == first_L, kind == "g"),
                             reads=[f"P{si}", f"vv{b}", f"vv{b}_ones"], writes=["O2"])

                for t in range(n + 1):
                    if t < n:
                        emit_qk(t)
                    if t >= 1 and P2_MODE >= 2:
                        emit_av(t - 1)
                    if t - 1 == 4 and P2_MODE >= 3:
                        st = ocst[m % 2]
                        s.op("dve", RECIP(r12[:, 2:3], Oc[:, 64:65]), reads=["Oc"], writes=["r12c"])
                        s.op("dve", TSMUL(st[:, :], Oc[:, 0:64], r12[:, 2:3]), reads=["Oc", "r12c"], writes=[f"ocst{m % 2}"])
                        s.dma("sp", f"ocst{m % 2}", DMA(D["oc_s"][m * 128:(m + 1) * 128, h * 64:(h + 1) * 64], st[:, :]),
                              reads=[f"ocst{m % 2}"], writes=["oc_s"])
                if P2_MODE < 3:
                    continue
                st = odst[m % 2]
                s.op("dve", RECIP(r12[:, 0:1], O1[:, 128:129]), reads=["O1"], writes=["r12a"])
                s.op("dve", RECIP(r12[:, 1:2], O2[:, 128:129]), reads=["O2"], writes=["r12b"])
                s.op("dve", TT(r12[:, 3:4], r12[:, 1:2], neglam[:, 0:1], ALU.mult), reads=["r12b", "neglam"], writes=["r12d"])
                s.op("dve", TSMUL(t1[:, :], O1[:, 0:128], r12[:, 0:1]), reads=["O1", "r12a"], writes=["t1"])
                s.op("dve", STT(o_[:, :], O2[:, 0:128], r12[:, 3:4], t1[:, :], ALU.mult, ALU.add), reads=["O2", "r12d", "t1"], writes=["o_"])
                s.op("dve", STT(oj[:, :], o_[:, :], 1.0, o_[:, :], ALU.mult, ALU.mult, accum_out=sq[:, 0:1]), reads=["o_"], writes=["oj", "sq"])
                s.op("dve", TS(sq[:, 1:2], sq[:, 0:1], 1.0 / 128, EPS, ALU.mult, ALU.add), reads=["sq"], writes=["sq1"])
                s.op("act", ACTF(sq[:, 1:2], sq[:, 1:2], AF.Ln), reads=["sq1"], writes=["sq1"])
                s.op("act", ACTF(sq[:, 1:2], sq[:, 1:2], AF.Exp, scale=-0.5), reads=["sq1"], writes=["sq1"])
                s.op("dve", STT(st[:, :], o_[:, :], sq[:, 1:2], sw8[:, :], ALU.mult, ALU.mult), reads=["o_", "sq1", "sw8"], writes=[f"odst{m % 2}"])
                s.dma("sp", f"odst{m % 2}", DMA(D["od_s"][m * 128:(m + 1) * 128, h * 128:(h + 1) * 128], st[:, :]),
                      reads=[f"odst{m % 2}"], writes=["od_s"])
        s.barrier()
        with nc.Block() as block:
            s.emit(block)


def phase3(nc, ses, D):
    with ExitStack() as es:
        def sb(name, shape, dt):
            return es.enter_context(nc.sbuf_tensor("c_" + name, shape, dt))

        def ps(name, shape, dt=F32):
            return es.enter_context(nc.psum_tensor("c_" + name, shape, dt))

        s = Sched(nc, ses, "c")
        c = Ctx()
        wbd = sb("wbd", [128, 8, 1024], BF16)
        wbc = sb("wbc", [128, 4, 1024], BF16)
        wg = sb("wg", [128, 8, 2048], BF16)
        wo = sb("wo", [128, 8, 1024], BF16)
        wst = [sb(f"wst{i}", [128, 1024], F32) for i in range(2)]
        n1 = sb("n1", [128, 1024], F32)
        bg = sb("bg", [128, 16], F32)
        c.ident = sb("ident", [128, 128], BF16)
        c.xt = [sb(f"xt{i}", [128, 4, 1024], F32) for i in range(2)]
        c.xnb = [sb(f"xnb{i}", [128, 4, 1024], BF16) for i in range(1)] * 2
        c.xnT = [sb(f"xnT{i}", [128, 8, 512], BF16) for i in range(1)] * 2
        c.ssq = sb("ssq", [128, 2, 4], F32)
        c.rstd = sb("rstd", [128, 2, 4], F32)
        c.junk = sb("junk", [128, 1024], BF16)
        odb = sb("odb", [128, 4, 1024], BF16)
        ocb = sb("ocb", [128, 4, 512], BF16)
        odT = sb("odT", [128, 8, 512], BF16)
        ocT = sb("ocT", [128, 4, 512], BF16)
        sg = [sb(f"sg{i}", [128, 512], F32) for i in range(2)]
        m12 = [sb(f"m12{i}", [128, 512], F32) for i in range(2)]
        mT = sb("mT", [128, 8, 512], BF16)
        h1 = [sb(f"h1{i}", [128, 1024], F32) for i in range(2)]
        c.tp = [ps(f"tp{i}", [128, 8, 128], BF16) for i in range(2)]
        pa = [ps(f"pa{i}", [128, 512], F32) for i in range(4)]
        ph = [ps(f"ph{i}", [128, 512], F32) for i in range(2)]

        make_ident(s, c.ident, "ident")
        s.dma("sp", "n1", DMA(n1[:], D["n1w"].partition_broadcast(128)), writes=["n1"])
        s.dma("sp", "n1", DMA(bg[:], D["bgate"]), writes=["bg"])
        cnt = [0]
        load_weight(s, c, wbd, "wbd", D["w_bd"], 8, 0, 1024, 0, wst, cnt)
        load_weight(s, c, wbc, "wbc", D["w_bc"], 4, 0, 1024, 0, wst, cnt)
        load_weight(s, c, wg, "wg", D["w_in"], 8, 4608, 2048, 0, wst, cnt)
        load_weight(s, c, wo, "wo", D["w_out"], 8, 0, 1024, 0, wst, cnt)
        for G in range(4):
            i = G % 2
            rows = [D["xall"][(4 * (4 * G + bi) + 4) * 128:(4 * (4 * G + bi) + 5) * 128, :] for bi in range(4)]
            xres = norm_tile_p3(s, c, i, rows, n1)
            s.dma("sp", "odb", DMA(odb[:, :, :], D["od_s"][G * 512:(G + 1) * 512, :].rearrange("(l p) d -> p l d", p=128)),
                  reads=["od_s"], writes=["odb"])
            s.dma("sp", "ocb", DMA(ocb[:, :, :], D["oc_s"][G * 512:(G + 1) * 512, :].rearrange("(l p) d -> p l d", p=128)),
                  reads=["oc_s"], writes=["ocb"])
            tcount = 0
            for bi in range(4):
                for (src, nk, dstT, sres, dres) in ((odb, 8, odT, "odb", "odT"), (ocb, 4, ocT, "ocb", "ocT")):
                    tpi = tcount % 2
                    tcount += 1
                    tpb = c.tp[tpi]
                    for k in range(nk):
                        s.op("pe", TR(tpb[:, k, :], src[:, bi, k * 128:(k + 1) * 128], c.ident[:]), reads=[sres, "ident"], writes=[f"tp{tpi}"])
                    if tpi == 0:
                        s.op("act", ACOPY(dstT[:, :, bi * 128:(bi + 1) * 128], tpb[:, 0:nk, :]), reads=[f"tp{tpi}"], writes=[dres])
                    else:
                        s.op("dve", CP(dstT[:, :, bi * 128:(bi + 1) * 128], tpb[:, 0:nk, :]), reads=[f"tp{tpi}"], writes=[dres])
            for oc in range(8):
                cs = slice(oc * 128, (oc + 1) * 128)
                A, B, C_, D_ = pa
                for k in range(8):
                    s.op("pe", MM(A[:, :], wbd[:, k, cs], odT[:, k, :], k == 0, k == 7), reads=["wbd", "odT"], writes=["pa0"])
                for k in range(4):
                    s.op("pe", MM(B[:, :], wbc[:, k, cs], ocT[:, k, :], k == 0, k == 3), reads=["wbc", "ocT"], writes=["pa1"])
                for k in range(8):
                    s.op("pe", MM(C_[:, :], wg[:, k, cs], c.xnT[0][:, k, :], k == 0, k == 7), reads=["wg"] + xres, writes=["pa2"])
                gs = slice(1024 + oc * 128, 1024 + (oc + 1) * 128)
                for k in range(8):
                    s.op("pe", MM(D_[:, :], wg[:, k, gs], c.xnT[0][:, k, :], k == 0, k == 7), reads=["wg"] + xres, writes=["pa3"])
                s.op("act", ACTF(sg[0][:, :], C_[:, :], AF.Sigmoid, bias=bg[:, oc:oc + 1]), reads=["pa2", "bg"], writes=["sg0"])
                s.op("act", ACTF(sg[1][:, :], D_[:, :], AF.Sigmoid, bias=bg[:, 8 + oc:9 + oc]), reads=["pa3", "bg"], writes=["sg1"])
                s.op("dve", TT(m12[0][:, :], A[:, :], sg[0][:, :], ALU.mult), reads=["pa0", "sg0"], writes=["m120"])
                s.op("dve", TT(m12[1][:, :], B[:, :], sg[1][:, :], ALU.mult), reads=["pa1", "sg1"], writes=["m121"])
                s.op("pool", TT(mT[:, oc, :], m12[0][:, :], m12[1][:, :], ALU.add), reads=["m120", "m121"], writes=["mT"])
            for bi in range(4):
                m = 4 * G + bi
                hb = h1[bi % 2]
                for half in range(2):
                    pb = ph[half]
                    for k in range(8):
                        s.op("pe", MM(pb[:, :], mT[:, k, bi * 128:(bi + 1) * 128], wo[:, k, half * 512:(half + 1) * 512], k == 0, k == 7),
                             reads=["mT", "wo"], writes=[f"ph{half}"])
                    s.op("dve", TT(hb[:, half * 512:(half + 1) * 512], pb[:, :], c.xt[i][:, bi, half * 512:(half + 1) * 512], ALU.add),
                         reads=[f"ph{half}", f"xt{i}"], writes=[f"h1{bi % 2}"])
                s.dma("sp", f"h1{bi % 2}", DMA(D["h1_s"][m * 128:(m + 1) * 128, :], hb[:, :]), reads=[f"h1{bi % 2}"], writes=["h1_s"])
        s.barrier()
        with nc.Block() as block:
            s.emit(block)


def norm_tile_p3(s, c, i, src_rows, n1):
    xt, xnb, xnT = c.xt[i], c.xnb[0], c.xnT[0]
    for bi, src in enumerate(src_rows):
        s.dma("sp", f"xt{i}", DMA(xt[:, bi, :], src), writes=[f"xt{i}"], part=(bi != 0))
    for bi in range(4):
        s.op("act", ACTF(c.junk[:], xt[:, bi, :], AF.Square, accum_out=c.ssq[:, i, bi:bi + 1]), reads=[f"xt{i}"],
             writes=["junk", f"ssq{i}"])
    s.op("dve", TS(c.rstd[:, i, :], c.ssq[:, i, :], 1.0 / 1024, EPS, ALU.mult, ALU.add), reads=[f"ssq{i}"], writes=[f"rstd{i}"])
    s.op("act", ACTF(c.rstd[:, i, :], c.rstd[:, i, :], AF.Ln), reads=[f"rstd{i}"], writes=[f"rstd{i}"])
    s.op("act", ACTF(c.rstd[:, i, :], c.rstd[:, i, :], AF.Exp, scale=-0.5), reads=[f"rstd{i}"], writes=[f"rstd{i}"])
    for bi in range(4):
        s.op("dve", STT(xnb[:, bi, :], xt[:, bi, :], c.rstd[:, i, bi:bi + 1], n1[:], ALU.mult, ALU.mult),
             reads=[f"xt{i}", f"rstd{i}", "n1"], writes=[f"xnb_{bi}"])
    for bi in range(4):
        tpb = c.tp[bi % 2]
        for k in range(8):
            s.op("pe", TR(tpb[:, k, :], xnb[:, bi, k * 128:(k + 1) * 128], c.ident[:]), reads=[f"xnb_{bi}", "ident"],
                 writes=[f"tp{bi % 2}"])
        if bi % 2 == 0:
            s.op("act", ACOPY(xnT[:, :, bi * 128:(bi + 1) * 128], tpb[:, :, :]), reads=[f"tp{bi % 2}"], writes=[f"xnT_{bi}"])
        else:
            s.op("dve", CP(xnT[:, :, bi * 128:(bi + 1) * 128], tpb[:, :, :]), reads=[f"tp{bi % 2}"], writes=[f"xnT_{bi}"])
    return [f"xnT_{bi}" for bi in range(4)]


NUV = 16


def phase4(nc, ses, D):
    with ExitStack() as es:
        def sb(name, shape, dt):
            return es.enter_context(nc.sbuf_tensor("d_" + name, shape, dt))

        def ps(name, shape, dt=F32):
            return es.enter_context(nc.psum_tensor("d_" + name, shape, dt))

        s = Sched(nc, ses, "d")
        wq = sb("wq", [128, 8, 1024], F32)
        kt0 = sb("kt0", [128, 8, 128], F32)
        keysT = sb("keysT", [128, 8, 128], F32)
        n2 = sb("n2", [128, 1024], F32)
        nf = sb("nf", [128, 1024], F32)
        ident = sb("identf", [128, 128], F32)
        h1 = [sb(f"h1{i}", [128, 1024], F32) for i in range(3)]
        xn2 = [sb(f"xn2{i}", [128, 1024], F32) for i in range(3)]
        junk = sb("junk", [128, 1024], F32)
        junk2 = sb("junk2", [128, 1024], BF16)
        xn2T = sb("xn2T", [128, 8, 128], F32)
        qTt = sb("qTt", [128, 8, 128], F32)
        sc = sb("sc", [128, 16, 128], F32)
        sc2 = sb("sc2", [128, 128], F32)
        mx = sb("mx", [128, 16, 16], F32)
        mi = sb("mi", [128, 16, 16], U32)
        mif = sb("mif", [128, 16, 16], F32)
        cand = sb("cand", [128, 8, 256], F32)
        cand2 = sb("cand2", [128, 256], F32)
        ci = sb("ci", [128, 8, 16], U32)
        cii = sb("cii", [128, 8, 16], U32)
        cij = sb("cij", [128, 8, 16], U32)
        ciif = sb("ciif", [128, 8, 16], F32)
        cijf = sb("cijf", [128, 8, 16], F32)
        iota16 = sb("iota16", [128, 16], F32)
        eq = sb("eq", [128, 8, 16, 16], F32)
        sel1 = sb("sel1", [128, 8, 16], F32)
        sel2 = sb("sel2", [128, 8, 16], F32)
        sc16 = sb("sc16", [128, 8, 16], F32)
        ex = sb("ex", [128, 8, 16], F32)
        zz = sb("zz", [128, 16], F32)
        gg = [sb(f"gg{i}", [128, 128], F32) for i in range(3)]
        ef = sb("ef", [128, 128], F32)
        ei = [sb(f"ei{i}", [128, 128], I32) for i in range(3)]
        hid = [sb(f"hid{i}", [128, 128], F32) for i in range(2)]
        gl = [sb(f"gl{i}", [128, 128], F32) for i in range(2)]
        aa = [sb(f"aa{i}", [128, 128], F32) for i in range(2)]
        dg = [sb(f"dg{i}", [128, 128], BF16) for i in range(4)]
        uvb = [sb(f"uv{i}", [128, 2048], BF16) for i in range(NUV)]
        xn2b = [sb(f"xn2b{i}", [128, 1024], BF16) for i in range(3)]
        h2 = sb("h2", [128, 1024], F32)
        ssq = sb("ssq", [128, 8], F32)
        yb = [sb("yb0", [128, 1024], F32)] * 2
        tq = [ps(f"tq{i}", [128, 512], F32) for i in range(2)]
        sps = [ps(f"sps{i}", [128, 512], F32) for i in range(4)]
        accp = [ps(f"accp{i}", [128, 512], F32) for i in range(2)]

        s.op("pool", MEMSET(ident[:], 0.0), writes=["ident"])
        s.op("pool", lambda e: e.affine_select(out=ident[:], in_=ident[:], pattern=[[-1, 128]], compare_op=ALU.not_equal,
                                               fill=1.0, base=0, channel_multiplier=1), reads=["ident"], writes=["ident"])
        s.op("pool", lambda e: e.iota(iota16[:], pattern=[[1, 16]], base=0, channel_multiplier=0, allow_small_or_imprecise_dtypes=True), writes=["iota16"])
        s.dma("sp", None, DMA(n2[:], D["n2w"].partition_broadcast(128)), writes=["n2"])
        s.dma("sp", None, DMA(nf[:], D["fnw"].partition_broadcast(128)), writes=["nf"])
        for k in range(8):
            s.dma("sp", None, DMA(wq[:, k, :], D["wq"][k * 128:(k + 1) * 128, :]), writes=["wq"])
        wqres = ["wq"]
        s.dma("sp", None, DMA(kt0[:, :, :].rearrange("n h (c d) -> n h c d", c=2), D["keys"].rearrange("h c n d -> n h c d")), writes=["kt0"])
        for h in range(8):
            s.op("pe", TR(tq[0][:, h % 4 * 128:(h % 4 + 1) * 128], kt0[:, h, :], ident[:]), reads=["kt0", "ident"], writes=["tq0"])
            if h % 4 == 3:
                s.op("act", ACOPY(keysT[:, h - 3:h + 1, :], tq[0][:, :].rearrange("p (h n) -> p h n", h=4)), reads=["tq0"], writes=["keysT"])

        def rms(src, src_res, w_tile, w_res, dst, dst_res, col, jk, jkres):
            s.op("act", ACTF(jk[:], src, AF.Square, accum_out=ssq[:, col:col + 1]), reads=[src_res], writes=[jkres, f"ssq{col}"])
            s.op("dve", TS(ssq[:, col + 1:col + 2], ssq[:, col:col + 1], 1.0 / 1024, EPS, ALU.mult, ALU.add), reads=[f"ssq{col}"], writes=[f"ssq{col + 1}"])
            s.op("act", ACTF(ssq[:, col + 1:col + 2], ssq[:, col + 1:col + 2], AF.Ln), reads=[f"ssq{col + 1}"], writes=[f"ssq{col + 1}"])
            s.op("act", ACTF(ssq[:, col + 1:col + 2], ssq[:, col + 1:col + 2], AF.Exp, scale=-0.5), reads=[f"ssq{col + 1}"], writes=[f"ssq{col + 1}"])
            s.op("dve", STT(dst, src, ssq[:, col + 1:col + 2], w_tile[:], ALU.mult, ALU.mult), reads=[src_res, f"ssq{col + 1}", w_res], writes=[dst_res])

        def idx_block(m):
            b = m % 3
            s.dma("sp", None, DMA(h1[b][:, :], D["h1_s"][m * 128:(m + 1) * 128, :]), reads=["h1_s"], writes=[f"h1{b}"])
            rms(h1[b][:, :], f"h1{b}", n2, "n2", xn2[b][:, :], f"xn2{b}", 0, junk, "junk")
            s.op("pool", CP(xn2b[b][:, :], xn2[b][:, :]), reads=[f"xn2{b}"], writes=[f"xn2b{b}"])
            yield
            for k in range(8):
                tb = tq[k // 4]
                s.op("pe", TR(tb[:, (k % 4) * 128:(k % 4 + 1) * 128], xn2[b][:, k * 128:(k + 1) * 128], ident[:]), reads=[f"xn2{b}", "ident"], writes=[f"tq{k // 4}"])
                if k % 4 == 3:
                    s.op("act", ACOPY(xn2T[:, k - 3:k + 1, :], tb[:, :].rearrange("p (h n) -> p h n", h=4)), reads=[f"tq{k // 4}"], writes=["xn2T"])
            for h in range(8):
                qb = tq[h // 4]
                for k in range(8):
                    s.op("pe", MM(qb[:, (h % 4) * 128:(h % 4 + 1) * 128], wq[:, k, h * 128:(h + 1) * 128], xn2T[:, k, :], k == 0, k == 7),
                         reads=wqres + ["xn2T"], writes=[f"tq{h // 4}"])
                if h % 4 == 3:
                    s.op("act", ACOPY(qTt[:, h - 3:h + 1, :], qb[:, :].rearrange("p (h n) -> p h n", h=4)), reads=[f"tq{h // 4}"], writes=["qTt"])
            for h in range(8):
                for cc in range(2):
                    bk = sps[cc * 2 + h // 4]
                    s.op("pe", MM(bk[:, (h % 4) * 128:(h % 4 + 1) * 128], qTt[cc * 64:(cc + 1) * 64, h, :], keysT[cc * 64:(cc + 1) * 64, h, :], True, True),
                         reads=["qTt", "keysT"], writes=[f"sps{cc * 2 + h // 4}"])
            for bq in range(4):
                s.op("act", ACOPY(sc[:, bq * 4:bq * 4 + 4, :], sps[bq][:, :].rearrange("p (h n) -> p h n", h=4)), reads=[f"sps{bq}"], writes=["sc"])
            yield
            for hc in range(16):
                if hc % 4 == 0 and hc:
                    yield
                s.op("dve", lambda e, hc=hc: e.max(out=mx[:, hc, 0:8], in_=sc[:, hc, :]), reads=["sc"], writes=["mx"])
                s.op("dve", lambda e, hc=hc: e.max_index(out=mi[:, hc, 0:8], in_max=mx[:, hc, 0:8], in_values=sc[:, hc, :]), reads=["sc", "mx"], writes=["mi"])
                s.op("dve", lambda e, hc=hc: e.match_replace(out=sc2[:, :], in_to_replace=mx[:, hc, 0:8], in_values=sc[:, hc, :], imm_value=-1e30), reads=["sc", "mx"], writes=["sc2"])
                s.op("dve", lambda e, hc=hc: e.max(out=mx[:, hc, 8:16], in_=sc2[:, :]), reads=["sc2"], writes=["mx"])
                s.op("dve", lambda e, hc=hc: e.max_index(out=mi[:, hc, 8:16], in_max=mx[:, hc, 8:16], in_values=sc2[:, :]), reads=["sc2", "mx"], writes=["mi"])
            s.op("dve", CP(mif[:, :, :], mi[:, :, :]), reads=["mi"], writes=["mif"])
            yield
            mx4 = mx[:, :, :].rearrange("p (c h) k -> p c h k", c=2)
            mif4 = mif[:, :, :].rearrange("p (c h) k -> p c h k", c=2)
            cand4 = cand[:, :, :].rearrange("p h (i j) -> p h i j", i=16)
            for h in range(8):
                s.op("dve", TT(cand4[:, h, :, :], mx4[:, 0, h, :].unsqueeze(2).to_broadcast([128, 16, 16]),
                               mx4[:, 1, h, :].unsqueeze(1).to_broadcast([128, 16, 16]), ALU.add), reads=["mx"], writes=["cand"])
            yield
            for h in range(8):
                if h == 4:
                    yield
                s.op("dve", lambda e, h=h: e.max(out=sc16[:, h, 0:8], in_=cand[:, h, :]), reads=["cand"], writes=["sc16"])
                s.op("dve", lambda e, h=h: e.max_index(out=ci[:, h, 0:8], in_max=sc16[:, h, 0:8], in_values=cand[:, h, :]), reads=["cand", "sc16"], writes=["ci"])
                s.op("dve", lambda e, h=h: e.match_replace(out=cand2[:, :], in_to_replace=sc16[:, h, 0:8], in_values=cand[:, h, :], imm_value=-1e30), reads=["cand", "sc16"], writes=["cand2"])
                s.op("dve", lambda e, h=h: e.max(out=sc16[:, h, 8:16], in_=cand2[:, :]), reads=["cand2"], writes=["sc16"])
                s.op("dve", lambda e, h=h: e.max_index(out=ci[:, h, 8:16], in_max=sc16[:, h, 8:16], in_values=cand2[:, :]), reads=["cand2", "sc16"], writes=["ci"])
            yield
            s.op("dve", TT(ex[:, :, :], sc16[:, :, :], sc16[:, :, 0:1].to_broadcast([128, 8, 16]), ALU.subtract), reads=["sc16"], writes=["ex"])
            s.op("act", ACTF(ex[:, :, :], ex[:, :, :], AF.Exp), reads=["ex"], writes=["ex"])
            s.op("dve", lambda e: e.reduce_sum(out=zz[:, 0:8], in_=ex[:, :, :], axis=mybir.AxisListType.X), reads=["ex"], writes=["zz"])
            s.op("dve", RECIP(zz[:, 8:16], zz[:, 0:8]), reads=["zz"], writes=["zz2"])
            g3 = gg[b][:, :].rearrange("p (h k) -> p h k", h=8)
            s.op("dve", TT(g3, ex[:, :, :], zz[:, 8:16].unsqueeze(2).to_broadcast([128, 8, 16]), ALU.mult), reads=["ex", "zz2"], writes=[f"gg{b}"])
            yield
            s.op("dve", lambda e: e.tensor_single_scalar(out=cii[:], in_=ci[:], scalar=4, op=ALU.logical_shift_right), reads=["ci"], writes=["cii"])
            s.op("dve", lambda e: e.tensor_single_scalar(out=cij[:], in_=ci[:], scalar=15, op=ALU.bitwise_and), reads=["ci"], writes=["cij"])
            s.op("dve", CP(ciif[:], cii[:]), reads=["cii"], writes=["ciif"])
            s.op("dve", CP(cijf[:], cij[:]), reads=["cij"], writes=["cijf"])
            iob = iota16[:, :].unsqueeze(1).unsqueeze(1).to_broadcast([128, 8, 16, 16])
            for (cf, cc_, sel, nm) in ((ciif, 0, sel1, "1"), (cijf, 1, sel2, "2")):
                yield
                s.op("dve", TT(eq[:], cf[:, :, :].unsqueeze(3).to_broadcast([128, 8, 16, 16]), iob, ALU.is_equal),
                     reads=["ciif", "cijf", "iota16", "eq"], writes=["eq"])
                s.op("dve", TT(eq[:], eq[:], mif4[:, cc_, :, :].unsqueeze(2).to_broadcast([128, 8, 16, 16]), ALU.mult), reads=["eq", "mif"], writes=["eq"])
                s.op("dve", (lambda sel: lambda e: e.reduce_sum(out=sel[:], in_=eq[:], axis=mybir.AxisListType.X))(sel), reads=["eq"], writes=["sel" + nm])
            s.op("dve", STT(ef[:, :].rearrange("p (h k) -> p h k", h=8), sel1[:], 128.0, sel2[:], ALU.mult, ALU.add), reads=["sel1", "sel2"], writes=["ef"])
            s.op("dve", lambda e: e.tensor_scalar_min(out=ef[:, :], in0=ef[:, :], scalar1=16383.0), reads=["ef"], writes=["ef"])
            s.op("dve", CP(ei[b][:, :], ef[:, :]), reads=["ef"], writes=[f"ei{b}"])

        def gather(tab, buf, bres, eib, col):
            s.dma("pool", None, lambda e: e.indirect_dma_start(out=buf[:, :], out_offset=None, in_=D[tab],
                                                                in_offset=bass.IndirectOffsetOnAxis(ap=eib[:, col:col + 1], axis=0)),
                  reads=[], writes=[bres])

        GR = 4
        NG = 128 // GR
        NSLOT = NUV // GR

        def rows_block(m, idxgen):
            b = m % 3
            b2 = m % 2
            eires = f"ei{b}"
            eib = ei[b]

            def U(g):
                for r in range(g * GR, (g + 1) * GR):
                    buf = uvb[r % NUV]
                    s.dma("pool", None, (lambda buf, r: lambda e: e.indirect_dma_start(
                        out=buf[:, :], out_offset=None, in_=D["puv16"],
                        in_offset=bass.IndirectOffsetOnAxis(ap=eib[:, r:r + 1], axis=0)))(buf, r),
                          reads=[eires], writes=[f"uvg{g % NSLOT}"], part=(r % GR != 0))

            def H(g):
                for r in range(g * GR, (g + 1) * GR):
                    s.op("dve", STT(junk2[:, :], uvb[r % NUV][:, 0:1024], 1.0, xn2b[b][:, :], ALU.mult, ALU.mult, accum_out=hid[b2][:, r:r + 1]),
                         reads=[f"uvg{g % NSLOT}", f"xn2b{b}", "junk2"], writes=["junk2", f"hid{b2}_{g}"] + (["hfence"] if r % GR == 0 else []))

            def A(g):
                cs = slice(g * GR, (g + 1) * GR)
                s.op("act", ACTF(gl[b2][:, cs], hid[b2][:, cs], AF.Gelu), reads=[f"hid{b2}_{g}", "hfence"], writes=[f"gl{b2}_{g}"])
                s.op("dve", TT(aa[b2][:, cs], gl[b2][:, cs], gg[b][:, cs], ALU.mult), reads=[f"gl{b2}_{g}", f"gg{b}"], writes=[f"aa{b2}_{g}", "hfence"])

            def Pm(g):
                for r in range(g * GR, (g + 1) * GR):
                    di = r % 4
                    s.op("act", (lambda di, r: lambda e: e.mul(out=dg[di][:, :], in_=ident[:, :], mul=aa[b2][:, r:r + 1]))(di, r),
                         reads=["ident", f"aa{b2}_{g}"], writes=[f"dg{di}"])
                    for half in range(2):
                        s.op("pe", MM(accp[half][:, :], dg[di][:, :], uvb[r % NUV][:, 1024 + half * 512:1024 + (half + 1) * 512], r == 0, r == 127),
                             reads=[f"dg{di}", f"uvg{g % NSLOT}"], writes=[f"accp{half}"])

            for g0 in range(NSLOT):
                U(g0)
            for g in range(NG + 1):
                if g < NG:
                    H(g)
                if g >= 1:
                    A(g - 1)
                    Pm(g - 1)
                    if g - 1 + NSLOT < NG:
                        U(g - 1 + NSLOT)
                if idxgen is not None:
                    next(idxgen, None)
            if idxgen is not None:
                for _ in idxgen:
                    pass
            for half in range(2):
                s.op("dve", TT(h2[:, half * 512:(half + 1) * 512], accp[half][:, :], h1[b][:, half * 512:(half + 1) * 512], ALU.add),
                     reads=[f"accp{half}", f"h1{b}"], writes=["h2"])
            rms(h2[:, :], "h2", nf, "nf", yb[0][:, :], "yb0", 2, junk, "junk")
            s.dma("sp", None, DMA(D["y"][m * 128:(m + 1) * 128, :], yb[0][:, :]), reads=["yb0"], writes=["y"])

        for _ in idx_block(0):
            pass
        for _ in idx_block(1):
            pass
        for m in range(NQ):
            rows_block(m, idx_block(m + 2) if m + 2 < NQ else None)
        s.barrier()
        with nc.Block() as block:
            s.emit(block)


PHASES = (1, 2, 3, 4)


def build_program(phases=PHASES, dbg=False):
    nc = bass.Bass("TRN2", target_bir_lowering=False)
    D = {}

    def din(name, shape, dt=F32):
        D[name] = nc.dram_tensor(name, shape, dt, kind="ExternalInput").ap()

    def dscr(name, shape, dt):
        D[name] = nc.dram_tensor(name, shape, dt, kind="ExternalOutput" if dbg else "Internal").ap()

    din("xall", [NT, 1024])
    din("n1w", [1, 1024])
    din("w_in", [1024, 6656])
    din("bgate", [128, 16])
    for k in ("lq1", "lk1", "lq2", "lk2"):
        din(k, [1, 64])
    din("subln_w", [1, 128])
    din("w_bd", [1024, 1024])
    din("w_bc", [512, 1024])
    din("w_out", [1024, 1024])
    din("n2w", [1, 1024])
    din("wq", [1024, 1024])
    din("keys", [8, 2, 128, 64])
    npe = 16384 if 4 in phases else 128
    din("pu", [npe, 1024])
    din("pv", [npe, 1024])
    din("fnw", [1, 1024])
    din("colb", [128, 8, 544])
    din("tdiag", [128, 8, 256])
    din("cbg", [128, 8, 5, 128])
    din("cb0", [128, 8, 5, 128])
    dscr("kdT_s", [8, 128, NT], BF16)
    dscr("kcT_s", [4, 128, NT], BF16)
    dscr("vd_s", [8, 128, NB, 128], BF16)
    dscr("vc_s", [8, 128, NB, 64], BF16)
    dscr("qdT_s", [8, 128, 2048], BF16)
    dscr("qcT_s", [4, 128, 2048], BF16)
    dscr("od_s", [2048, 1024], BF16)
    dscr("oc_s", [2048, 512], BF16)
    dscr("h1_s", [2048, 1024], F32)
    D["puv16"] = nc.dram_tensor("puv16", [npe, 2048], BF16, kind="Internal").ap()
    D["y"] = nc.dram_tensor("y", [2048, 1024], F32, kind="ExternalOutput").ap()
    with ExitStack() as ses:
        for ph, fn in ((1, phase1), (2, phase2), (3, phase3), (4, phase4)):
            if ph in phases:
                fn(nc, ses, D)
    return nc


def host_tables(j, rel_bias):
    slopes = np.exp2(-8.0 * np.arange(1, 9, dtype=np.float64) / 8)
    p = np.arange(128, dtype=np.float64)
    colb = np.zeros((128, 8, 544), np.float32)
    for m in range(16):
        for L in range(4 * m + 4):
            idx = 2 * m * m + 2 * m + L
            dl = 4 * m + 4 - L
            if L < 4 - j:
                colb[:, :, idx] = NEG
            else:
                colb[:, :, idx] = (slopes[None, :] * (p[:, None] - 128.0 * dl - 127.0)).astype(np.float32)
    ki = p[:, None]
    qi = p[None, :]
    allowed = (np.floor(ki / 64) <= np.floor(qi / 64))
    td = np.where(allowed[None], slopes[:, None, None] * (np.minimum(ki, 2 * qi - ki)[None] - 127.0), NEG)
    tdiag = np.zeros((128, 8, 256), np.float32)
    tdiag[:, :, 0:128] = 8.0 * td.transpose(1, 0, 2)
    tdiag[:, :, 128:256] = 8.0 * td.transpose(1, 0, 2)
    cbg = np.zeros((128, 8, 5, 128), np.float32)
    cb0 = np.zeros((128, 8, 5, 128), np.float32)
    kI = np.arange(128)[:, None]
    qI = np.arange(128)[None, :]
    for dlt in range(5):
        rel = (qI - kI) + 128 * dlt
        qc = (512 + qI) // 64
        kc = (512 - 128 * dlt + kI) // 64
        ok = (qc - kc >= 0) & (qc - kc <= 8)
        tab = rel_bias[:, np.clip(rel, -128, 128) + 128]
        tile_ = np.where(ok[None], tab, np.float32(NEG)).astype(np.float32)
        cbg[:, :, dlt, :] = tile_.transpose(1, 0, 2)
        if dlt <= j:
            cb0[:, :, dlt, :] = tile_.transpose(1, 0, 2)
        else:
            cb0[:, :, dlt, :] = NEG
    return colb, tdiag, cbg, cb0


def make_in_maps(inputs):
    x = np.asarray(inputs["x"], np.float32)
    rel = np.asarray(inputs["chunk_rel_bias"], np.float32)[0]
    shared = {
        "n1w": np.ascontiguousarray(inputs["norm1_w"], dtype=np.float32).reshape(1, 1024),
        "w_in": np.ascontiguousarray(inputs["w_in"][0], dtype=np.float32),
        "bgate": np.ascontiguousarray(np.asarray(inputs["b_gate"][0], np.float32).reshape(16, 128).T),
        "lq1": np.asarray(inputs["diff_lq1"], np.float32).reshape(1, 64),
        "lk1": np.asarray(inputs["diff_lk1"], np.float32).reshape(1, 64),
        "lq2": np.asarray(inputs["diff_lq2"], np.float32).reshape(1, 64),
        "lk2": np.asarray(inputs["diff_lk2"], np.float32).reshape(1, 64),
        "subln_w": np.asarray(inputs["diff_subln_w"], np.float32).reshape(1, 128),
        "w_bd": np.ascontiguousarray(inputs["w_branch_diff"][0], dtype=np.float32),
        "w_bc": np.ascontiguousarray(inputs["w_branch_chunk"][0], dtype=np.float32),
        "w_out": np.ascontiguousarray(inputs["w_out"][0], dtype=np.float32),
        "n2w": np.asarray(inputs["norm2_w"], np.float32).reshape(1, 1024),
        "wq": np.ascontiguousarray(inputs["peer_wq"][0], dtype=np.float32),
        "keys": np.ascontiguousarray(inputs["peer_keys"][0], dtype=np.float32),
        "pu": np.ascontiguousarray(inputs["peer_u"][0], dtype=np.float32),
        "pv": np.ascontiguousarray(inputs["peer_v"][0], dtype=np.float32),
        "fnw": np.asarray(inputs["final_norm_w"], np.float32).reshape(1, 1024),
    }
    maps = []
    for c in range(8):
        b, j = c // 4, c % 4
        xall = np.zeros((NT, 1024), np.float32)
        g0 = max(0, j - 4)
        lo = (4 - j) * 128
        n = min(NT - lo, 8192)
        xall[lo:lo + n] = x[b, 0:n]
        colb, tdiag, cbg, cb0 = host_tables(j, rel)
        d = dict(shared)
        d.update({"xall": xall, "colb": colb, "tdiag": tdiag, "cbg": cbg, "cb0": cb0})
        maps.append(d)
    return maps


_NC_CACHE = {}


def kernel(**inputs):
    if "nc" not in _NC_CACHE:
        _NC_CACHE["nc"] = build_program()
    nc = _NC_CACHE["nc"]
    maps = make_in_maps(inputs)
    res = run_bass_kernel_spmd(nc, maps, core_ids=list(range(8)))
    out = np.zeros((2, 8192, 1024), np.float32)
    for c in range(8):
        b, j = c // 4, c % 4
        y = np.asarray(res.results[c]["y"]).reshape(16, 128, 1024)
        for m in range(16):
            gb = 4 * m + j
            out[b, gb * 128:(gb + 1) * 128] = y[m]
    return out
```

```python
import numpy as np
from contextlib import ExitStack
import concourse.bass as bass
import concourse.mybir as mybir
from concourse.bass_utils import run_bass_kernel_spmd

F32 = mybir.dt.float32
BF16 = mybir.dt.bfloat16
I32 = mybir.dt.int32
U32 = mybir.dt.uint32
ALU = mybir.AluOpType
AF = mybir.ActivationFunctionType

NB = 68
NT = NB * 128
NQ = 16
EPS = 1e-6
NEG = -30000.0
P2_HEADS = 8
P2_MODE = 3
P2_NQ = 16


class Sched:
    ENGS = ("pe", "act", "dve", "pool", "sp")

    def __init__(self, nc, es, prefix="p"):
        self.nc = nc
        self.es = es
        self.prefix = prefix
        self.sem = {}
        self.cnt = {}
        self.prog = {}
        for name in self.ENGS:
            self.sem[name] = es.enter_context(nc.semaphore(prefix + "s_" + name))
            self.cnt[name] = 0
            self.prog[name] = []
        self.seen = {name: {} for name in self.ENGS}
        self.res_w = {}
        self.res_r = {}
        self.dma_sem = {}
        self.dma_cnt = {}

    def _dma_tag(self, tag):
        if tag not in self.dma_sem:
            self.dma_sem[tag] = self.es.enter_context(self.nc.semaphore(self.prefix + "d_" + tag))
            self.dma_cnt[tag] = 0
        return self.dma_sem[tag]

    def _sem_of(self, key):
        return self.sem[key[1]] if key[0] == "e" else self.dma_sem[key[1]]

    def _collect(self, reads, writes, skip_waw=False):
        deps = {}

        def add(tok):
            if tok is None:
                return
            k, v = tok
            if deps.get(k, 0) < v:
                deps[k] = v

        for r in reads:
            add(self.res_w.get(r))
        for w in writes:
            if not skip_waw:
                add(self.res_w.get(w))
            for k, v in self.res_r.get(w, {}).items():
                add((k, v))
        return deps

    def _emit_waits(self, ename, deps):
        seen = self.seen[ename]
        for k, v in deps.items():
            if k == ("e", "pe") and ename == "pe":
                continue
            if seen.get(k, 0) >= v:
                continue
            self.prog[ename].append(("wait", self._sem_of(k), v))
            seen[k] = v

    def _update(self, tok, reads, writes):
        k, v = tok
        for w in writes:
            self.res_w[w] = tok
            self.res_r[w] = {}
        for r in reads:
            d = self.res_r.setdefault(r, {})
            if d.get(k, 0) < v:
                d[k] = v

    def op(self, ename, fn, reads=(), writes=()):
        deps = self._collect(reads, writes)
        self._emit_waits(ename, deps)
        self.cnt[ename] += 1
        self.prog[ename].append(("ins", fn, self.sem[ename], 1))
        self._update((("e", ename), self.cnt[ename]), reads, writes)

    def dma(self, qname, tag, fn, reads=(), writes=(), part=False):
        tag = writes[0] if writes else reads[0]
        if tag in ("kdT_s", "kcT_s", "vd_s", "vc_s", "qdT_s", "qcT_s", "od_s", "oc_s", "h1_s", "y", "puv16"):
            tag = tag + "_" + (reads[0] if reads else "w")
        sem = self._dma_tag(tag)
        deps = self._collect(reads, writes, skip_waw=part)
        self._emit_waits(qname, deps)
        self.dma_cnt[tag] += 16
        self.prog[qname].append(("ins", fn, sem, 16))
        self._update((("d", tag), self.dma_cnt[tag]), reads, writes)

    def barrier(self):
        for ename in self.ENGS:
            for tag, c in self.dma_cnt.items():
                if c:
                    self.prog[ename].append(("wait", self.dma_sem[tag], c))
            for name, c in self.cnt.items():
                if c and name != ename:
                    self.prog[ename].append(("wait", self.sem[name], c))

    def replay(self, ename, e):
        for item in self.prog[ename]:
            if item[0] == "wait":
                e.wait_ge(item[1], item[2])
            else:
                item[1](e).then_inc(item[2], item[3])

    def emit(self, block):
        s = self

        @block.tensor
        def _(e):
            s.replay("pe", e)

        @block.scalar
        def _(e):
            s.replay("act", e)

        @block.vector
        def _(e):
            s.replay("dve", e)

        @block.gpsimd
        def _(e):
            s.replay("pool", e)

        @block.sync
        def _(e):
            s.replay("sp", e)


def MM(out, lhsT, rhs, start, stop):
    return lambda e: e.matmul(out, lhsT=lhsT, rhs=rhs, start=start, stop=stop)


def TR(out, in_, ident):
    return lambda e: e.transpose(out, in_, ident)


def ACTF(out, in_, func, **kw):
    return lambda e: e.activation(out=out, in_=in_, func=func, **kw)


def ACOPY(out, in_):
    return lambda e: e.copy(out=out, in_=in_)


def CP(out, in_):
    return lambda e: e.tensor_copy(out=out, in_=in_)


def DMA(out, in_):
    return lambda e: e.dma_start(out=out, in_=in_)


def STT(out, in0, scalar, in1, op0, op1, accum_out=None):
    if accum_out is None:
        return lambda e: e.scalar_tensor_tensor(out=out, in0=in0, scalar=scalar, in1=in1, op0=op0, op1=op1)
    return lambda e: e.scalar_tensor_tensor(out=out, in0=in0, scalar=scalar, in1=in1, op0=op0, op1=op1, accum_out=accum_out)


def TS(out, in0, s1, s2, op0, op1):
    return lambda e: e.tensor_scalar(out=out, in0=in0, scalar1=s1, scalar2=s2, op0=op0, op1=op1)


def TSMUL(out, in0, s1):
    return lambda e: e.tensor_scalar_mul(out=out, in0=in0, scalar1=s1)


def TT(out, in0, in1, op):
    return lambda e: e.tensor_tensor(out=out, in0=in0, in1=in1, op=op)


def RECIP(out, in_):
    return lambda e: e.reciprocal(out=out, in_=in_)


def MEMSET(ap, v):
    return lambda e: e.memset(ap, v)


def make_ident(s, ident, name):
    s.op("pool", MEMSET(ident[:], 0.0), writes=[name])
    s.op("pool", lambda e: e.affine_select(out=ident[:], in_=ident[:], pattern=[[-1, 128]], compare_op=ALU.not_equal,
                                           fill=1.0, base=0, channel_multiplier=1), reads=[name], writes=[name])


class Ctx:
    pass


def load_weight(s, c, dst, dst_res, src, rows_k, col0, ncols, dcol0, wst, cnt):
    for k in range(rows_k):
        for cc in range(0, ncols, 1024):
            n = min(1024, ncols - cc)
            j = cnt[0] % len(wst)
            cnt[0] += 1
            s.dma("sp", f"wst{j}", DMA(wst[j][:, 0:n], src[k * 128:(k + 1) * 128, col0 + cc:col0 + cc + n]), writes=[f"wst{j}"])
            eng = "pool" if (cnt[0] % 2) else "dve"
            s.op(eng, CP(dst[:, k, dcol0 + cc:dcol0 + cc + n], wst[j][:, 0:n]), reads=[f"wst{j}"], writes=[dst_res])


def tile_loads(s, c, i, src_rows):
    for bi, src in enumerate(src_rows):
        s.dma("sp", f"xt{i}", DMA(c.xt[i][:, bi, :], src), writes=[f"xt{i}"], part=(bi != 0))


def norm_tile(s, c, i, src_rows, w_tile, w_res, bf16_out=True, preloaded=False):
    xt, xnb, xnT = c.xt[i], c.xnb[i], c.xnT[i]
    if not preloaded:
        tile_loads(s, c, i, src_rows)
    for bi in range(4):
        s.op("act", ACTF(c.junk[:], xt[:, bi, :], AF.Square, accum_out=c.ssq[:, i, bi:bi + 1]), reads=[f"xt{i}"],
             writes=["junk", f"ssq{i}"])
    s.op("dve", TS(c.rstd[:, i, :], c.ssq[:, i, :], 1.0 / 1024, EPS, ALU.mult, ALU.add), reads=[f"ssq{i}"], writes=[f"rstd{i}"])
    s.op("act", ACTF(c.rstd[:, i, :], c.rstd[:, i, :], AF.Ln), reads=[f"rstd{i}"], writes=[f"rstd{i}"])
    s.op("act", ACTF(c.rstd[:, i, :], c.rstd[:, i, :], AF.Exp, scale=-0.5), reads=[f"rstd{i}"], writes=[f"rstd{i}"])
    for bi in range(4):
        s.op("dve", STT(xnb[:, bi, :], xt[:, bi, :], c.rstd[:, i, bi:bi + 1], w_tile[:], ALU.mult, ALU.mult),
             reads=[f"xt{i}", f"rstd{i}", w_res], writes=[f"xnb{i}_{bi}"])
    for bi in range(4):
        tpb = c.tp[bi % 2]
        for k in range(8):
            s.op("pe", TR(tpb[:, k, :], xnb[:, bi, k * 128:(k + 1) * 128], c.ident[:]), reads=[f"xnb{i}_{bi}", "ident"],
                 writes=[f"tp{bi % 2}"])
        if bi % 2 == 0:
            s.op("act", ACOPY(xnT[:, :, bi * 128:(bi + 1) * 128], tpb[:, :, :]), reads=[f"tp{bi % 2}"], writes=[f"xnT{i}_{bi}"])
        else:
            s.op("dve", CP(xnT[:, :, bi * 128:(bi + 1) * 128], tpb[:, :, :]), reads=[f"tp{bi % 2}"], writes=[f"xnT{i}_{bi}"])
    return [f"xnT{i}_{bi}" for bi in range(4)]


def phase1(nc, ses, D):
    with ExitStack() as es:
        def sb(name, shape, dt):
            return es.enter_context(nc.sbuf_tensor("a_" + name, shape, dt))

        def ps(name, shape, dt=F32):
            return es.enter_context(nc.psum_tensor("a_" + name, shape, dt))

        s = Sched(nc, ses, "a")
        c = Ctx()
        wb = sb("wb", [128, 8, 3072], BF16)
        wst = [sb(f"wst{i}", [128, 1024], F32) for i in range(2)]
        n1 = sb("n1", [128, 1024], F32)
        c.ident = sb("ident", [128, 128], BF16)
        c.xt = [sb(f"xt{i}", [128, 4, 1024], F32) for i in range(2)]
        c.xnb = [sb(f"xnb{i}", [128, 4, 1024], BF16) for i in range(2)]
        c.xnT = [sb(f"xnT{i}", [128, 8, 512], BF16) for i in range(2)]
        c.ssq = sb("ssq", [128, 2, 4], F32)
        c.rstd = sb("rstd", [128, 2, 4], F32)
        c.junk = sb("junk", [128, 1024], BF16)
        stgk = [sb(f"stgk{i}", [128, 12, 512], BF16) for i in range(2)]
        stgv = [sb(f"stgv{i}", [128, 4, 1536], BF16) for i in range(2)]
        c.tp = [ps(f"tp{i}", [128, 8, 128], BF16) for i in range(2)]
        mp = [ps(f"mp{i}", [128, 512], F32) for i in range(4)]

        make_ident(s, c.ident, "ident")
        s.dma("sp", "n1", DMA(n1[:], D["n1w"].partition_broadcast(128)), writes=["n1"])
        cnt = [0]
        w_in = D["w_in"]
        load_weight(s, c, wb, "wb", w_in, 8, 1024, 2048, 0, wst, cnt)
        load_weight(s, c, wb, "wb", w_in, 8, 3584, 1024, 2048, wst, cnt)
        mcnt = [0]

        def evac(dst, bank, bres, dres, scale=None):
            eng = "act" if (mcnt[0] % 2 == 0) else "dve"
            if scale is not None:
                s.op("act", (lambda dst, bank: lambda e: e.mul(out=dst, in_=bank, mul=scale))(dst, bank), reads=[bres], writes=[dres])
            elif eng == "act":
                s.op("act", ACOPY(dst, bank), reads=[bres], writes=[dres])
            else:
                s.op("dve", CP(dst, bank), reads=[bres], writes=[dres])

        def featmajor_proj(i, xres, ncolchunks, colfn, stg, stgres, qc_scale=False):
            for hh in range(ncolchunks):
                col0 = colfn(hh)
                bi_ = mcnt[0] % 4
                bank = mp[bi_]
                for k in range(8):
                    s.op("pe", MM(bank[:, :], wb[:, k, col0:col0 + 128], c.xnT[i][:, k, :], k == 0, k == 7),
                         reads=["wb"] + xres, writes=[f"mp{bi_}"])
                evac(stg[:, hh, :], bank[:, :], f"mp{bi_}", stgres, scale=(0.125 if (qc_scale and hh >= 8) else None))
                mcnt[0] += 1

        def kv_rows(T):
            return [D["xall"][(T * 4 + bi) * 128:(T * 4 + bi + 1) * 128, :] for bi in range(4)]

        tile_loads(s, c, 0, kv_rows(0))
        for T in range(NB // 4):
            i = T % 2
            xres = norm_tile(s, c, i, kv_rows(T), n1, "n1", preloaded=True)
            if T + 1 < NB // 4:
                tile_loads(s, c, (T + 1) % 2, kv_rows(T + 1))
            featmajor_proj(i, xres, 12, lambda hh: hh * 128 if hh < 8 else 2048 + (hh - 8) * 128, stgk[i], f"stgk{i}")
            s.dma("sp", f"stgk{i}", DMA(D["kdT_s"][:, :, T * 512:(T + 1) * 512].rearrange("h p t -> p h t"), stgk[i][:, 0:8, :]),
                  reads=[f"stgk{i}"], writes=["kdT_s"])
            s.dma("sp", f"stgk{i}", DMA(D["kcT_s"][:, :, T * 512:(T + 1) * 512].rearrange("h p t -> p h t"), stgk[i][:, 8:12, :]),
                  reads=[f"stgk{i}"], writes=["kcT_s"])
            for bi in range(4):
                for half in range(3):
                    col0 = 1024 + half * 512 if half < 2 else 2560
                    bi_ = mcnt[0] % 4
                    bank = mp[bi_]
                    for k in range(8):
                        s.op("pe", MM(bank[:, :], c.xnT[i][:, k, bi * 128:(bi + 1) * 128], wb[:, k, col0:col0 + 512], k == 0, k == 7),
                             reads=["wb"] + xres, writes=[f"mp{bi_}"])
                    evac(stgv[i][:, bi, half * 512:(half + 1) * 512], bank[:, :], f"mp{bi_}", f"stgv{i}")
                    mcnt[0] += 1
            for bi in range(4):
                s.dma("sp", f"stgv{i}", DMA(D["vd_s"][:, :, T * 4 + bi, :].rearrange("h p e -> p h e"),
                                            stgv[i][:, bi, 0:1024].rearrange("p (h e) -> p h e", h=8)),
                      reads=[f"stgv{i}"], writes=["vd_s"])
                s.dma("sp", f"stgv{i}", DMA(D["vc_s"][:, :, T * 4 + bi, :].rearrange("h p e -> p h e"),
                                            stgv[i][:, bi, 1024:1536].rearrange("p (h e) -> p h e", h=8)),
                      reads=[f"stgv{i}"], writes=["vc_s"])
        load_weight(s, c, wb, "wb", w_in, 8, 0, 1024, 0, wst, cnt)
        load_weight(s, c, wb, "wb", w_in, 8, 3072, 512, 1024, wst, cnt)
        for G in range(4):
            i = G % 2
            rows = [D["xall"][(4 * (4 * G + bi) + 4) * 128:(4 * (4 * G + bi) + 5) * 128, :] for bi in range(4)]
            xres = norm_tile(s, c, i, rows, n1, "n1")
            featmajor_proj(i, xres, 12, lambda hh: hh * 128, stgk[i], f"stgk{i}", qc_scale=True)
            s.dma("sp", f"stgk{i}", DMA(D["qdT_s"][:, :, G * 512:(G + 1) * 512].rearrange("h p t -> p h t"), stgk[i][:, 0:8, :]),
                  reads=[f"stgk{i}"], writes=["qdT_s"])
            s.dma("sp", f"stgk{i}", DMA(D["qcT_s"][:, :, G * 512:(G + 1) * 512].rearrange("h p t -> p h t"), stgk[i][:, 8:12, :]),
                  reads=[f"stgk{i}"], writes=["qcT_s"])
        s.barrier()
        with nc.Block() as block:
            s.emit(block)


def phase2(nc, ses, D):
    with ExitStack() as es:
        def sb(name, shape, dt):
            return es.enter_context(nc.sbuf_tensor("b_" + name, shape, dt))

        def ps(name, shape, dt=F32):
            return es.enter_context(nc.psum_tensor("b_" + name, shape, dt))

        s = Sched(nc, ses, "b")
        kT = [sb(f"kT{i}", [128, NT], BF16) for i in range(2)]
        vv = [sb(f"vv{i}", [128, NB, 132], BF16) for i in range(2)]
        qT = [sb(f"qT{i}", [128, 2048], BF16) for i in range(2)]
        kcT = [sb(f"kcT{i}", [128, NT], BF16) for i in range(2)]
        qcT = [sb(f"qcT{i}", [128, 2048], BF16) for i in range(2)]
        vc = [sb(f"vc{i}", [128, NB, 68], BF16) for i in range(2)]
        colb = [sb(f"colb{i}", [128, 544], F32) for i in range(2)]
        tdg = [sb(f"tdg{i}", [128, 256], F32) for i in range(2)]
        cbg = [sb(f"cbg{i}", [128, 5, 128], F32) for i in range(2)]
        cb0 = [sb(f"cb0{i}", [128, 5, 128], F32) for i in range(2)]
        P = [sb(f"P{i}", [128, 256], BF16) for i in range(4)]
        tmp = [sb(f"tmp{i}", [128, 256], F32) for i in range(2)]
        lq = sb("lq", [128, 4, 64], F32)
        lj = sb("lj", [128, 64], F32)
        lsc = sb("lsc", [128, 4], F32)
        neglam = sb("neglam", [128, 1], F32)
        sw8 = sb("sw8", [128, 128], F32)
        r12 = sb("r12", [128, 4], F32)
        t1 = sb("t1", [128, 128], F32)
        o_ = sb("o_", [128, 128], F32)
        oj = sb("oj", [128, 128], F32)
        sq = sb("sq", [128, 2], F32)
        odst = [sb(f"odst{i}", [128, 128], BF16) for i in range(2)]
        ocst = [sb(f"ocst{i}", [128, 64], BF16) for i in range(2)]
        identf = sb("identf", [128, 128], F32)
        S = [ps(f"S{i}", [128, 2, 512], F32) for i in range(2)]
        O1 = ps("O1", [128, 512], F32)
        O2 = ps("O2", [128, 512], F32)
        Oc = ps("Oc", [128, 512], F32)

        do_conv = D["pu"].shape[0] == 16384
        if do_conv:
            cin = [sb(f"cin{i}", [128, 2, 1024], F32) for i in range(3)]
            cout = [sb(f"cout{i}", [128, 2, 1024], BF16) for i in range(2)]
            dst_v = D["puv16"].rearrange("(p r) d -> p r d", p=128)

            def conv_src(c):
                tab = "pu" if c < 64 else "pv"
                return D[tab].rearrange("(p r) d -> p r d", p=128)[:, 2 * (c % 64):2 * (c % 64) + 2, :]

            def conv_in(c):
                s.dma("sp", None, DMA(cin[c % 3][:, :, :], conv_src(c)), writes=[f"cin{c % 3}"])

            def conv_step(c):
                if c + 1 < 128:
                    conv_in(c + 1)
                s.op("dve", CP(cout[c % 2][:, :, :], cin[c % 3][:, :, :]), reads=[f"cin{c % 3}"], writes=[f"cout{c % 2}"])
                off = 0 if c < 64 else 1024
                s.dma("sp", None, DMA(dst_v[:, 2 * (c % 64):2 * (c % 64) + 2, off:off + 1024], cout[c % 2][:, :, :]),
                      reads=[f"cout{c % 2}"], writes=["puv16"])

            conv_in(0)
        make_ident(s, identf, "identf")
        for i in range(2):
            s.op("pool", MEMSET(vv[i][:, :, 128:132], 1.0), writes=[f"vv{i}_ones"])
            s.op("pool", MEMSET(vc[i][:, :, 64:68], 1.0), writes=[f"vc{i}_ones"])
        for n_, key in enumerate(["lq1", "lk1", "lq2", "lk2"]):
            s.dma("sp", "lq", DMA(lq[:, n_, :], D[key].partition_broadcast(128)), writes=["lq"])
        s.dma("sp", "lq", DMA(sw8[:], D["subln_w"].partition_broadcast(128)), writes=["sw8"])
        s.op("dve", STT(lj[:], lq[:, 0, :], 1.0, lq[:, 1, :], ALU.mult, ALU.mult, accum_out=lsc[:, 0:1]), reads=["lq"], writes=["lj", "lsc"])
        s.op("dve", STT(lj[:], lq[:, 2, :], 1.0, lq[:, 3, :], ALU.mult, ALU.mult, accum_out=lsc[:, 1:2]), reads=["lq", "lj"], writes=["lj", "lsc"])
        s.op("act", ACTF(lsc[:, 2:4], lsc[:, 0:2], AF.Exp), reads=["lsc"], writes=["lsc2"])
        s.op("dve", TT(neglam[:], lsc[:, 3:4], lsc[:, 2:3], ALU.subtract), reads=["lsc2"], writes=["neglam"])
        s.op("dve", TS(neglam[:], neglam[:], -0.2, None, ALU.add, ALU.add) if False else
             (lambda e: e.tensor_scalar_add(out=neglam[:], in0=neglam[:], scalar1=-0.2)), reads=["neglam"], writes=["neglam"])
        s.op("dve", TSMUL(sw8[:], sw8[:], 0.8), reads=["sw8"], writes=["sw8"])

        def head_loads(h):
            b = h % 2
            pb_ = (h // 2) % 2
            s.dma("sp", f"kT{b}", DMA(kT[b][:, :], D["kdT_s"][h]), reads=["kdT_s"], writes=[f"kT{b}"])
            s.dma("sp", f"vv{b}", DMA(vv[b][:, :, 0:128], D["vd_s"][h]), reads=["vd_s"], writes=[f"vv{b}"])
            s.dma("sp", f"qT{b}", DMA(qT[b][:, :], D["qdT_s"][h]), reads=["qdT_s"], writes=[f"qT{b}"])
            if h % 2 == 0:
                s.dma("sp", f"kcT{pb_}", DMA(kcT[pb_][:, :], D["kcT_s"][h // 2]), reads=["kcT_s"], writes=[f"kcT{pb_}"])
                s.dma("sp", f"qcT{pb_}", DMA(qcT[pb_][:, :], D["qcT_s"][h // 2]), reads=["qcT_s"], writes=[f"qcT{pb_}"])
            s.dma("sp", f"vc{b}", DMA(vc[b][:, :, 0:64], D["vc_s"][h]), reads=["vc_s"], writes=[f"vc{b}"])
            s.dma("sp", f"cst{b}", DMA(colb[b][:, :], D["colb"][:, h, :]), writes=[f"colb{b}"])
            s.dma("sp", f"cst{b}", DMA(tdg[b][:, :], D["tdiag"][:, h, :]), writes=[f"tdg{b}"])
            s.dma("sp", f"cst{b}", DMA(cbg[b][:, :, :], D["cbg"][:, h, :, :]), writes=[f"cbg{b}"])
            s.dma("sp", f"cst{b}", DMA(cb0[b][:, :, :], D["cb0"][:, h, :, :]), writes=[f"cb0{b}"])

        head_loads(0)
        for h in range(P2_HEADS):
            b = h % 2
            pb_ = (h // 2) % 2
            if h + 1 < P2_HEADS:
                head_loads(h + 1)
            hp = (h % 2) * 64
            for m in range(P2_NQ):
                items = [("c", 4 * m + 4 - dlt, dlt) for dlt in (4, 3, 2, 1, 0)]
                dmax = int(160.0 / (128.0 * 2.0 ** (-(h + 1)))) + 1
                diff_L = [L for L in range(4 * m + 4) if (4 * m + 4 - L) <= dmax]
                first_L = diff_L[0] if diff_L else 4 * m + 4
                items += [("d", L, 0) for L in diff_L]
                items.append(("g", 4 * m + 4, 0))
                n = len(items)
                qs = slice(m * 128, (m + 1) * 128)
                if do_conv:
                    conv_step(h * NQ + m)

                def emit_qk(t):
                    kind, L, dlt = items[t]
                    si = t % 2
                    pi = t % 4
                    Sb, Pb = S[si], P[pi]
                    ks = slice(L * 128, (L + 1) * 128)
                    if kind == "c":
                        bt = cb0[b] if m == 0 else cbg[b]
                        btr = f"cb0{b}" if m == 0 else f"cbg{b}"
                        s.op("pe", MM(Sb[:, 0, 0:128], kcT[pb_][hp:hp + 64, ks], qcT[pb_][hp:hp + 64, qs], True, False),
                             reads=[f"kcT{pb_}", f"qcT{pb_}"], writes=[f"S{si}"])
                        s.op("pe", MM(Sb[:, 0, 0:128], identf[:, :], bt[:, dlt, :], False, True),
                             reads=["identf", btr], writes=[f"S{si}"])
                        s.op("act", ACTF(Pb[:, 0:128], Sb[:, 0, 0:128], AF.Exp), reads=[f"S{si}"], writes=[f"P{pi}"])
                    else:
                        dg_ = (kind == "g")
                        s.op("pe", MM(Sb[:, 0, 0:128], kT[b][0:64, ks], qT[b][0:64, qs], True, not dg_),
                             reads=[f"kT{b}", f"qT{b}"], writes=[f"S{si}"])
                        if dg_:
                            s.op("pe", MM(Sb[:, 0, 0:128], identf[:, :], tdg[b][:, 0:128], False, True),
                                 reads=["identf", f"tdg{b}"], writes=[f"S{si}"])
                        s.op("pe", MM(Sb[:, 1, 0:128], kT[b][64:128, ks], qT[b][64:128, qs], True, not dg_),
                             reads=[f"kT{b}", f"qT{b}"], writes=[f"S{si}"])
                        if dg_:
                            s.op("pe", MM(Sb[:, 1, 0:128], identf[:, :], tdg[b][:, 128:256], False, True),
                                 reads=["identf", f"tdg{b}"], writes=[f"S{si}"])
                        if kind == "d":
                            idx = 2 * m * m + 2 * m + L
                            s.op("act", ACTF(Pb[:, 0:256].rearrange("p (a n) -> p a n", a=2), Sb[:, :, 0:128], AF.Exp,
                                             bias=colb[b][:, idx:idx + 1], scale=0.125),
                                 reads=[f"S{si}", f"colb{b}"], writes=[f"P{pi}"])
                        else:
                            s.op("act", ACTF(Pb[:, 0:256].rearrange("p (a n) -> p a n", a=2), Sb[:, :, 0:128], AF.Exp, scale=0.125),
                                 reads=[f"S{si}"], writes=[f"P{pi}"])

                def emit_av(t):
                    kind, L, dlt = items[t]
                    si = t % 4
                    Pb = P[si]
                    if kind == "c":
                        s.op("pe", MM(Oc[:, 0:68], Pb[:, 0:128], vc[b][:, L, :], dlt == 4, dlt == 0),
                             reads=[f"P{si}", f"vc{b}", f"vc{b}_ones"], writes=["Oc"])
                    else:
                        s.op("pe", MM(O1[:, 0:132], Pb[:, 0:128], vv[b][:, L, :], L == first_L, kind == "g"),
                             reads=[f"P{si}", f"vv{b}", f"vv{b}_ones"], writes=["O1"])
                        s.op("pe", MM(O2[:, 0:132], Pb[:, 128:256], vv[b][:, L, :], L == first_L, kind == "g"),
                             reads=[f"P{si}", f"vv{b}", f"vv{b}_ones"], writes=["O2"])

                for t in range(n + 1):
                    if t < n:
                        emit_qk(t)
                    if t >= 1 and P2_MODE >= 2:
                        emit_av(t - 1)
                    if t - 1 == 4 and P2_MODE >= 3:
                        st = ocst[m % 2]
                        s.op("dve", RECIP(r12[:, 2:3], Oc[:, 64:65]), reads=["Oc"], writes=["r12c"])
                        s.op("dve", TSMUL(st[:, :], Oc[:, 0:64], r12[:, 2:3]), reads=["Oc", "r12c"], writes=[f"ocst{m % 2}"])
                        s.dma("sp", f"ocst{m % 2}", DMA(D["oc_s"][m * 128:(m + 1) * 128, h * 64:(h + 1) * 64], st[:, :]),
                              reads=[f"ocst{m % 2}"], writes=["oc_s"])
                if P2_MODE < 3:
                    continue
                st = odst[m % 2]
                s.op("dve", RECIP(r12[:, 0:1], O1[:, 128:129]), reads=["O1"], writes=["r12a"])
                s.op("dve", RECIP(r12[:, 1:2], O2[:, 128:129]), reads=["O2"], writes=["r12b"])
                s.op("dve", TT(r12[:, 3:4], r12[:, 1:2], neglam[:, 0:1], ALU.mult), reads=["r12b", "neglam"], writes=["r12d"])
                s.op("dve", TSMUL(t1[:, :], O1[:, 0:128], r12[:, 0:1]), reads=["O1", "r12a"], writes=["t1"])
                s.op("dve", STT(o_[:, :], O2[:, 0:128], r12[:, 3:4], t1[:, :], ALU.mult, ALU.add), reads=["O2", "r12d", "t1"], writes=["o_"])
                s.op("dve", STT(oj[:, :], o_[:, :], 1.0, o_[:, :], ALU.mult, ALU.mult, accum_out=sq[:, 0:1]), reads=["o_"], writes=["oj", "sq"])
                s.op("dve", TS(sq[:, 1:2], sq[:, 0:1], 1.0 / 128, EPS, ALU.mult, ALU.add), reads=["sq"], writes=["sq1"])
                s.op("act", ACTF(sq[:, 1:2], sq[:, 1:2], AF.Ln), reads=["sq1"], writes=["sq1"])
                s.op("act", ACTF(sq[:, 1:2], sq[:, 1:2], AF.Exp, scale=-0.5), reads=["sq1"], writes=["sq1"])
                s.op("dve", STT(st[:, :], o_[:, :], sq[:, 1:2], sw8[:, :], ALU.mult, ALU.mult), reads=["o_", "sq1", "sw8"], writes=[f"odst{m % 2}"])
                s.dma("sp", f"odst{m % 2}", DMA(D["od_s"][m * 128:(m + 1) * 128, h * 128:(h + 1) * 128], st[:, :]),
                      reads=[f"odst{m % 2}"], writes=["od_s"])
        s.barrier()
        with nc.Block() as block:
            s.emit(block)


def phase3(nc, ses, D):
    with ExitStack() as es:
        def sb(name, shape, dt):
            return es.enter_context(nc.sbuf_tensor("c_" + name, shape, dt))

        def ps(name, shape, dt=F32):
            return es.enter_context(nc.psum_tensor("c_" + name, shape, dt))

        s = Sched(nc, ses, "c")
        c = Ctx()
        wbd = sb("wbd", [128, 8, 1024], BF16)
        wbc = sb("wbc", [128, 4, 1024], BF16)
        wg = sb("wg", [128, 8, 2048], BF16)
        wo = sb("wo", [128, 8, 1024], BF16)
        wst = [sb(f"wst{i}", [128, 1024], F32) for i in range(2)]
        n1 = sb("n1", [128, 1024], F32)
        bg = sb("bg", [128, 16], F32)
        c.ident = sb("ident", [128, 128], BF16)
        c.xt = [sb(f"xt{i}", [128, 4, 1024], F32) for i in range(2)]
        c.xnb = [sb(f"xnb{i}", [128, 4, 1024], BF16) for i in range(1)] * 2
        c.xnT = [sb(f"xnT{i}", [128, 8, 512], BF16) for i in range(1)] * 2
        c.ssq = sb("ssq", [128, 2, 4], F32)
        c.rstd = sb("rstd", [128, 2, 4], F32)
        c.junk = sb("junk", [128, 1024], BF16)
        odb = sb("odb", [128, 4, 1024], BF16)
        ocb = sb("ocb", [128, 4, 512], BF16)
        odT = sb("odT", [128, 8, 512], BF16)
        ocT = sb("ocT", [128, 4, 512], BF16)
        sg = [sb(f"sg{i}", [128, 512], F32) for i in range(2)]
        m12 = [sb(f"m12{i}", [128, 512], F32) for i in range(2)]
        mT = sb("mT", [128, 8, 512], BF16)
        h1 = [sb(f"h1{i}", [128, 1024], F32) for i in range(2)]
        c.tp = [ps(f"tp{i}", [128, 8, 128], BF16) for i in range(2)]
        pa = [ps(f"pa{i}", [128, 512], F32) for i in range(4)]
        ph = [ps(f"ph{i}", [128, 512], F32) for i in range(2)]

        make_ident(s, c.ident, "ident")
        s.dma("sp", "n1", DMA(n1[:], D["n1w"].partition_broadcast(128)), writes=["n1"])
        s.dma("sp", "n1", DMA(bg[:], D["bgate"]), writes=["bg"])
        cnt = [0]
        load_weight(s, c, wbd, "wbd", D["w_bd"], 8, 0, 1024, 0, wst, cnt)
        load_weight(s, c, wbc, "wbc", D["w_bc"], 4, 0, 1024, 0, wst, cnt)
        load_weight(s, c, wg, "wg", D["w_in"], 8, 4608, 2048, 0, wst, cnt)
        load_weight(s, c, wo, "wo", D["w_out"], 8, 0, 1024, 0, wst, cnt)
        for G in range(4):
            i = G % 2
            rows = [D["xall"][(4 * (4 * G + bi) + 4) * 128:(4 * (4 * G + bi) + 5) * 128, :] for bi in range(4)]
            xres = norm_tile_p3(s, c, i, rows, n1)
            s.dma("sp", "odb", DMA(odb[:, :, :], D["od_s"][G * 512:(G + 1) * 512, :].rearrange("(l p) d -> p l d", p=128)),
                  reads=["od_s"], writes=["odb"])
            s.dma("sp", "ocb", DMA(ocb[:, :, :], D["oc_s"][G * 512:(G + 1) * 512, :].rearrange("(l p) d -> p l d", p=128)),
                  reads=["oc_s"], writes=["ocb"])
            tcount = 0
            for bi in range(4):
                for (src, nk, dstT, sres, dres) in ((odb, 8, odT, "odb", "odT"), (ocb, 4, ocT, "ocb", "ocT")):
                    tpi = tcount % 2
                    tcount += 1
                    tpb = c.tp[tpi]
                    for k in range(nk):
                        s.op("pe", TR(tpb[:, k, :], src[:, bi, k * 128:(k + 1) * 128], c.ident[:]), reads=[sres, "ident"], writes=[f"tp{tpi}"])
                    if tpi == 0:
                        s.op("act", ACOPY(dstT[:, :, bi * 128:(bi + 1) * 128], tpb[:, 0:nk, :]), reads=[f"tp{tpi}"], writes=[dres])
                    else:
                        s.op("dve", CP(dstT[:, :, bi * 128:(bi + 1) * 128], tpb[:, 0:nk, :]), reads=[f"tp{tpi}"], writes=[dres])
            for oc in range(8):
                cs = slice(oc * 128, (oc + 1) * 128)
                A, B, C_, D_ = pa
                for k in range(8):
                    s.op("pe", MM(A[:, :], wbd[:, k, cs], odT[:, k, :], k == 0, k == 7), reads=["wbd", "odT"], writes=["pa0"])
                for k in range(4):
                    s.op("pe", MM(B[:, :], wbc[:, k, cs], ocT[:, k, :], k == 0, k == 3), reads=["wbc", "ocT"], writes=["pa1"])
                for k in range(8):
                    s.op("pe", MM(C_[:, :], wg[:, k, cs], c.xnT[0][:, k, :], k == 0, k == 7), reads=["wg"] + xres, writes=["pa2"])
                gs = slice(1024 + oc * 128, 1024 + (oc + 1) * 128)
                for k in range(8):
                    s.op("pe", MM(D_[:, :], wg[:, k, gs], c.xnT[0][:, k, :], k == 0, k == 7), reads=["wg"] + xres, writes=["pa3"])
                s.op("act", ACTF(sg[0][:, :], C_[:, :], AF.Sigmoid, bias=bg[:, oc:oc + 1]), reads=["pa2", "bg"], writes=["sg0"])
                s.op("act", ACTF(sg[1][:, :], D_[:, :], AF.Sigmoid, bias=bg[:, 8 + oc:9 + oc]), reads=["pa3", "bg"], writes=["sg1"])
                s.op("dve", TT(m12[0][:, :], A[:, :], sg[0][:, :], ALU.mult), reads=["pa0", "sg0"], writes=["m120"])
                s.op("dve", TT(m12[1][:, :], B[:, :], sg[1][:, :], ALU.mult), reads=["pa1", "sg1"], writes=["m121"])
                s.op("pool", TT(mT[:, oc, :], m12[0][:, :], m12[1][:, :], ALU.add), reads=["m120", "m121"], writes=["mT"])
            for bi in range(4):
                m = 4 * G + bi
                hb = h1[bi % 2]
                for half in range(2):
                    pb = ph[half]
                    for k in range(8):
                        s.op("pe", MM(pb[:, :], mT[:, k, bi * 128:(bi + 1) * 128], wo[:, k, half * 512:(half + 1) * 512], k == 0, k == 7),
                             reads=["mT", "wo"], writes=[f"ph{half}"])
                    s.op("dve", TT(hb[:, half * 512:(half + 1) * 512], pb[:, :], c.xt[i][:, bi, half * 512:(half + 1) * 512], ALU.add),
                         reads=[f"ph{half}", f"xt{i}"], writes=[f"h1{bi % 2}"])
                s.dma("sp", f"h1{bi % 2}", DMA(D["h1_s"][m * 128:(m + 1) * 128, :], hb[:, :]), reads=[f"h1{bi % 2}"], writes=["h1_s"])
        s.barrier()
        with nc.Block() as block:
            s.emit(block)


def norm_tile_p3(s, c, i, src_rows, n1):
    xt, xnb, xnT = c.xt[i], c.xnb[0], c.xnT[0]
    for bi, src in enumerate(src_rows):
        s.dma("sp", f"xt{i}", DMA(xt[:, bi, :], src), writes=[f"xt{i}"], part=(bi != 0))
    for bi in range(4):
        s.op("act", ACTF(c.junk[:], xt[:, bi, :], AF.Square, accum_out=c.ssq[:, i, bi:bi + 1]), reads=[f"xt{i}"],
             writes=["junk", f"ssq{i}"])
    s.op("dve", TS(c.rstd[:, i, :], c.ssq[:, i, :], 1.0 / 1024, EPS, ALU.mult, ALU.add), reads=[f"ssq{i}"], writes=[f"rstd{i}"])
    s.op("act", ACTF(c.rstd[:, i, :], c.rstd[:, i, :], AF.Ln), reads=[f"rstd{i}"], writes=[f"rstd{i}"])
    s.op("act", ACTF(c.rstd[:, i, :], c.rstd[:, i, :], AF.Exp, scale=-0.5), reads=[f"rstd{i}"], writes=[f"rstd{i}"])
    for bi in range(4):
        s.op("dve", STT(xnb[:, bi, :], xt[:, bi, :], c.rstd[:, i, bi:bi + 1], n1[:], ALU.mult, ALU.mult),
             reads=[f"xt{i}", f"rstd{i}", "n1"], writes=[f"xnb_{bi}"])
    for bi in range(4):
        tpb = c.tp[bi % 2]
        for k in range(8):
            s.op("pe", TR(tpb[:, k, :], xnb[:, bi, k * 128:(k + 1) * 128], c.ident[:]), reads=[f"xnb_{bi}", "ident"],
                 writes=[f"tp{bi % 2}"])
        if bi % 2 == 0:
            s.op("act", ACOPY(xnT[:, :, bi * 128:(bi + 1) * 128], tpb[:, :, :]), reads=[f"tp{bi % 2}"], writes=[f"xnT_{bi}"])
        else:
            s.op("dve", CP(xnT[:, :, bi * 128:(bi + 1) * 128], tpb[:, :, :]), reads=[f"tp{bi % 2}"], writes=[f"xnT_{bi}"])
    return [f"xnT_{bi}" for bi in range(4)]


NUV = 16


def phase4(nc, ses, D):
    with ExitStack() as es:
        def sb(name, shape, dt):
            return es.enter_context(nc.sbuf_tensor("d_" + name, shape, dt))

        def ps(name, shape, dt=F32):
            return es.enter_context(nc.psum_tensor("d_" + name, shape, dt))

        s = Sched(nc, ses, "d")
        wq = sb("wq", [128, 8, 1024], F32)
        kt0 = sb("kt0", [128, 8, 128], F32)
        keysT = sb("keysT", [128, 8, 128], F32)
        n2 = sb("n2", [128, 1024], F32)
        nf = sb("nf", [128, 1024], F32)
        ident = sb("identf", [128, 128], F32)
        h1 = [sb(f"h1{i}", [128, 1024], F32) for i in range(3)]
        xn2 = [sb(f"xn2{i}", [128, 1024], F32) for i in range(3)]
        junk = sb("junk", [128, 1024], F32)
        junk2 = sb("junk2", [128, 1024], BF16)
        xn2T = sb("xn2T", [128, 8, 128], F32)
        qTt = sb("qTt", [128, 8, 128], F32)
        sc = sb("sc", [128, 16, 128], F32)
        sc2 = sb("sc2", [128, 128], F32)
        mx = sb("mx", [128, 16, 16], F32)
        mi = sb("mi", [128, 16, 16], U32)
        mif = sb("mif", [128, 16, 16], F32)
        cand = sb("cand", [128, 8, 256], F32)
        cand2 = sb("cand2", [128, 256], F32)
        ci = sb("ci", [128, 8, 16], U32)
        cii = sb("cii", [128, 8, 16], U32)
        cij = sb("cij", [128, 8, 16], U32)
        ciif = sb("ciif", [128, 8, 16], F32)
        cijf = sb("cijf", [128, 8, 16], F32)
        iota16 = sb("iota16", [128, 16], F32)
        eq = sb("eq", [128, 8, 16, 16], F32)
        sel1 = sb("sel1", [128, 8, 16], F32)
        sel2 = sb("sel2", [128, 8, 16], F32)
        sc16 = sb("sc16", [128, 8, 16], F32)
        ex = sb("ex", [128, 8, 16], F32)
        zz = sb("zz", [128, 16], F32)
        gg = [sb(f"gg{i}", [128, 128], F32) for i in range(3)]
        ef = sb("ef", [128, 128], F32)
        ei = [sb(f"ei{i}", [128, 128], I32) for i in range(3)]
        hid = [sb(f"hid{i}", [128, 128], F32) for i in range(2)]
        gl = [sb(f"gl{i}", [128, 128], F32) for i in range(2)]
        aa = [sb(f"aa{i}", [128, 128], F32) for i in range(2)]
        dg = [sb(f"dg{i}", [128, 128], BF16) for i in range(4)]
        uvb = [sb(f"uv{i}", [128, 2048], BF16) for i in range(NUV)]
        xn2b = [sb(f"xn2b{i}", [128, 1024], BF16) for i in range(3)]
        h2 = sb("h2", [128, 1024], F32)
        ssq = sb("ssq", [128, 8], F32)
        yb = [sb("yb0", [128, 1024], F32)] * 2
        tq = [ps(f"tq{i}", [128, 512], F32) for i in range(2)]
        sps = [ps(f"sps{i}", [128, 512], F32) for i in range(4)]
        accp = [ps(f"accp{i}", [128, 512], F32) for i in range(2)]

        s.op("pool", MEMSET(ident[:], 0.0), writes=["ident"])
        s.op("pool", lambda e: e.affine_select(out=ident[:], in_=ident[:], pattern=[[-1, 128]], compare_op=ALU.not_equal,
                                               fill=1.0, base=0, channel_multiplier=1), reads=["ident"], writes=["ident"])
        s.op("pool", lambda e: e.iota(iota16[:], pattern=[[1, 16]], base=0, channel_multiplier=0, allow_small_or_imprecise_dtypes=True), writes=["iota16"])
        s.dma("sp", None, DMA(n2[:], D["n2w"].partition_broadcast(128)), writes=["n2"])
        s.dma("sp", None, DMA(nf[:], D["fnw"].partition_broadcast(128)), writes=["nf"])
        for k in range(8):
            s.dma("sp", None, DMA(wq[:, k, :], D["wq"][k * 128:(k + 1) * 128, :]), writes=["wq"])
        wqres = ["wq"]
        s.dma("sp", None, DMA(kt0[:, :, :].rearrange("n h (c d) -> n h c d", c=2), D["keys"].rearrange("h c n d -> n h c d")), writes=["kt0"])
        for h in range(8):
            s.op("pe", TR(tq[0][:, h % 4 * 128:(h % 4 + 1) * 128], kt0[:, h, :], ident[:]), reads=["kt0", "ident"], writes=["tq0"])
            if h % 4 == 3:
                s.op("act", ACOPY(keysT[:, h - 3:h + 1, :], tq[0][:, :].rearrange("p (h n) -> p h n", h=4)), reads=["tq0"], writes=["keysT"])

        def rms(src, src_res, w_tile, w_res, dst, dst_res, col, jk, jkres):
            s.op("act", ACTF(jk[:], src, AF.Square, accum_out=ssq[:, col:col + 1]), reads=[src_res], writes=[jkres, f"ssq{col}"])
            s.op("dve", TS(ssq[:, col + 1:col + 2], ssq[:, col:col + 1], 1.0 / 1024, EPS, ALU.mult, ALU.add), reads=[f"ssq{col}"], writes=[f"ssq{col + 1}"])
            s.op("act", ACTF(ssq[:, col + 1:col + 2], ssq[:, col + 1:col + 2], AF.Ln), reads=[f"ssq{col + 1}"], writes=[f"ssq{col + 1}"])
            s.op("act", ACTF(ssq[:, col + 1:col + 2], ssq[:, col + 1:col + 2], AF.Exp, scale=-0.5), reads=[f"ssq{col + 1}"], writes=[f"ssq{col + 1}"])
            s.op("dve", STT(dst, src, ssq[:, col + 1:col + 2], w_tile[:], ALU.mult, ALU.mult), reads=[src_res, f"ssq{col + 1}", w_res], writes=[dst_res])

        def idx_block(m):
            b = m % 3
            s.dma("sp", None, DMA(h1[b][:, :], D["h1_s"][m * 128:(m + 1) * 128, :]), reads=["h1_s"], writes=[f"h1{b}"])
            rms(h1[b][:, :], f"h1{b}", n2, "n2", xn2[b][:, :], f"xn2{b}", 0, junk, "junk")
            s.op("pool", CP(xn2b[b][:, :], xn2[b][:, :]), reads=[f"xn2{b}"], writes=[f"xn2b{b}"])
            yield
            for k in range(8):
                tb = tq[k // 4]
                s.op("pe", TR(tb[:, (k % 4) * 128:(k % 4 + 1) * 128], xn2[b][:, k * 128:(k + 1) * 128], ident[:]), reads=[f"xn2{b}", "ident"], writes=[f"tq{k // 4}"])
                if k % 4 == 3:
                    s.op("act", ACOPY(xn2T[:, k - 3:k + 1, :], tb[:, :].rearrange("p (h n) -> p h n", h=4)), reads=[f"tq{k // 4}"], writes=["xn2T"])
            for h in range(8):
                qb = tq[h // 4]
                for k in range(8):
                    s.op("pe", MM(qb[:, (h % 4) * 128:(h % 4 + 1) * 128], wq[:, k, h * 128:(h + 1) * 128], xn2T[:, k, :], k == 0, k == 7),
                         reads=wqres + ["xn2T"], writes=[f"tq{h // 4}"])
                if h % 4 == 3:
                    s.op("act", ACOPY(qTt[:, h - 3:h + 1, :], qb[:, :].rearrange("p (h n) -> p h n", h=4)), reads=[f"tq{h // 4}"], writes=["qTt"])
            for h in range(8):
                for cc in range(2):
                    bk = sps[cc * 2 + h // 4]
                    s.op("pe", MM(bk[:, (h % 4) * 128:(h % 4 + 1) * 128], qTt[cc * 64:(cc + 1) * 64, h, :], keysT[cc * 64:(cc + 1) * 64, h, :], True, True),
                         reads=["qTt", "keysT"], writes=[f"sps{cc * 2 + h // 4}"])
            for bq in range(4):
                s.op("act", ACOPY(sc[:, bq * 4:bq * 4 + 4, :], sps[bq][:, :].rearrange("p (h n) -> p h n", h=4)), reads=[f"sps{bq}"], writes=["sc"])
            yield
            for hc in range(16):
                if hc % 4 == 0 and hc:
                    yield
                s.op("dve", lambda e, hc=hc: e.max(out=mx[:, hc, 0:8], in_=sc[:, hc, :]), reads=["sc"], writes=["mx"])
                s.op("dve", lambda e, hc=hc: e.max_index(out=mi[:, hc, 0:8], in_max=mx[:, hc, 0:8], in_values=sc[:, hc, :]), reads=["sc", "mx"], writes=["mi"])
                s.op("dve", lambda e, hc=hc: e.match_replace(out=sc2[:, :], in_to_replace=mx[:, hc, 0:8], in_values=sc[:, hc, :], imm_value=-1e30), reads=["sc", "mx"], writes=["sc2"])
                s.op("dve", lambda e, hc=hc: e.max(out=mx[:, hc, 8:16], in_=sc2[:, :]), reads=["sc2"], writes=["mx"])
                s.op("dve", lambda e, hc=hc: e.max_index(out=mi[:, hc, 8:16], in_max=mx[:, hc, 8:16], in_values=sc2[:, :]), reads=["sc2", "mx"], writes=["mi"])
            s.op("dve", CP(mif[:, :, :], mi[:, :, :]), reads=["mi"], writes=["mif"])
            yield
            mx4 = mx[:, :, :].rearrange("p (c h) k -> p c h k", c=2)
            mif4 = mif[:, :, :].rearrange("p (c h) k -> p c h k", c=2)
            cand4 = cand[:, :, :].rearrange("p h (i j) -> p h i j", i=16)
            for h in range(8):
                s.op("dve", TT(cand4[:, h, :, :], mx4[:, 0, h, :].unsqueeze(2).to_broadcast([128, 16, 16]),
                               mx4[:, 1, h, :].unsqueeze(1).to_broadcast([128, 16, 16]), ALU.add), reads=["mx"], writes=["cand"])
            yield
            for h in range(8):
                if h == 4:
                    yield
                s.op("dve", lambda e, h=h: e.max(out=sc16[:, h, 0:8], in_=cand[:, h, :]), reads=["cand"], writes=["sc16"])
                s.op("dve", lambda e, h=h: e.max_index(out=ci[:, h, 0:8], in_max=sc16[:, h, 0:8], in_values=cand[:, h, :]), reads=["cand", "sc16"], writes=["ci"])
                s.op("dve", lambda e, h=h: e.match_replace(out=cand2[:, :], in_to_replace=sc16[:, h, 0:8], in_values=cand[:, h, :], imm_value=-1e30), reads=["cand", "sc16"], writes=["cand2"])
                s.op("dve", lambda e, h=h: e.max(out=sc16[:, h, 8:16], in_=cand2[:, :]), reads=["cand2"], writes=["sc16"])
                s.op("dve", lambda e, h=h: e.max_index(out=ci[:, h, 8:16], in_max=sc16[:, h, 8:16], in_values=cand2[:, :]), reads=["cand2", "sc16"], writes=["ci"])
            yield
            s.op("dve", TT(ex[:, :, :], sc16[:, :, :], sc16[:, :, 0:1].to_broadcast([128, 8, 16]), ALU.subtract), reads=["sc16"], writes=["ex"])
            s.op("act", ACTF(ex[:, :, :], ex[:, :, :], AF.Exp), reads=["ex"], writes=["ex"])
            s.op("dve", lambda e: e.reduce_sum(out=zz[:, 0:8], in_=ex[:, :, :], axis=mybir.AxisListType.X), reads=["ex"], writes=["zz"])
            s.op("dve", RECIP(zz[:, 8:16], zz[:, 0:8]), reads=["zz"], writes=["zz2"])
            g3 = gg[b][:, :].rearrange("p (h k) -> p h k", h=8)
            s.op("dve", TT(g3, ex[:, :, :], zz[:, 8:16].unsqueeze(2).to_broadcast([128, 8, 16]), ALU.mult), reads=["ex", "zz2"], writes=[f"gg{b}"])
            yield
            s.op("dve", lambda e: e.tensor_single_scalar(out=cii[:], in_=ci[:], scalar=4, op=ALU.logical_shift_right), reads=["ci"], writes=["cii"])
            s.op("dve", lambda e: e.tensor_single_scalar(out=cij[:], in_=ci[:], scalar=15, op=ALU.bitwise_and), reads=["ci"], writes=["cij"])
            s.op("dve", CP(ciif[:], cii[:]), reads=["cii"], writes=["ciif"])
            s.op("dve", CP(cijf[:], cij[:]), reads=["cij"], writes=["cijf"])
            iob = iota16[:, :].unsqueeze(1).unsqueeze(1).to_broadcast([128, 8, 16, 16])
            for (cf, cc_, sel, nm) in ((ciif, 0, sel1, "1"), (cijf, 1, sel2, "2")):
                yield
                s.op("dve", TT(eq[:], cf[:, :, :].unsqueeze(3).to_broadcast([128, 8, 16, 16]), iob, ALU.is_equal),
                     reads=["ciif", "cijf", "iota16", "eq"], writes=["eq"])
                s.op("dve", TT(eq[:], eq[:], mif4[:, cc_, :, :].unsqueeze(2).to_broadcast([128, 8, 16, 16]), ALU.mult), reads=["eq", "mif"], writes=["eq"])
                s.op("dve", (lambda sel: lambda e: e.reduce_sum(out=sel[:], in_=eq[:], axis=mybir.AxisListType.X))(sel), reads=["eq"], writes=["sel" + nm])
            s.op("dve", STT(ef[:, :].rearrange("p (h k) -> p h k", h=8), sel1[:], 128.0, sel2[:], ALU.mult, ALU.add), reads=["sel1", "sel2"], writes=["ef"])
            s.op("dve", lambda e: e.tensor_scalar_min(out=ef[:, :], in0=ef[:, :], scalar1=16383.0), reads=["ef"], writes=["ef"])
            s.op("dve", CP(ei[b][:, :], ef[:, :]), reads=["ef"], writes=[f"ei{b}"])

        def gather(tab, buf, bres, eib, col):
            s.dma("pool", None, lambda e: e.indirect_dma_start(out=buf[:, :], out_offset=None, in_=D[tab],
                                                                in_offset=bass.IndirectOffsetOnAxis(ap=eib[:, col:col + 1], axis=0)),
                  reads=[], writes=[bres])

        GR = 4
        NG = 128 // GR
        NSLOT = NUV // GR

        def rows_block(m, idxgen):
            b = m % 3
            b2 = m % 2
            eires = f"ei{b}"
            eib = ei[b]

            def U(g):
                for r in range(g * GR, (g + 1) * GR):
                    buf = uvb[r % NUV]
                    s.dma("pool", None, (lambda buf, r: lambda e: e.indirect_dma_start(
                        out=buf[:, :], out_offset=None, in_=D["puv16"],
                        in_offset=bass.IndirectOffsetOnAxis(ap=eib[:, r:r + 1], axis=0)))(buf, r),
                          reads=[eires], writes=[f"uvg{g % NSLOT}"], part=(r % GR != 0))

            def H(g):
                for r in range(g * GR, (g + 1) * GR):
                    s.op("dve", STT(junk2[:, :], uvb[r % NUV][:, 0:1024], 1.0, xn2b[b][:, :], ALU.mult, ALU.mult, accum_out=hid[b2][:, r:r + 1]),
                         reads=[f"uvg{g % NSLOT}", f"xn2b{b}", "junk2"], writes=["junk2", f"hid{b2}_{g}"] + (["hfence"] if r % GR == 0 else []))

            def A(g):
                cs = slice(g * GR, (g + 1) * GR)
                s.op("act", ACTF(gl[b2][:, cs], hid[b2][:, cs], AF.Gelu), reads=[f"hid{b2}_{g}", "hfence"], writes=[f"gl{b2}_{g}"])
                s.op("dve", TT(aa[b2][:, cs], gl[b2][:, cs], gg[b][:, cs], ALU.mult), reads=[f"gl{b2}_{g}", f"gg{b}"], writes=[f"aa{b2}_{g}", "hfence"])

            def Pm(g):
                for r in range(g * GR, (g + 1) * GR):
                    di = r % 4
                    s.op("act", (lambda di, r: lambda e: e.mul(out=dg[di][:, :], in_=ident[:, :], mul=aa[b2][:, r:r + 1]))(di, r),
                         reads=["ident", f"aa{b2}_{g}"], writes=[f"dg{di}"])
                    for half in range(2):
                        s.op("pe", MM(accp[half][:, :], dg[di][:, :], uvb[r % NUV][:, 1024 + half * 512:1024 + (half + 1) * 512], r == 0, r == 127),
                             reads=[f"dg{di}", f"uvg{g % NSLOT}"], writes=[f"accp{half}"])

            for g0 in range(NSLOT):
                U(g0)
            for g in range(NG + 1):
                if g < NG:
                    H(g)
                if g >= 1:
                    A(g - 1)
                    Pm(g - 1)
                    if g - 1 + NSLOT < NG:
                        U(g - 1 + NSLOT)
                if idxgen is not None:
                    next(idxgen, None)
            if idxgen is not None:
                for _ in idxgen:
                    pass
            for half in range(2):
                s.op("dve", TT(h2[:, half * 512:(half + 1) * 512], accp[half][:, :], h1[b][:, half * 512:(half + 1) * 512], ALU.add),
                     reads=[f"accp{half}", f"h1{b}"], writes=["h2"])
            rms(h2[:, :], "h2", nf, "nf", yb[0][:, :], "yb0", 2, junk, "junk")
            s.dma("sp", None, DMA(D["y"][m * 128:(m + 1) * 128, :], yb[0][:, :]), reads=["yb0"], writes=["y"])

        for _ in idx_block(0):
            pass
        for _ in idx_block(1):
            pass
        for m in range(NQ):
            rows_block(m, idx_block(m + 2) if m + 2 < NQ else None)
        s.barrier()
        with nc.Block() as block:
            s.emit(block)


PHASES = (1, 2, 3, 4)


def build_program(phases=PHASES, dbg=False):
    nc = bass.Bass("TRN2", target_bir_lowering=False)
    D = {}

    def din(name, shape, dt=F32):
        D[name] = nc.dram_tensor(name, shape, dt, kind="ExternalInput").ap()

    def dscr(name, shape, dt):
        D[name] = nc.dram_tensor(name, shape, dt, kind="ExternalOutput" if dbg else "Internal").ap()

    din("xall", [NT, 1024])
    din("n1w", [1, 1024])
    din("w_in", [1024, 6656])
    din("bgate", [128, 16])
    for k in ("lq1", "lk1", "lq2", "lk2"):
        din(k, [1, 64])
    din("subln_w", [1, 128])
    din("w_bd", [1024, 1024])
    din("w_bc", [512, 1024])
    din("w_out", [1024, 1024])
    din("n2w", [1, 1024])
    din("wq", [1024, 1024])
    din("keys", [8, 2, 128, 64])
    npe = 16384 if 4 in phases else 128
    din("pu", [npe, 1024])
    din("pv", [npe, 1024])
    din("fnw", [1, 1024])
    din("colb", [128, 8, 544])
    din("tdiag", [128, 8, 256])
    din("cbg", [128, 8, 5, 128])
    din("cb0", [128, 8, 5, 128])
    dscr("kdT_s", [8, 128, NT], BF16)
    dscr("kcT_s", [4, 128, NT], BF16)
    dscr("vd_s", [8, 128, NB, 128], BF16)
    dscr("vc_s", [8, 128, NB, 64], BF16)
    dscr("qdT_s", [8, 128, 2048], BF16)
    dscr("qcT_s", [4, 128, 2048], BF16)
    dscr("od_s", [2048, 1024], BF16)
    dscr("oc_s", [2048, 512], BF16)
    dscr("h1_s", [2048, 1024], F32)
    D["puv16"] = nc.dram_tensor("puv16", [npe, 2048], BF16, kind="Internal").ap()
    D["y"] = nc.dram_tensor("y", [2048, 1024], F32, kind="ExternalOutput").ap()
    with ExitStack() as ses:
        for ph, fn in ((1, phase1), (2, phase2), (3, phase3), (4, phase4)):
            if ph in phases:
                fn(nc, ses, D)
    return nc


def host_tables(j, rel_bias):
    slopes = np.exp2(-8.0 * np.arange(1, 9, dtype=np.float64) / 8)
    p = np.arange(128, dtype=np.float64)
    colb = np.zeros((128, 8, 544), np.float32)
    for m in range(16):
        for L in range(4 * m + 4):
            idx = 2 * m * m + 2 * m + L
            dl = 4 * m + 4 - L
            if L < 4 - j:
                colb[:, :, idx] = NEG
            else:
                colb[:, :, idx] = (slopes[None, :] * (p[:, None] - 128.0 * dl - 127.0)).astype(np.float32)
    ki = p[:, None]
    qi = p[None, :]
    allowed = (np.floor(ki / 64) <= np.floor(qi / 64))
    td = np.where(allowed[None], slopes[:, None, None] * (np.minimum(ki, 2 * qi - ki)[None] - 127.0), NEG)
    tdiag = np.zeros((128, 8, 256), np.float32)
    tdiag[:, :, 0:128] = 8.0 * td.transpose(1, 0, 2)
    tdiag[:, :, 128:256] = 8.0 * td.transpose(1, 0, 2)
    cbg = np.zeros((128, 8, 5, 128), np.float32)
    cb0 = np.zeros((128, 8, 5, 128), np.float32)
    kI = np.arange(128)[:, None]
    qI = np.arange(128)[None, :]
    for dlt in range(5):
        rel = (qI - kI) + 128 * dlt
        qc = (512 + qI) // 64
        kc = (512 - 128 * dlt + kI) // 64
        ok = (qc - kc >= 0) & (qc - kc <= 8)
        tab = rel_bias[:, np.clip(rel, -128, 128) + 128]
        tile_ = np.where(ok[None], tab, np.float32(NEG)).astype(np.float32)
        cbg[:, :, dlt, :] = tile_.transpose(1, 0, 2)
        if dlt <= j:
            cb0[:, :, dlt, :] = tile_.transpose(1, 0, 2)
        else:
            cb0[:, :, dlt, :] = NEG
    return colb, tdiag, cbg, cb0


def make_in_maps(inputs):
    x = np.asarray(inputs["x"], np.float32)
    rel = np.asarray(inputs["chunk_rel_bias"], np.float32)[0]
    shared = {
        "n1w": np.ascontiguousarray(inputs["norm1_w"], dtype=np.float32).reshape(1, 1024),
        "w_in": np.ascontiguousarray(inputs["w_in"][0], dtype=np.float32),
        "bgate": np.ascontiguousarray(np.asarray(inputs["b_gate"][0], np.float32).reshape(16, 128).T),
        "lq1": np.asarray(inputs["diff_lq1"], np.float32).reshape(1, 64),
        "lk1": np.asarray(inputs["diff_lk1"], np.float32).reshape(1, 64),
        "lq2": np.asarray(inputs["diff_lq2"], np.float32).reshape(1, 64),
        "lk2": np.asarray(inputs["diff_lk2"], np.float32).reshape(1, 64),
        "subln_w": np.asarray(inputs["diff_subln_w"], np.float32).reshape(1, 128),
        "w_bd": np.ascontiguousarray(inputs["w_branch_diff"][0], dtype=np.float32),
        "w_bc": np.ascontiguousarray(inputs["w_branch_chunk"][0], dtype=np.float32),
        "w_out": np.ascontiguousarray(inputs["w_out"][0], dtype=np.float32),
        "n2w": np.asarray(inputs["norm2_w"], np.float32).reshape(1, 1024),
        "wq": np.ascontiguousarray(inputs["peer_wq"][0], dtype=np.float32),
        "keys": np.ascontiguousarray(inputs["peer_keys"][0], dtype=np.float32),
        "pu": np.ascontiguousarray(inputs["peer_u"][0], dtype=np.float32),
        "pv": np.ascontiguousarray(inputs["peer_v"][0], dtype=np.float32),
        "fnw": np.asarray(inputs["final_norm_w"], np.float32).reshape(1, 1024),
    }
    maps = []
    for c in range(8):
        b, j = c // 4, c % 4
        xall = np.zeros((NT, 1024), np.float32)
        g0 = max(0, j - 4)
        lo = (4 - j) * 128
        n = min(NT - lo, 8192)
        xall[lo:lo + n] = x[b, 0:n]
        colb, tdiag, cbg, cb0 = host_tables(j, rel)
        d = dict(shared)
        d.update({"xall": xall, "colb": colb, "tdiag": tdiag, "cbg": cbg, "cb0": cb0})
        maps.append(d)
    return maps


_NC_CACHE = {}


def kernel(**inputs):
    if "nc" not in _NC_CACHE:
        _NC_CACHE["nc"] = build_program()
    nc = _NC_CACHE["nc"]
    maps = make_in_maps(inputs)
    res = run_bass_kernel_spmd(nc, maps, core_ids=list(range(8)))
    out = np.zeros((2, 8192, 1024), np.float32)
    for c in range(8):
        b, j = c // 4, c % 4
        y = np.asarray(res.results[c]["y"]).reshape(16, 128, 1024)
        for m in range(16):
            gb = 4 * m + j
            out[b, gb * 128:(gb + 1) * 128] = y[m]
    return out
```
